# Optimizing a Trainium2 kernel written in Bass

```python
import math
import jax, jax.numpy as jnp
from jax import lax
import numpy as np

D_MODEL = 1024
BATCH = 8
SEQ = 4096
DEPTH = 4

D_MIX = D_MODEL
N_ATTN_HEADS = 8
HEAD_DIM = 64
D_ATTN = N_ATTN_HEADS * HEAD_DIM
D_REC = D_MIX - D_ATTN
N_REC_BLOCKS = 8
REC_BLOCK = D_REC // N_REC_BLOCKS
CONV_WIDTH = 4
RG_C = 8.0
D_FF = 2816
Q_BLOCK = 128
EPS = 1e-6
D_IN = 3 * D_ATTN + N_ATTN_HEADS + 2 * D_REC
SPLITS = (D_ATTN, 2 * D_ATTN, 3 * D_ATTN, 3 * D_ATTN + N_ATTN_HEADS, 3 * D_ATTN + N_ATTN_HEADS + D_REC)

kernel_name = "fox_rglru_macaron_hybrid"


def rmsnorm(x, g):
    xf = x.astype(jnp.float32)
    y = xf * lax.rsqrt(jnp.mean(xf * xf, axis=-1, keepdims=True) + EPS)
    return (y * g.astype(jnp.float32)).astype(x.dtype)


def swiglu(h, w_in, w_out):
    gu = h @ w_in
    gate, up = jnp.split(gu, 2, axis=-1)
    return (jax.nn.silu(gate) * up) @ w_out


def forgetting_attention(q, k, v, log_f):
    b, s, h, dh = q.shape
    scale = 1.0 / math.sqrt(dh)
    c = jnp.cumsum(log_f, axis=1).transpose(0, 2, 1)
    qh, kh, vh = (t.transpose(0, 2, 1, 3) for t in (q, k, v))
    outs = []
    for start in range(0, s, Q_BLOCK):
        end = start + Q_BLOCK
        qb = qh[:, :, start:end]
        kb = kh[:, :, :end]
        vb = vh[:, :, :end]
        logits = jnp.einsum('bhqd,bhkd->bhqk', qb, kb).astype(jnp.float32) * scale
        logits = logits + (c[:, :, start:end, None] - c[:, :, None, :end])
        causal = (start + jnp.arange(Q_BLOCK))[:, None] >= jnp.arange(end)[None, :]
        logits = jnp.where(causal, logits, jnp.finfo(jnp.float32).min)
        p = jax.nn.softmax(logits, axis=-1)
        outs.append(jnp.einsum('bhqk,bhkd->bhqd', p.astype(vb.dtype), vb))
    o = jnp.concatenate(outs, axis=2)
    return o.transpose(0, 2, 1, 3)


def causal_depthwise_conv(x, w, bias):
    c = x.shape[-1]
    y = lax.conv_general_dilated(
        x, w[:, None, :].astype(x.dtype), window_strides=(1,), padding=[(CONV_WIDTH - 1, 0)],
        dimension_numbers=('NWC', 'WIO', 'NWC'), feature_group_count=c)
    return y + bias


def block_diag_linear(x, w, bias):
    b, s, c = x.shape
    xb = x.reshape(b, s, N_REC_BLOCKS, REC_BLOCK)
    y = jnp.einsum('bsnc,ncd->bsnd', xb, w).reshape(b, s, c)
    return y + bias


def rg_lru(x, w_a, b_a, w_x, b_x, lam):
    r = jax.nn.sigmoid(block_diag_linear(x, w_a, b_a).astype(jnp.float32))
    i = jax.nn.sigmoid(block_diag_linear(x, w_x, b_x).astype(jnp.float32))
    log_a = -RG_C * r * jax.nn.softplus(-lam.astype(jnp.float32))
    a = jnp.exp(log_a)
    mult = jnp.sqrt(-jnp.expm1(2.0 * log_a))
    u = mult * (i * x.astype(jnp.float32))

    def combine(left, right):
        a1, b1 = left
        a2, b2 = right
        return a1 * a2, a2 * b1 + b2

    _, h = lax.associative_scan(combine, (a, u), axis=1)
    return h.astype(x.dtype)


def hybrid_mixer(h, w_in, b_f, conv_w, conv_b, w_rg_a, b_rg_a, w_rg_x, b_rg_x, rg_lambda, w_out):
    b, s, _ = h.shape
    z = h @ w_in
    q, k, v, f_logit, xr, gr = jnp.split(z, SPLITS, axis=-1)
    q = q.reshape(b, s, N_ATTN_HEADS, HEAD_DIM)
    k = k.reshape(b, s, N_ATTN_HEADS, HEAD_DIM)
    v = v.reshape(b, s, N_ATTN_HEADS, HEAD_DIM)
    log_f = jax.nn.log_sigmoid((f_logit + b_f).astype(jnp.float32))
    y_attn = forgetting_attention(q, k, v, log_f).reshape(b, s, D_ATTN)
    xr = causal_depthwise_conv(xr, conv_w, conv_b)
    y_rec = rg_lru(xr, w_rg_a, b_rg_a, w_rg_x, b_rg_x, rg_lambda) * jax.nn.gelu(gr)
    return jnp.concatenate([y_attn, y_rec], axis=-1) @ w_out


def setup_inputs(seed: int = 0) -> dict:
    key = jax.random.key(seed)
    ks = jax.random.split(key, 16)
    f32 = jnp.float32
    x = jax.random.normal(ks[0], (BATCH, SEQ, D_MODEL), f32)
    norm_g = 1.0 + 0.02 * jax.random.normal(ks[1], (DEPTH, 3, D_MODEL), f32)
    w_in = jax.random.normal(ks[2], (DEPTH, D_MODEL, D_IN), f32) * D_MODEL ** -0.5
    b_f = 3.0 + 0.1 * jax.random.normal(ks[3], (DEPTH, N_ATTN_HEADS), f32)
    conv_w = jax.random.normal(ks[4], (DEPTH, CONV_WIDTH, D_REC), f32) * CONV_WIDTH ** -0.5
    conv_b = 0.01 * jax.random.normal(ks[5], (DEPTH, D_REC), f32)
    w_rg_a = jax.random.normal(ks[6], (DEPTH, N_REC_BLOCKS, REC_BLOCK, REC_BLOCK), f32) * REC_BLOCK ** -0.5
    b_rg_a = 0.01 * jax.random.normal(ks[7], (DEPTH, D_REC), f32)
    w_rg_x = jax.random.normal(ks[8], (DEPTH, N_REC_BLOCKS, REC_BLOCK, REC_BLOCK), f32) * REC_BLOCK ** -0.5
    b_rg_x = 0.01 * jax.random.normal(ks[9], (DEPTH, D_REC), f32)
    a_c = jax.random.uniform(ks[10], (DEPTH, D_REC), f32, 0.9, 0.999)
    s_base = a_c ** (1.0 / RG_C)
    rg_lambda = jnp.log(s_base) - jnp.log1p(-s_base)
    w_out = jax.random.normal(ks[11], (DEPTH, D_MIX, D_MODEL), f32) * D_MIX ** -0.5
    w_ffn_in = jax.random.normal(ks[12], (DEPTH, 2, D_MODEL, 2 * D_FF), f32) * D_MODEL ** -0.5
    w_ffn_out = jax.random.normal(ks[13], (DEPTH, 2, D_FF, D_MODEL), f32) * D_FF ** -0.5
    final_g = 1.0 + 0.02 * jax.random.normal(ks[14], (D_MODEL,), f32)
    return {"x": x, "norm_g": norm_g, "w_in": w_in, "b_f": b_f, "conv_w": conv_w, "conv_b": conv_b,
            "w_rg_a": w_rg_a, "b_rg_a": b_rg_a, "w_rg_x": w_rg_x, "b_rg_x": b_rg_x,
            "rg_lambda": rg_lambda, "w_out": w_out, "w_ffn_in": w_ffn_in, "w_ffn_out": w_ffn_out,
            "final_g": final_g}


def reference(x, norm_g, w_in, b_f, conv_w, conv_b, w_rg_a, b_rg_a, w_rg_x, b_rg_x,
              rg_lambda, w_out, w_ffn_in, w_ffn_out, final_g):
    for l in range(DEPTH):
        x = x + 0.5 * swiglu(rmsnorm(x, norm_g[l, 0]), w_ffn_in[l, 0], w_ffn_out[l, 0])
        x = x + hybrid_mixer(rmsnorm(x, norm_g[l, 1]), w_in[l], b_f[l], conv_w[l], conv_b[l],
                             w_rg_a[l], b_rg_a[l], w_rg_x[l], b_rg_x[l], rg_lambda[l], w_out[l])
        x = x + 0.5 * swiglu(rmsnorm(x, norm_g[l, 2]), w_ffn_in[l, 1], w_ffn_out[l, 1])
    return rmsnorm(x, final_g)
```

```python
import contextlib
import math
import numpy as np
import concourse.bass as bass
import concourse.mybir as mybir
from concourse.bass_utils import run_bass_kernel_spmd

F32 = mybir.dt.float32
BF16 = mybir.dt.bfloat16
AF = mybir.ActivationFunctionType
ALU = mybir.AluOpType

D = 1024
S = 4096
NB = 8
DEPTH = 4
DFF = 2816
NJ = DFF // 128
TB = 512
NTB = S // TB
EPS = 1e-6
PL = 64
GELU_K = 0.044715
GELU_S = 2.0 * math.sqrt(2.0 / math.pi)
MASKVAL = -30000.0


class Ctx:
    COMPUTE = ("pe", "act", "dve", "pool")

    def __init__(self, nc, stack):
        self.nc = nc
        self.stack = stack
        self.eng = {"pe": nc.tensor, "act": nc.scalar, "dve": nc.vector, "pool": nc.gpsimd, "sp": nc.sync}
        self.sems = {}
        self.cnt = {}
        for e in self.COMPUTE:
            self.sems[e] = stack.enter_context(nc.semaphore("s_" + e))
            self.cnt[e] = 0
        self.waited = {e: {} for e in self.eng}
        self.tw = {}
        self.tr = {}
        self.n_wait = 0
        self.n_ops = {e: 0 for e in self.eng}

    def _deps(self, reads, writes):
        deps = {}
        for k in reads:
            for s, v in self.tw.get(k, {}).items():
                if deps.get(s, 0) < v:
                    deps[s] = v
        for k in writes:
            for s, v in self.tw.get(k, {}).items():
                if deps.get(s, 0) < v:
                    deps[s] = v
            for s, v in self.tr.get(k, {}).items():
                if deps.get(s, 0) < v:
                    deps[s] = v
        return deps

    def _emit_waits(self, ename, deps):
        E = self.eng[ename]
        wd = self.waited[ename]
        for s, v in deps.items():
            if s == "pe" and ename == "pe":
                continue
            if wd.get(s, 0) >= v:
                continue
            if s == "pe" and v > self.cnt["pe"]:
                raise RuntimeError("wait on an un-signalled PE op")
            E.wait_ge(self.sems[s], v)
            wd[s] = v
            self.n_wait += 1

    def _record(self, semkey, val, reads, writes):
        for k in reads:
            d = self.tr.setdefault(k, {})
            if d.get(semkey, 0) < val:
                d[semkey] = val
        for k in writes:
            d = self.tw.setdefault(k, {})
            if d.get(semkey, 0) < val:
                d[semkey] = val

    def op(self, ename, fn, reads=(), writes=(), signal=True):
        reads = list(reads) + ["*"]
        self._emit_waits(ename, self._deps(reads, writes))
        inst = fn(self.eng[ename])
        self.n_ops[ename] += 1
        if signal:
            self.cnt[ename] += 1
            inst.then_inc(self.sems[ename], 1)
            val = self.cnt[ename]
        else:
            assert ename == "pe"
            val = self.cnt[ename] + 1
        self._record(ename, val, reads, writes)
        return inst

    def dma(self, qname, out, in_, reads, writes, semkey, **kw):
        sk = ("dma", semkey)
        if sk not in self.sems:
            self.sems[sk] = self.stack.enter_context(self.nc.semaphore("d%d" % len(self.sems)))
            self.cnt[sk] = 0
        reads = list(reads) + ["*"]
        self._emit_waits(qname, self._deps(reads, writes))
        inst = self.eng[qname].dma_start(out=out, in_=in_, **kw)
        self.n_ops[qname] += 1
        self.cnt[sk] += 16
        inst.then_inc(self.sems[sk], 16)
        self._record(sk, self.cnt[sk], reads, writes)
        return inst

    def wait_all(self, ename, keys):
        deps = {}
        for k in keys:
            for d in (self.tw.get(k, {}), self.tr.get(k, {})):
                for s, v in d.items():
                    if deps.get(s, 0) < v:
                        deps[s] = v
        if ename == "pe":
            deps.pop("pe", None)
        self._emit_waits(ename, deps)

    def barrier(self):
        for e in ("pe", "act", "dve", "pool", "sp"):
            self.wait_all(e, ["*"])

    def mm(self, out, lhsT, rhs, start, stop, reads, writes, signal=False):
        return self.op("pe", lambda e: e.matmul(out, lhsT=lhsT, rhs=rhs, start=start, stop=stop),
                       reads=reads, writes=writes, signal=signal)

    def act(self, out, in_, func, reads, writes, **kw):
        return self.op("act", lambda e: e.activation(out=out, in_=in_, func=func, **kw), reads=reads, writes=writes)


class WStream:
    def __init__(self, c, name, buf, nslots, items):
        self.c, self.name, self.buf, self.nslots, self.items = c, name, buf, nslots, items
        self.issued = 0
        self.consumed = 0
        for _ in range(nslots):
            self._issue()

    def _issue(self):
        if self.issued < len(self.items):
            src, keys = self.items[self.issued]
            s = self.issued % self.nslots
            self.c.dma("sp", self.buf[:, s, :], src, reads=keys, writes=[(self.name, s)], semkey=(self.name, s))
            self.issued += 1

    def next(self):
        s = self.consumed % self.nslots
        self.consumed += 1
        return s, (self.name, s)

    def release(self, n=1):
        for _ in range(n):
            self._issue()


def build_program(NL, final):
    nc = bass.Bass("TRN2", target_bir_lowering=False)
    NC = NL * PL + 8

    def din(name, shape, dt=F32):
        return nc.dram_tensor(name, shape, dt, kind="ExternalInput").ap()

    def dint(name, shape, dt=BF16):
        return nc.dram_tensor(name, shape, dt, kind="Internal").ap()

    xT = din("xT", [D, S])
    cst_d = din("cst", [128, NC])
    cmat_d = din("cmat", [128, 384])
    wi_d = din("wi", [NL, 2, NJ, 128, 2048])
    wo_d = din("wo", [NL, 2, 8, 128, 2816])
    wq_d = din("wq", [NL, 8, 128, 2048])
    wv_d = din("wv", [NL, 128, 4096])
    wf_d = din("wf", [NL, 128, 64])
    wor_d = din("wor", [NL, 128, 4096])
    woa_d = din("woa", [NL, 64, 8192])
    wbd_d = din("wbd", [NL, 128, 1024])
    oT = nc.dram_tensor("oT", [D, S], F32, kind="ExternalOutput").ap()

    wi_b = dint("wi_b", [NL, 2, NJ, 128, 2048])
    wo_b = dint("wo_b", [NL, 2, 8, 128, 2816])
    wq_b = dint("wq_b", [NL, 8, 128, 2048])
    wv_b = dint("wv_b", [NL, 128, 4096])
    wf_b = dint("wf_b", [NL, 128, 64])
    wor_b = dint("wor_b", [NL, 128, 4096])
    woa_b = dint("woa_b", [NL, 64, 8192])
    wbd_b = dint("wbd_b", [NL, 128, 1024])
    qT_s = dint("qT_s", [512, S])
    kT_s = dint("kT_s", [512, S])
    v_s = dint("v_s", [4, 128, 32, 2, 65])

    with contextlib.ExitStack() as st:
        c = Ctx(nc, st)

        uid = [0]

        def sb(stack, name, shape, dt):
            uid[0] += 1
            return stack.enter_context(nc.sbuf_tensor("%s_%d" % (name, uid[0]), shape, dt))

        X = sb(st, "X", [128, 8, S], F32)
        cst = sb(st, "cst_sb", [128, NC], F32)
        cmat = sb(st, "cmat_sb", [128, 384], F32)
        ident_bf = sb(st, "ident_bf", [128, 128], BF16)
        mask_bf = sb(st, "mask_bf", [128, 128], BF16)
        ones_bf = sb(st, "ones_bf", [128, 128], BF16)
        ones_f = sb(st, "ones_f", [128, 128], F32)
        spc = sb(st, "spc", [128, NL * 8], F32)
        clog = sb(st, "clog", [128, 32, 8], F32)
        cref = sb(st, "cref", [128, NTB, 8], F32)
        PS = [st.enter_context(nc.psum_tensor("ps%d" % b, [128, 512], F32)) for b in range(8)]

        def psk(b):
            return ("ps", b)

        def xk(fc, tb):
            return ("X", fc, tb)

        def blk(tb):
            return slice(tb * TB, (tb + 1) * TB)

        def cast(dst, src, key, tag, maxrows=1024):
            rows = dst.shape[0]
            r0 = 0
            i = 0
            while r0 < rows:
                r1 = min(rows, r0 + maxrows)
                c.dma("pool", dst[r0:r1, :], src[r0:r1, :], reads=[], writes=[key], semkey=("cast", tag, i))
                r0 = r1
                i += 1

        def cast_ffn(l, i):
            for half in range(2):
                js = slice(half * 11, half * 11 + 11)
                cast(wi_b[l, i, js].rearrange("j p e -> (j p) e"), wi_d[l, i, js].rearrange("j p e -> (j p) e"),
                     ("wi_b", l, i, half), ("wi", i, half), maxrows=704)
            for half in range(2):
                fs = slice(half * 4, half * 4 + 4)
                cast(wo_b[l, i, fs].rearrange("f p (a e) -> (f p a) e", a=2),
                     wo_d[l, i, fs].rearrange("f p (a e) -> (f p a) e", a=2),
                     ("wo_b", l, i, half), ("wo", i, half), maxrows=1024)

        def cast_mix(l):
            k = ("mixw", l)
            cast(wq_b[l].rearrange("t p e -> (t p) e"), wq_d[l].rearrange("t p e -> (t p) e"), k, ("wq",), maxrows=512)
            cast(wv_b[l].rearrange("p (a e) -> (p a) e", a=2), wv_d[l].rearrange("p (a e) -> (p a) e", a=2), k, ("wv",))
            cast(wf_b[l], wf_d[l], k, ("wf",))
            cast(wbd_b[l], wbd_d[l], k, ("wbd",))
            cast(wor_b[l].rearrange("p (a e) -> (p a) e", a=2), wor_d[l].rearrange("p (a e) -> (p a) e", a=2), k, ("wor",))
            cast(woa_b[l].rearrange("p (a e) -> (p a) e", a=4), woa_d[l].rearrange("p (a e) -> (p a) e", a=4), k, ("woa",))

        def cast_layer(l):
            cast_ffn(l, 0)
            cast_mix(l)
            cast_ffn(l, 1)

        cast_layer(0)
        c.dma("sp", cst[:, :], cst_d[:, :], reads=[], writes=["cst"], semkey="cst")
        c.dma("sp", cmat[:, :], cmat_d[:, :], reads=[], writes=["cmat"], semkey="cmat")
        for fc in range(8):
            c.dma("sp", X[:, fc, :], xT[fc * 128:(fc + 1) * 128, :], reads=[],
                  writes=[xk(fc, tb) for tb in range(NTB)], semkey=("Xld", fc))
        c.op("dve", lambda e: e.memset(ones_bf[:, :], 1.0), writes=["ones_bf"])
        c.op("dve", lambda e: e.memset(ones_f[:, :], 1.0), writes=["ones_f"])
        c.act(ident_bf[:, :], cmat[:, 0:128], AF.Copy, reads=["cmat"], writes=["ident_bf"])
        c.act(mask_bf[:, :], cmat[:, 128:256], AF.Copy, reads=["cmat"], writes=["mask_bf"])
        tri_f = cmat[:, 256:384]
        for l in range(NL):
            lam = cst[:, l * PL + 52:l * PL + 56]
            c.act(spc[:, l * 8:l * 8 + 4], lam, AF.Sigmoid, reads=["cst"], writes=["spc"])
            c.act(spc[:, l * 8:l * 8 + 4], spc[:, l * 8:l * 8 + 4], AF.Ln, reads=["spc"], writes=["spc"])
            c.op("dve", lambda e: e.tensor_scalar(out=spc[:, l * 8 + 4:l * 8 + 8], in0=spc[:, l * 8:l * 8 + 4],
                                                  scalar1=16.0, scalar2=None, op0=ALU.mult),
                 reads=["spc"], writes=["spc"])
            c.op("dve", lambda e: e.tensor_scalar(out=spc[:, l * 8:l * 8 + 4], in0=spc[:, l * 8:l * 8 + 4],
                                                  scalar1=8.0, scalar2=None, op0=ALU.mult),
                 reads=["spc"], writes=["spc"])

        def rmsnorm_stats(tb, sq, rs_tmp, rstd, ps_stat):
            for fc in range(8):
                b = fc % 2
                c.act(sq[:, b, :], X[:, fc, blk(tb)], AF.Square, reads=[xk(fc, tb)], writes=[("sq", b)])
                c.mm(PS[ps_stat][:, :], ones_bf[:, :], sq[:, b, :], start=(fc == 0), stop=(fc == 7),
                     reads=[("sq", b), "ones_bf"], writes=[psk(ps_stat)], signal=True)
            c.act(rs_tmp[:, :], PS[ps_stat][:, :], AF.Sqrt, reads=[psk(ps_stat)], writes=["rs_tmp"],
                  scale=1.0 / D, bias=EPS)
            c.op("dve", lambda e: e.reciprocal(out=rstd[:, :], in_=rs_tmp[:, :]), reads=["rs_tmp"], writes=["rstd"])

        def rmsnorm_apply(tb, gbase, rstd, xn):
            for fc in range(8):
                c.op("dve", lambda e: e.scalar_tensor_tensor(out=xn[:, fc, :], in0=X[:, fc, blk(tb)],
                                                             scalar=cst[:, gbase + fc:gbase + fc + 1], in1=rstd[:, :],
                                                             op0=ALU.mult, op1=ALU.mult),
                     reads=[xk(fc, tb), "rstd", "cst"], writes=[("xn", fc)])

        def ffn_phase(l, i, gbase):
            c.barrier()
            with contextlib.ExitStack() as ph:
                xn = sb(ph, "f_xn", [128, 8, TB], BF16)
                sq = sb(ph, "f_sq", [128, 2, TB], BF16)
                rs_tmp = sb(ph, "f_rs", [128, TB], F32)
                rstd = sb(ph, "f_rstd", [128, TB], F32)
                h = sb(ph, "f_h", [128, NJ, TB], BF16)
                sg = sb(ph, "f_sg", [128, 2, TB], F32)
                wib = sb(ph, "f_wi", [128, 4, 2048], BF16)
                wob = sb(ph, "f_wo", [128, 3, 2816], BF16)
                wi_items = [(wi_b[l, i, j], [("wi_b", l, i, j // 11)]) for _ in range(NTB) for j in range(NJ)]
                wo_items = [(wo_b[l, i, fo], [("wo_b", l, i, fo // 4)]) for _ in range(NTB) for fo in range(8)]
                wis = WStream(c, "wi_s", wib, 4, wi_items)
                wos = WStream(c, "wo_s", wob, 3, wo_items)
                for tb in range(NTB):
                    rmsnorm_stats(tb, sq, rs_tmp, rstd, 6)
                    rmsnorm_apply(tb, gbase, rstd, xn)
                    for j in range(NJ):
                        s, skey = wis.next()
                        wt = wib[:, s, :].rearrange("p (k g m) -> p k g m", k=8, g=2)
                        pg, pu = j % 2, 2 + j % 2
                        for gu, pb in ((0, pg), (1, pu)):
                            for kc in range(8):
                                c.mm(PS[pb][:, :], wt[:, kc, gu, :], xn[:, kc, :], start=(kc == 0), stop=(kc == 7),
                                     reads=[skey, ("xn", kc)], writes=[psk(pb)], signal=(kc == 7))
                        wis.release()
                        c.act(sg[:, j % 2, :], PS[pg][:, :], AF.Silu, reads=[psk(pg)], writes=[("sg", j % 2)])
                        c.op("dve", lambda e: e.tensor_tensor(out=h[:, j, :], in0=PS[pu][:, :], in1=sg[:, j % 2, :],
                                                              op=ALU.mult),
                             reads=[psk(pu), ("sg", j % 2)], writes=[("h", j)])
                    for fo in range(8):
                        s, skey = wos.next()
                        wt = wob[:, s, :].rearrange("p (j m) -> p j m", j=NJ)
                        pb = 4 + fo % 2
                        for j in range(NJ):
                            c.mm(PS[pb][:, :], wt[:, j, :], h[:, j, :], start=(j == 0), stop=(j == NJ - 1),
                                 reads=[skey, ("h", j)], writes=[psk(pb)], signal=(j == NJ - 1))
                        wos.release()
                        c.op("dve", lambda e: e.scalar_tensor_tensor(out=X[:, fo, blk(tb)], in0=PS[pb][:, :], scalar=0.5,
                                                                     in1=X[:, fo, blk(tb)], op0=ALU.mult, op1=ALU.add),
                             reads=[psk(pb), xk(fo, tb)], writes=[xk(fo, tb)])
                c.barrier()

        def m1_phase(l):
            c.barrier()
            cb = l * PL
            with contextlib.ExitStack() as ph:
                xn = sb(ph, "m_xn", [128, 8, TB], BF16)
                sq = sb(ph, "m_sq", [128, 2, TB], BF16)
                rs_tmp = sb(ph, "m_rs", [128, TB], F32)
                rstd = sb(ph, "m_rstd", [128, TB], F32)
                wsb = sb(ph, "m_ws", [128, 3, 2048], BF16)
                wf_sb = sb(ph, "m_wf", [128, 64], BF16)
                wbd_sb = sb(ph, "m_wbd", [128, 1024], BF16)
                wor_sb = sb(ph, "m_wor", [128, 4096], BF16)
                stq = sb(ph, "m_stq", [128, 4, TB], BF16)
                stk = sb(ph, "m_stk", [128, 4, TB], BF16)
                stv = sb(ph, "m_stv", [128, 4, 8, 65], BF16)
                xr_sb = sb(ph, "m_xr", [128, 4, TB + 3], F32)
                hcar = sb(ph, "m_hcar", [128, 4], F32)
                carry = sb(ph, "m_carry", [128, 8], F32)
                fb = sb(ph, "m_fb", [128, 8], F32)
                T = [sb(ph, "m_t%d" % k, [128, TB], F32) for k in range(6)]
                xc_bf = sb(ph, "m_xcbf", [128, TB], BF16)
                yrec = sb(ph, "m_yrec", [128, 4, TB], BF16)
                mk = ("mixw", l)
                c.dma("sp", wf_sb[:, :], wf_b[l], reads=[mk], writes=["wf_sb"], semkey="wf_sb")
                c.dma("sp", wbd_sb[:, :], wbd_b[l], reads=[mk], writes=["wbd_sb"], semkey="wbd_sb")
                c.dma("sp", wor_sb[:, :], wor_b[l], reads=[mk], writes=["wor_sb"], semkey="wor_sb")
                items = []
                for _ in range(NTB):
                    for t in range(8):
                        items.append((wq_b[l, t], [mk]))
                    items.append((wv_b[l, :, 0:2048], [mk]))
                    items.append((wv_b[l, :, 2048:4096], [mk]))
                ws = WStream(c, "m_ws", wsb, 3, items)
                c.op("dve", lambda e: e.memset(stv[:, :, :, :], 1.0), writes=["stv"])
                c.op("dve", lambda e: e.memset(xr_sb[:, :, :], 0.0), writes=[("xr", k) for k in range(4)])
                c.op("dve", lambda e: e.memset(hcar[:, :], 0.0), writes=["hcar"])
                c.op("dve", lambda e: e.memset(carry[:, :], 0.0), writes=["carry"])
                wbd_v = wbd_sb[:, :].rearrange("p (c g m) -> p c g m", c=4, g=2)
                wor_v = wor_sb[:, :].rearrange("p (k f) -> p k f", k=4)
                wf_v = wf_sb[:, :].rearrange("p (k h) -> p k h", k=8)
                rot = [0]

                def nextbank():
                    b = rot[0] % 4
                    rot[0] += 1
                    return b

                def col(idx):
                    return cst[:, idx:idx + 1]

                for tb in range(NTB):
                    rmsnorm_stats(tb, sq, rs_tmp, rstd, 6)
                    rmsnorm_apply(tb, cb + 8, rstd, xn)
                    gr_bank = {}
                    for t in range(8):
                        s, skey = ws.next()
                        wt = wsb[:, s, :].rearrange("p (k g m) -> p k g m", k=8, g=2)
                        for cc in range(2):
                            ch = 2 * t + cc
                            pb = nextbank()
                            for kc in range(8):
                                c.mm(PS[pb][:, :], wt[:, kc, cc, :], xn[:, kc, :], start=(kc == 0), stop=(kc == 7),
                                     reads=[skey, ("xn", kc)], writes=[psk(pb)], signal=(kc == 7))
                            if ch < 4:
                                c.act(stq[:, ch, :], PS[pb][:, :], AF.Copy, reads=[psk(pb)], writes=["stq"])
                            elif ch < 8:
                                c.act(stk[:, ch - 4, :], PS[pb][:, :], AF.Copy, reads=[psk(pb)], writes=["stk"])
                            elif ch < 12:
                                k = ch - 8
                                c.act(xr_sb[:, k, 3:TB + 3], PS[pb][:, :], AF.Copy, reads=[psk(pb)], writes=[("xr", k)])
                            else:
                                k = ch - 12
                                rec_chunk(l, k, pb, xr_sb, hcar, T, xc_bf, yrec, wbd_v, nextbank, col)
                        ws.release()
                    for fo in range(8):
                        pb = 4 + fo % 2
                        for kc in range(4):
                            c.mm(PS[pb][:, :], wor_v[:, kc, fo * 128:(fo + 1) * 128], yrec[:, kc, :],
                                 start=(kc == 0), stop=(kc == 3), reads=["wor_sb", ("yrec", kc)], writes=[psk(pb)],
                                 signal=(kc == 3))
                        c.op("dve", lambda e: e.tensor_tensor(out=X[:, fo, blk(tb)], in0=PS[pb][:, :],
                                                              in1=X[:, fo, blk(tb)], op=ALU.add),
                             reads=[psk(pb), xk(fo, tb)], writes=[xk(fo, tb)])
                    sA, kA = ws.next()
                    sB, kB = ws.next()
                    for tt in range(4):
                        pb = 4 + tt % 2
                        tok = slice(tt * 128, (tt + 1) * 128)
                        for kc in range(8):
                            sl, kk = (sA, kA) if kc < 4 else (sB, kB)
                            wv_t = wsb[:, sl, :].rearrange("p (k n) -> p k n", k=4)
                            c.mm(PS[pb][:, :], xn[:, kc, tok], wv_t[:, kc % 4, :], start=(kc == 0), stop=(kc == 7),
                                 reads=[kk, ("xn", kc)], writes=[psk(pb)], signal=(kc == 7))
                        c.act(stv[:, tt, :, 0:64], PS[pb][:, :].rearrange("p (h d) -> p h d", h=8), AF.Copy,
                              reads=[psk(pb)], writes=["stv"])
                        for kc in range(8):
                            c.mm(PS[7][:, 0:8], xn[:, kc, tok], wf_v[:, kc, :], start=(kc == 0), stop=(kc == 7),
                                 reads=["wf_sb", ("xn", kc)], writes=[psk(7)], signal=(kc == 7))
                        bf_bc = cst[:, cb + 56:cb + 64]
                        c.op("dve", lambda e: e.tensor_tensor(out=fb[:, :], in0=PS[7][:, 0:8], in1=bf_bc, op=ALU.add),
                             reads=[psk(7), "cst"], writes=["fb"])
                        c.act(fb[:, :], fb[:, :], AF.Sigmoid, reads=["fb"], writes=["fb"])
                        c.act(fb[:, :], fb[:, :], AF.Ln, reads=["fb"], writes=["fb"])
                        c.mm(PS[7][:, 8:16], tri_f, fb[:, :], start=True, stop=True, reads=["cmat", "fb"],
                             writes=[psk(7)], signal=True)
                        c.mm(PS[7][:, 16:24], ones_f[:, :], fb[:, :], start=True, stop=True, reads=["ones_f", "fb"],
                             writes=[psk(7)], signal=True)
                        n = tb * 4 + tt
                        c.op("dve", lambda e: e.tensor_tensor(out=clog[:, n, :], in0=PS[7][:, 8:16], in1=carry[:, :],
                                                              op=ALU.add),
                             reads=[psk(7), "carry"], writes=["clog"])
                        c.op("dve", lambda e: e.tensor_tensor(out=carry[:, :], in0=PS[7][:, 16:24], in1=carry[:, :],
                                                              op=ALU.add),
                             reads=[psk(7), "carry"], writes=["carry"])
                        if tt == 1:
                            c.op("dve", lambda e: e.tensor_copy(out=cref[:, tb, :], in_=carry[:, :]),
                                 reads=["carry"], writes=["cref"])
                    ws.release(2)
                    qk_keys = [("qT_s", p) for p in range(4)]
                    c.dma("sp", qT_s.rearrange("(c p) t -> p c t", p=128)[:, :, blk(tb)], stq[:, :, :],
                          reads=["stq"], writes=qk_keys, semkey="stq")
                    c.dma("sp", kT_s.rearrange("(c p) t -> p c t", p=128)[:, :, blk(tb)], stk[:, :, :],
                          reads=["stk"], writes=[("kT_s", p) for p in range(4)], semkey="stk")
                    for p in range(4):
                        c.dma("sp", v_s[p, :, tb * 4:(tb + 1) * 4, :, :], stv[:, :, 2 * p:2 * p + 2, :],
                              reads=["stv"], writes=[("v_s", p)], semkey="stv")
                c.barrier()

        def rec_chunk(l, k, pb_gr, xr_sb, hcar, T, xc_bf, yrec, wbd_v, nextbank, col):
            cb = l * PL
            acc, tr, ti, ta, tth, tx = T
            xk_ = ("xr", k)
            c.op("dve", lambda e: e.tensor_scalar(out=acc[:, :], in0=xr_sb[:, k, 0:TB], scalar1=col(cb + 24 + 0 * 4 + k),
                                                  scalar2=col(cb + 40 + k), op0=ALU.mult, op1=ALU.add),
                 reads=[xk_, "cst"], writes=["t_acc"])
            for tap in range(1, 4):
                c.op("dve", lambda e: e.scalar_tensor_tensor(out=acc[:, :], in0=xr_sb[:, k, tap:tap + TB],
                                                             scalar=col(cb + 24 + tap * 4 + k), in1=acc[:, :],
                                                             op0=ALU.mult, op1=ALU.add),
                     reads=[xk_, "cst", "t_acc"], writes=["t_acc"])
            c.act(xr_sb[:, k, 0:3], xr_sb[:, k, TB:TB + 3], AF.Copy, reads=[xk_], writes=[xk_])
            c.act(xc_bf[:, :], acc[:, :], AF.Copy, reads=["t_acc"], writes=["xc_bf"])
            pa = nextbank()
            c.mm(PS[pa][:, :], wbd_v[:, k, 0, :], xc_bf[:, :], start=True, stop=True, reads=["wbd_sb", "xc_bf"],
                 writes=[psk(pa)], signal=True)
            px = nextbank()
            c.mm(PS[px][:, :], wbd_v[:, k, 1, :], xc_bf[:, :], start=True, stop=True, reads=["wbd_sb", "xc_bf"],
                 writes=[psk(px)], signal=True)
            c.act(tr[:, :], PS[pa][:, :], AF.Sigmoid, reads=[psk(pa), "cst"], writes=["t_r"], bias=col(cb + 44 + k))
            c.act(ti[:, :], PS[px][:, :], AF.Sigmoid, reads=[psk(px), "cst"], writes=["t_i"], bias=col(cb + 48 + k))
            sp1 = spc[:, l * 8 + k:l * 8 + k + 1]
            sp2 = spc[:, l * 8 + 4 + k:l * 8 + 4 + k + 1]
            c.act(ta[:, :], tr[:, :], AF.Exp, reads=["t_r", "spc"], writes=["t_a"], scale=sp1)
            c.act(tth[:, :], tr[:, :], AF.Tanh, reads=["t_r", "spc"], writes=["t_th"], scale=sp1)
            c.act(tr[:, :], tr[:, :], AF.Exp, reads=["t_r", "spc"], writes=["t_r"], scale=sp2)
            c.op("dve", lambda e: e.scalar_tensor_tensor(out=tth[:, :], in0=tr[:, :], scalar=1.0, in1=tth[:, :],
                                                         op0=ALU.add, op1=ALU.mult),
                 reads=["t_r", "t_th"], writes=["t_th"])
            c.act(tth[:, :], tth[:, :], AF.Sqrt, reads=["t_th"], writes=["t_th"], scale=-1.0)
            c.op("dve", lambda e: e.tensor_tensor(out=ti[:, :], in0=ti[:, :], in1=acc[:, :], op=ALU.mult),
                 reads=["t_i", "t_acc"], writes=["t_i"])
            c.op("dve", lambda e: e.tensor_tensor(out=ti[:, :], in0=ti[:, :], in1=tth[:, :], op=ALU.mult),
                 reads=["t_i", "t_th"], writes=["t_i"])
            c.op("dve", lambda e: e.tensor_tensor_scan(out=tr[:, :], data0=ta[:, :], data1=ti[:, :],
                                                       initial=hcar[:, k:k + 1], op0=ALU.mult, op1=ALU.add),
                 reads=["t_a", "t_i", "hcar", "t_r"], writes=["t_r"])
            c.act(hcar[:, k:k + 1], tr[:, TB - 1:TB], AF.Copy, reads=["t_r"], writes=["hcar"])
            c.act(tx[:, :], PS[pb_gr][:, :], AF.Square, reads=[psk(pb_gr)], writes=["t_x"], scale=math.sqrt(GELU_K))
            c.op("dve", lambda e: e.scalar_tensor_tensor(out=tx[:, :], in0=tx[:, :], scalar=1.0, in1=PS[pb_gr][:, :],
                                                         op0=ALU.add, op1=ALU.mult),
                 reads=["t_x", psk(pb_gr)], writes=["t_x"])
            c.act(tx[:, :], tx[:, :], AF.Sigmoid, reads=["t_x"], writes=["t_x"], scale=GELU_S)
            c.op("dve", lambda e: e.tensor_tensor(out=tx[:, :], in0=PS[pb_gr][:, :], in1=tx[:, :], op=ALU.mult),
                 reads=["t_x", psk(pb_gr)], writes=["t_x"])
            c.op("dve", lambda e: e.tensor_tensor(out=yrec[:, k, :], in0=tx[:, :], in1=tr[:, :], op=ALU.mult),
                 reads=["t_x", "t_r"], writes=[("yrec", k)])

        def m2_phase(l):
            c.barrier()
            mk = ("mixw", l)
            with contextlib.ExitStack() as ph:
                kT_sb = sb(ph, "a_kT", [128, 2, S], BF16)
                v_sb = sb(ph, "a_v", [128, 2, 32, 2, 65], BF16)
                q_sb = sb(ph, "a_q", [128, 2, TB], BF16)
                P_sb = sb(ph, "a_P", [128, 4, TB], BF16)
                bias_sb = sb(ph, "a_bias", [128, 2, 32], F32)
                rrow = sb(ph, "a_rrow", [128, TB], F32)
                rb_sb = sb(ph, "a_rb", [128, 2, TB], F32)
                y_sb = sb(ph, "a_y", [128, 2, 2, TB], BF16)
                woa_sb = sb(ph, "a_woa", [64, 8192], BF16)
                woa_v = woa_sb[:, :].rearrange("p (h f) -> p h f", h=8)
                c.dma("sp", woa_sb[:, :], woa_b[l], reads=[mk], writes=["woa_sb"], semkey="woa_sb")

                def load_pair(hp):
                    b = hp % 2
                    c.dma("sp", kT_sb[:, b, :], kT_s[hp * 128:(hp + 1) * 128, :], reads=[("kT_s", hp)],
                          writes=[("kT_sb", b)], semkey=("kT_sb", b))
                    c.dma("sp", v_sb[:, b, :, :, :], v_s[hp], reads=[("v_s", hp)], writes=[("v_sb", b)],
                          semkey=("v_sb", b))

                def load_q(hp, qb, qi):
                    b = qi % 2
                    c.dma("sp", q_sb[:, b, :], qT_s[hp * 128:(hp + 1) * 128, blk(qb)], reads=[("qT_s", hp)],
                          writes=[("q_sb", b)], semkey=("q_sb", b))

                seq = [(hp, qb) for hp in range(4) for qb in range(NTB)]
                load_pair(0)
                load_q(0, 0, 0)
                pcount = [0]
                scount = [0]
                for qi, (hp, qb) in enumerate(seq):
                    kb = hp % 2
                    qbuf = qi % 2
                    if qb == 0 and hp + 1 < 4:
                        load_pair(hp + 1)
                    if qi + 1 < len(seq):
                        load_q(seq[qi + 1][0], seq[qi + 1][1], qi + 1)
                    for hh in range(2):
                        hd = 2 * hp + hh
                        c.op("dve", lambda e: e.tensor_scalar(out=bias_sb[:, hh, :], in0=clog[:, :, hd], scalar1=-1.0,
                                                              scalar2=cref[:, qb, hd:hd + 1], op0=ALU.mult, op1=ALU.add),
                             reads=["clog", "cref"], writes=[("bias", hh)])
                    nkt = 4 * qb + 4
                    for kt in range(nkt):
                        n0 = max(0, kt * 128 - qb * TB)
                        N = TB - n0
                        diag = kt * 128 >= qb * TB
                        for hh in range(2):
                            hs = slice(hh * 64, hh * 64 + 64)
                            pS = scount[0] % 2
                            scount[0] += 1
                            pO = 2 + 2 * (qi % 2) + hh
                            c.mm(PS[pS][:, 0:N], kT_sb[hs, kb, kt * 128:(kt + 1) * 128], q_sb[hs, qbuf, n0:TB],
                                 start=True, stop=(not diag), reads=[("kT_sb", kb), ("q_sb", qbuf)], writes=[psk(pS)],
                                 signal=(not diag))
                            if diag:
                                c.mm(PS[pS][:, 0:128], ident_bf[:, :], mask_bf[:, :], start=False, stop=True,
                                     reads=["ident_bf", "mask_bf"], writes=[psk(pS)], signal=True)
                            pbuf = pcount[0] % 4
                            pcount[0] += 1
                            c.act(P_sb[:, pbuf, 0:N], PS[pS][:, 0:N], AF.Exp, reads=[psk(pS), ("bias", hh)],
                                  writes=[("P", pbuf)], scale=0.125, bias=bias_sb[:, hh, kt:kt + 1])
                            c.mm(PS[pO][0:65, n0:TB], v_sb[:, kb, kt, hh, :], P_sb[:, pbuf, 0:N], start=(kt == 0),
                                 stop=(kt == nkt - 1), reads=[("v_sb", kb), ("P", pbuf)], writes=[psk(pO)], signal=True)
                    yb = qi % 2
                    for hh in range(2):
                        pO = 2 + 2 * (qi % 2) + hh
                        c.op("dve", lambda e: e.reciprocal(out=rrow[64:65, :], in_=PS[pO][64:65, :]),
                             reads=[psk(pO)], writes=["rrow"])
                        c.mm(PS[6][0:64, :], ones_f[64:65, 0:64], rrow[64:65, :], start=True, stop=True,
                             reads=["ones_f", "rrow"], writes=[psk(6)], signal=True)
                        c.act(rb_sb[0:64, hh, :], PS[6][0:64, :], AF.Copy, reads=[psk(6)], writes=[("rb", hh)])
                        c.op("dve", lambda e: e.tensor_tensor(out=y_sb[0:64, yb, hh, :], in0=PS[pO][0:64, :],
                                                              in1=rb_sb[0:64, hh, :], op=ALU.mult),
                             reads=[psk(pO), ("rb", hh)], writes=[("y", yb, hh)])
                    for fo in range(8):
                        for hh in range(2):
                            c.mm(PS[7][:, :], woa_v[0:64, 2 * hp + hh, fo * 128:(fo + 1) * 128], y_sb[0:64, yb, hh, :],
                                 start=(hh == 0), stop=(hh == 1), reads=["woa_sb", ("y", yb, hh)], writes=[psk(7)],
                                 signal=(hh == 1))
                        c.op("dve", lambda e: e.tensor_tensor(out=X[:, fo, blk(qb)], in0=PS[7][:, :],
                                                              in1=X[:, fo, blk(qb)], op=ALU.add),
                             reads=[psk(7), xk(fo, qb)], writes=[xk(fo, qb)])
                c.barrier()

        for l in range(NL):
            if l + 1 < NL:
                cast_layer(l + 1)
            ffn_phase(l, 0, l * PL + 0)
            m1_phase(l)
            m2_phase(l)
            ffn_phase(l, 1, l * PL + 16)

        c.barrier()
        if final:
            with contextlib.ExitStack() as ph:
                sq = sb(ph, "e_sq", [128, 2, TB], BF16)
                rs_tmp = sb(ph, "e_rs", [128, TB], F32)
                rstd = sb(ph, "e_rstd", [128, TB], F32)
                gb = NL * PL
                for tb in range(NTB):
                    rmsnorm_stats(tb, sq, rs_tmp, rstd, tb % 2)
                    for fc in range(8):
                        c.op("dve", lambda e: e.scalar_tensor_tensor(out=X[:, fc, blk(tb)], in0=X[:, fc, blk(tb)],
                                                                     scalar=cst[:, gb + fc:gb + fc + 1], in1=rstd[:, :],
                                                                     op0=ALU.mult, op1=ALU.mult),
                             reads=[xk(fc, tb), "rstd", "cst"], writes=[xk(fc, tb)])
                    c.dma("sp", oT.rearrange("(c p) t -> p c t", p=128)[:, :, blk(tb)], X[:, :, blk(tb)],
                          reads=[xk(fc, tb) for fc in range(8)], writes=["oT"], semkey=("Xout", tb))
                c.wait_all("sp", ["oT"])
        else:
            for fc in range(8):
                c.dma("sp", oT[fc * 128:(fc + 1) * 128, :], X[:, fc, :], reads=[xk(fc, tb) for tb in range(NTB)],
                      writes=["oT"], semkey=("Xout", fc))
            c.wait_all("sp", ["oT"])
        c.barrier()
        build_program.stats = (dict(c.n_ops), c.n_wait)
    return nc


def _prep_weights(inp, layers):
    NL = len(layers)
    f32 = np.float32
    L = list(layers)
    w_ffn_in = np.asarray(inp["w_ffn_in"], f32)[L]
    w_ffn_out = np.asarray(inp["w_ffn_out"], f32)[L]
    w_in = np.asarray(inp["w_in"], f32)[L]
    w_out = np.asarray(inp["w_out"], f32)[L]
    t = w_ffn_in.reshape(NL, 2, 8, 128, 2, NJ, 128)
    wi = np.ascontiguousarray(t.transpose(0, 1, 5, 3, 2, 4, 6)).reshape(NL, 2, NJ, 128, 2048)
    t = w_ffn_out.reshape(NL, 2, NJ, 128, 8, 128)
    wo = np.ascontiguousarray(t.transpose(0, 1, 4, 3, 2, 5)).reshape(NL, 2, 8, 128, 2816)
    colbase = [0, 128, 256, 384, 512, 640, 768, 896, 1544, 1672, 1800, 1928, 2056, 2184, 2312, 2440]
    chunks = np.stack([w_in[:, :, b:b + 128] for b in colbase], axis=1)
    t = chunks.reshape(NL, 8, 2, 8, 128, 128)
    wq = np.ascontiguousarray(t.transpose(0, 1, 4, 3, 2, 5)).reshape(NL, 8, 128, 2048)
    t = w_in[:, :, 1024:1536].reshape(NL, 8, 128, 512)
    wv = np.ascontiguousarray(t.transpose(0, 2, 1, 3)).reshape(NL, 128, 4096)
    t = w_in[:, :, 1536:1544].reshape(NL, 8, 128, 8)
    wf = np.ascontiguousarray(t.transpose(0, 2, 1, 3)).reshape(NL, 128, 64)
    t = w_out[:, 512:1024, :].reshape(NL, 4, 128, 1024)
    wor = np.ascontiguousarray(t.transpose(0, 2, 1, 3)).reshape(NL, 128, 4096)
    t = w_out[:, 0:512, :].reshape(NL, 8, 64, 1024)
    woa = np.ascontiguousarray(t.transpose(0, 2, 1, 3)).reshape(NL, 64, 8192)
    wa = np.asarray(inp["w_rg_a"], f32)[L]
    wx = np.asarray(inp["w_rg_x"], f32)[L]
    wbd = np.zeros((NL, 128, 4, 2, 128), f32)
    for cch in range(4):
        for half in range(2):
            rs = slice(half * 64, half * 64 + 64)
            wbd[:, rs, cch, 0, rs] = wa[:, 2 * cch + half]
            wbd[:, rs, cch, 1, rs] = wx[:, 2 * cch + half]
    wbd = wbd.reshape(NL, 128, 1024)
    cst = np.zeros((128, NL * PL + 8), f32)
    ng = np.asarray(inp["norm_g"], f32)[L]
    cw = np.asarray(inp["conv_w"], f32)[L]
    for li in range(NL):
        b = li * PL
        cst[:, b:b + 24] = ng[li].reshape(3, 8, 128).transpose(2, 0, 1).reshape(128, 24)
        cst[:, b + 24:b + 40] = cw[li].reshape(4, 4, 128).transpose(2, 0, 1).reshape(128, 16)
        cst[:, b + 40:b + 44] = np.asarray(inp["conv_b"], f32)[L[li]].reshape(4, 128).T
        cst[:, b + 44:b + 48] = np.asarray(inp["b_rg_a"], f32)[L[li]].reshape(4, 128).T
        cst[:, b + 48:b + 52] = np.asarray(inp["b_rg_x"], f32)[L[li]].reshape(4, 128).T
        cst[:, b + 52:b + 56] = np.asarray(inp["rg_lambda"], f32)[L[li]].reshape(4, 128).T
        cst[:, b + 56:b + 64] = np.asarray(inp["b_f"], f32)[L[li]][None, :]
    cst[:, NL * PL:NL * PL + 8] = np.asarray(inp["final_g"], f32).reshape(8, 128).T
    cmat = np.zeros((128, 384), f32)
    cmat[:, 0:128] = np.eye(128, dtype=f32)
    kk = np.arange(128)[:, None]
    qq = np.arange(128)[None, :]
    cmat[:, 128:256] = np.where(kk > qq, MASKVAL, 0.0)
    cmat[:, 256:384] = (kk <= qq).astype(f32)
    return dict(cst=cst, cmat=cmat, wi=wi, wo=wo, wq=wq, wv=wv, wf=wf, wor=wor, woa=woa, wbd=wbd)


FUSED = True
_PROGS = {}


def _prog(NL, final):
    key = (NL, final)
    if key not in _PROGS:
        _PROGS[key] = build_program(NL, final)
    return _PROGS[key]


def kernel(**inputs):
    x = np.asarray(inputs["x"], np.float32)
    xT = [np.ascontiguousarray(x[b].T) for b in range(NB)]
    if FUSED:
        groups = [list(range(DEPTH))]
    else:
        groups = [[l] for l in range(DEPTH)]
    for gi, layers in enumerate(groups):
        final = gi == len(groups) - 1
        w = _prep_weights(inputs, layers)
        nc = _prog(len(layers), final)
        in_maps = [dict(w, xT=xT[b]) for b in range(NB)]
        res = run_bass_kernel_spmd(nc, in_maps, core_ids=list(range(NB)))
        xT = [np.asarray(res.results[b]["oT"], np.float32) for b in range(NB)]
    out = np.stack([xT[b].T for b in range(NB)], axis=0)
    return np.ascontiguousarray(out.astype(np.float32))
```

```python
import contextlib
import math
import numpy as np
import concourse.bass as bass
import concourse.mybir as mybir
from concourse.bass_utils import run_bass_kernel_spmd

F32 = mybir.dt.float32
BF16 = mybir.dt.bfloat16
AF = mybir.ActivationFunctionType
ALU = mybir.AluOpType

D = 1024
S = 4096
NB = 8
DEPTH = 4
DFF = 2816
NJ = DFF // 128
TB = 512
NTB = S // TB
EPS = 1e-6
PL = 64
GELU_K = 0.044715
GELU_S = 2.0 * math.sqrt(2.0 / math.pi)
MASKVAL = -30000.0


class Ctx:
    COMPUTE = ("pe", "act", "dve", "pool")

    def __init__(self, nc, stack):
        self.nc = nc
        self.stack = stack
        self.eng = {"pe": nc.tensor, "act": nc.scalar, "dve": nc.vector, "pool": nc.gpsimd, "sp": nc.sync}
        self.sems = {}
        self.cnt = {}
        for e in self.COMPUTE:
            self.sems[e] = stack.enter_context(nc.semaphore("s_" + e))
            self.cnt[e] = 0
        self.waited = {e: {} for e in self.eng}
        self.tw = {}
        self.tr = {}
        self.n_wait = 0
        self.n_ops = {e: 0 for e in self.eng}

    def _deps(self, reads, writes):
        deps = {}
        for k in reads:
            for s, v in self.tw.get(k, {}).items():
                if deps.get(s, 0) < v:
                    deps[s] = v
        for k in writes:
            for s, v in self.tw.get(k, {}).items():
                if deps.get(s, 0) < v:
                    deps[s] = v
            for s, v in self.tr.get(k, {}).items():
                if deps.get(s, 0) < v:
                    deps[s] = v
        return deps

    def _emit_waits(self, ename, deps):
        E = self.eng[ename]
        wd = self.waited[ename]
        for s, v in deps.items():
            if s == "pe" and ename == "pe":
                continue
            if wd.get(s, 0) >= v:
                continue
            if s == "pe" and v > self.cnt["pe"]:
                raise RuntimeError("wait on an un-signalled PE op")
            E.wait_ge(self.sems[s], v)
            wd[s] = v
            self.n_wait += 1

    def _record(self, semkey, val, reads, writes):
        for k in reads:
            d = self.tr.setdefault(k, {})
            if d.get(semkey, 0) < val:
                d[semkey] = val
        for k in writes:
            d = self.tw.setdefault(k, {})
            if d.get(semkey, 0) < val:
                d[semkey] = val

    def op(self, ename, fn, reads=(), writes=(), signal=True):
        reads = list(reads) + ["*"]
        self._emit_waits(ename, self._deps(reads, writes))
        inst = fn(self.eng[ename])
        self.n_ops[ename] += 1
        if signal:
            self.cnt[ename] += 1
            inst.then_inc(self.sems[ename], 1)
            val = self.cnt[ename]
        else:
            assert ename == "pe"
            val = self.cnt[ename] + 1
        self._record(ename, val, reads, writes)
        return inst

    def dma(self, qname, out, in_, reads, writes, semkey, **kw):
        sk = ("dma", semkey)
        if sk not in self.sems:
            self.sems[sk] = self.stack.enter_context(self.nc.semaphore("d%d" % len(self.sems)))
            self.cnt[sk] = 0
        reads = list(reads) + ["*"]
        self._emit_waits(qname, self._deps(reads, writes))
        inst = self.eng[qname].dma_start(out=out, in_=in_, **kw)
        self.n_ops[qname] += 1
        self.cnt[sk] += 16
        inst.then_inc(self.sems[sk], 16)
        self._record(sk, self.cnt[sk], reads, writes)
        return inst

    def wait_all(self, ename, keys):
        deps = {}
        for k in keys:
            for d in (self.tw.get(k, {}), self.tr.get(k, {})):
                for s, v in d.items():
                    if deps.get(s, 0) < v:
                        deps[s] = v
        if ename == "pe":
            deps.pop("pe", None)
        self._emit_waits(ename, deps)

    def barrier(self):
        for e in ("pe", "act", "dve", "pool", "sp"):
            self.wait_all(e, ["*"])

    def mm(self, out, lhsT, rhs, start, stop, reads, writes, signal=False):
        return self.op("pe", lambda e: e.matmul(out, lhsT=lhsT, rhs=rhs, start=start, stop=stop),
                       reads=reads, writes=writes, signal=signal)

    def act(self, out, in_, func, reads, writes, **kw):
        return self.op("act", lambda e: e.activation(out=out, in_=in_, func=func, **kw), reads=reads, writes=writes)


class WStream:
    def __init__(self, c, name, buf, nslots, items):
        self.c, self.name, self.buf, self.nslots, self.items = c, name, buf, nslots, items
        self.issued = 0
        self.consumed = 0
        for _ in range(nslots):
            self._issue()

    def _issue(self):
        if self.issued < len(self.items):
            src, keys = self.items[self.issued]
            s = self.issued % self.nslots
            self.c.dma("sp", self.buf[:, s, :], src, reads=keys, writes=[(self.name, s)], semkey=(self.name, s))
            self.issued += 1

    def next(self):
        s = self.consumed % self.nslots
        self.consumed += 1
        return s, (self.name, s)

    def release(self, n=1):
        for _ in range(n):
            self._issue()


def build_program(NL, final):
    nc = bass.Bass("TRN2", target_bir_lowering=False)
    NC = NL * PL + 8

    def din(name, shape, dt=F32):
        return nc.dram_tensor(name, shape, dt, kind="ExternalInput").ap()

    def dint(name, shape, dt=BF16):
        return nc.dram_tensor(name, shape, dt, kind="Internal").ap()

    xT = din("xT", [D, S])
    cst_d = din("cst", [128, NC])
    cmat_d = din("cmat", [128, 384])
    wi_d = din("wi", [NL, 2, NJ, 128, 2048])
    wo_d = din("wo", [NL, 2, 8, 128, 2816])
    wq_d = din("wq", [NL, 8, 128, 2048])
    wv_d = din("wv", [NL, 128, 4096])
    wf_d = din("wf", [NL, 128, 64])
    wor_d = din("wor", [NL, 128, 4096])
    woa_d = din("woa", [NL, 128, 4096])
    wbd_d = din("wbd", [NL, 128, 1024])
    oT = nc.dram_tensor("oT", [D, S], F32, kind="ExternalOutput").ap()

    wi_b = dint("wi_b", [NL, 2, NJ, 128, 2048])
    wo_b = dint("wo_b", [NL, 2, 8, 128, 2816])
    wq_b = dint("wq_b", [NL, 8, 128, 2048])
    wv_b = dint("wv_b", [NL, 128, 4096])
    wf_b = dint("wf_b", [NL, 128, 64])
    wor_b = dint("wor_b", [NL, 128, 4096])
    woa_b = dint("woa_b", [NL, 128, 4096])
    wbd_b = dint("wbd_b", [NL, 128, 1024])
    qT_s = dint("qT_s", [512, S])
    kT_s = dint("kT_s", [512, S])
    v_s = dint("v_s", [4, 128, 32, 2, 64])

    with contextlib.ExitStack() as st:
        c = Ctx(nc, st)

        uid = [0]

        def sb(stack, name, shape, dt):
            uid[0] += 1
            return stack.enter_context(nc.sbuf_tensor("%s_%d" % (name, uid[0]), shape, dt))

        X = sb(st, "X", [128, 8, S], F32)
        cst = sb(st, "cst_sb", [128, NC], F32)
        cmat = sb(st, "cmat_sb", [128, 384], F32)
        ident_bf = sb(st, "ident_bf", [128, 128], BF16)
        mask_bf = sb(st, "mask_bf", [128, 128], BF16)
        ones_bf = sb(st, "ones_bf", [128, 128], BF16)
        ones_f = sb(st, "ones_f", [128, 128], F32)
        spc = sb(st, "spc", [128, NL * 8], F32)
        clog = sb(st, "clog", [128, 32, 8], F32)
        cref = sb(st, "cref", [128, NTB, 8], F32)
        PS = [st.enter_context(nc.psum_tensor("ps%d" % b, [128, 512], F32)) for b in range(8)]

        def psk(b):
            return ("ps", b)

        def xk(fc, tb):
            return ("X", fc, tb)

        def blk(tb):
            return slice(tb * TB, (tb + 1) * TB)

        def cast(dst, src, key, tag, maxrows=1024):
            rows = dst.shape[0]
            r0 = 0
            i = 0
            while r0 < rows:
                r1 = min(rows, r0 + maxrows)
                c.dma("pool", dst[r0:r1, :], src[r0:r1, :], reads=[], writes=[key], semkey=("cast", tag, i))
                r0 = r1
                i += 1

        def cast_ffn(l, i):
            for half in range(2):
                js = slice(half * 11, half * 11 + 11)
                cast(wi_b[l, i, js].rearrange("j p e -> (j p) e"), wi_d[l, i, js].rearrange("j p e -> (j p) e"),
                     ("wi_b", l, i, half), ("wi", i, half), maxrows=704)
            for half in range(2):
                fs = slice(half * 4, half * 4 + 4)
                cast(wo_b[l, i, fs].rearrange("f p (a e) -> (f p a) e", a=2),
                     wo_d[l, i, fs].rearrange("f p (a e) -> (f p a) e", a=2),
                     ("wo_b", l, i, half), ("wo", i, half), maxrows=1024)

        def cast_mix(l):
            k = ("mixw", l)
            cast(wq_b[l].rearrange("t p e -> (t p) e"), wq_d[l].rearrange("t p e -> (t p) e"), k, ("wq",), maxrows=512)
            cast(wv_b[l].rearrange("p (a e) -> (p a) e", a=2), wv_d[l].rearrange("p (a e) -> (p a) e", a=2), k, ("wv",))
            cast(wf_b[l], wf_d[l], k, ("wf",))
            cast(wbd_b[l], wbd_d[l], k, ("wbd",))
            cast(wor_b[l].rearrange("p (a e) -> (p a) e", a=2), wor_d[l].rearrange("p (a e) -> (p a) e", a=2), k, ("wor",))
            cast(woa_b[l].rearrange("p (a e) -> (p a) e", a=2), woa_d[l].rearrange("p (a e) -> (p a) e", a=2), k, ("woa",))

        def cast_layer(l):
            cast_ffn(l, 0)
            cast_mix(l)
            cast_ffn(l, 1)

        cast_layer(0)
        c.dma("sp", cst[:, :], cst_d[:, :], reads=[], writes=["cst"], semkey="cst")
        c.dma("sp", cmat[:, :], cmat_d[:, :], reads=[], writes=["cmat"], semkey="cmat")
        for fc in range(8):
            c.dma("sp", X[:, fc, :], xT[fc * 128:(fc + 1) * 128, :], reads=[],
                  writes=[xk(fc, tb) for tb in range(NTB)], semkey=("Xld", fc))
        c.op("dve", lambda e: e.memset(ones_bf[:, :], 1.0), writes=["ones_bf"])
        c.op("dve", lambda e: e.memset(ones_f[:, :], 1.0), writes=["ones_f"])
        c.act(ident_bf[:, :], cmat[:, 0:128], AF.Copy, reads=["cmat"], writes=["ident_bf"])
        c.act(mask_bf[:, :], cmat[:, 128:256], AF.Copy, reads=["cmat"], writes=["mask_bf"])
        tri_f = cmat[:, 256:384]
        for l in range(NL):
            lam = cst[:, l * PL + 52:l * PL + 56]
            c.act(spc[:, l * 8:l * 8 + 4], lam, AF.Sigmoid, reads=["cst"], writes=["spc"])
            c.act(spc[:, l * 8:l * 8 + 4], spc[:, l * 8:l * 8 + 4], AF.Ln, reads=["spc"], writes=["spc"])
            c.op("dve", lambda e: e.tensor_scalar(out=spc[:, l * 8 + 4:l * 8 + 8], in0=spc[:, l * 8:l * 8 + 4],
                                                  scalar1=16.0, scalar2=None, op0=ALU.mult),
                 reads=["spc"], writes=["spc"])
            c.op("dve", lambda e: e.tensor_scalar(out=spc[:, l * 8:l * 8 + 4], in0=spc[:, l * 8:l * 8 + 4],
                                                  scalar1=8.0, scalar2=None, op0=ALU.mult),
                 reads=["spc"], writes=["spc"])

        def rmsnorm_stats(tb, sq, rs_tmp, rstd, ps_stat):
            for fc in range(8):
                b = fc % 2
                c.act(sq[:, b, :], X[:, fc, blk(tb)], AF.Square, reads=[xk(fc, tb)], writes=[("sq", b)])
                c.mm(PS[ps_stat][:, :], ones_bf[:, :], sq[:, b, :], start=(fc == 0), stop=(fc == 7),
                     reads=[("sq", b), "ones_bf"], writes=[psk(ps_stat)], signal=True)
            c.act(rs_tmp[:, :], PS[ps_stat][:, :], AF.Sqrt, reads=[psk(ps_stat)], writes=["rs_tmp"],
                  scale=1.0 / D, bias=EPS)
            c.op("dve", lambda e: e.reciprocal(out=rstd[:, :], in_=rs_tmp[:, :]), reads=["rs_tmp"], writes=["rstd"])

        def rmsnorm_apply(tb, gbase, rstd, xn):
            for fc in range(8):
                c.op("dve", lambda e: e.scalar_tensor_tensor(out=xn[:, fc, :], in0=X[:, fc, blk(tb)],
                                                             scalar=cst[:, gbase + fc:gbase + fc + 1], in1=rstd[:, :],
                                                             op0=ALU.mult, op1=ALU.mult),
                     reads=[xk(fc, tb), "rstd", "cst"], writes=[("xn", fc)])

        def ffn_phase(l, i, gbase):
            c.barrier()
            with contextlib.ExitStack() as ph:
                xn = sb(ph, "f_xn", [128, 8, TB], BF16)
                sq = sb(ph, "f_sq", [128, 2, TB], BF16)
                rs_tmp = sb(ph, "f_rs", [128, TB], F32)
                rstd = sb(ph, "f_rstd", [128, TB], F32)
                h = sb(ph, "f_h", [128, NJ, TB], BF16)
                sg = sb(ph, "f_sg", [128, 2, TB], F32)
                wib = sb(ph, "f_wi", [128, 4, 2048], BF16)
                wob = sb(ph, "f_wo", [128, 3, 2816], BF16)
                wi_items = [(wi_b[l, i, j], [("wi_b", l, i, j // 11)]) for _ in range(NTB) for j in range(NJ)]
                wo_items = [(wo_b[l, i, fo], [("wo_b", l, i, fo // 4)]) for _ in range(NTB) for fo in range(8)]
                wis = WStream(c, "wi_s", wib, 4, wi_items)
                wos = WStream(c, "wo_s", wob, 3, wo_items)
                for tb in range(NTB):
                    rmsnorm_stats(tb, sq, rs_tmp, rstd, 6)
                    rmsnorm_apply(tb, gbase, rstd, xn)
                    for j in range(NJ):
                        s, skey = wis.next()
                        wt = wib[:, s, :].rearrange("p (k g m) -> p k g m", k=8, g=2)
                        pg, pu = j % 2, 2 + j % 2
                        for gu, pb in ((0, pg), (1, pu)):
                            for kc in range(8):
                                c.mm(PS[pb][:, :], wt[:, kc, gu, :], xn[:, kc, :], start=(kc == 0), stop=(kc == 7),
                                     reads=[skey, ("xn", kc)], writes=[psk(pb)], signal=(kc == 7))
                        wis.release()
                        c.act(sg[:, j % 2, :], PS[pg][:, :], AF.Silu, reads=[psk(pg)], writes=[("sg", j % 2)])
                        c.op("dve", lambda e: e.tensor_tensor(out=h[:, j, :], in0=PS[pu][:, :], in1=sg[:, j % 2, :],
                                                              op=ALU.mult),
                             reads=[psk(pu), ("sg", j % 2)], writes=[("h", j)])
                    for fo in range(8):
                        s, skey = wos.next()
                        wt = wob[:, s, :].rearrange("p (j m) -> p j m", j=NJ)
                        pb = 4 + fo % 2
                        for j in range(NJ):
                            c.mm(PS[pb][:, :], wt[:, j, :], h[:, j, :], start=(j == 0), stop=(j == NJ - 1),
                                 reads=[skey, ("h", j)], writes=[psk(pb)], signal=(j == NJ - 1))
                        wos.release()
                        c.op("dve", lambda e: e.scalar_tensor_tensor(out=X[:, fo, blk(tb)], in0=PS[pb][:, :], scalar=0.5,
                                                                     in1=X[:, fo, blk(tb)], op0=ALU.mult, op1=ALU.add),
                             reads=[psk(pb), xk(fo, tb)], writes=[xk(fo, tb)])
                c.barrier()

        def m1_phase(l):
            c.barrier()
            cb = l * PL
            with contextlib.ExitStack() as ph:
                xn = sb(ph, "m_xn", [128, 8, TB], BF16)
                sq = sb(ph, "m_sq", [128, 2, TB], BF16)
                rs_tmp = sb(ph, "m_rs", [128, TB], F32)
                rstd = sb(ph, "m_rstd", [128, TB], F32)
                wsb = sb(ph, "m_ws", [128, 3, 2048], BF16)
                wf_sb = sb(ph, "m_wf", [128, 64], BF16)
                wbd_sb = sb(ph, "m_wbd", [128, 1024], BF16)
                wor_sb = sb(ph, "m_wor", [128, 4096], BF16)
                stq = sb(ph, "m_stq", [128, 4, TB], BF16)
                stk = sb(ph, "m_stk", [128, 4, TB], BF16)
                stv = sb(ph, "m_stv", [128, 4, 8, 64], BF16)
                xr_sb = sb(ph, "m_xr", [128, 4, TB + 3], F32)
                hcar = sb(ph, "m_hcar", [128, 4], F32)
                carry = sb(ph, "m_carry", [128, 8], F32)
                fb = sb(ph, "m_fb", [128, 8], F32)
                T = [sb(ph, "m_t%d" % k, [128, TB], F32) for k in range(6)]
                xc_bf = sb(ph, "m_xcbf", [128, TB], BF16)
                yrec = sb(ph, "m_yrec", [128, 4, TB], BF16)
                mk = ("mixw", l)
                c.dma("sp", wf_sb[:, :], wf_b[l], reads=[mk], writes=["wf_sb"], semkey="wf_sb")
                c.dma("sp", wbd_sb[:, :], wbd_b[l], reads=[mk], writes=["wbd_sb"], semkey="wbd_sb")
                c.dma("sp", wor_sb[:, :], wor_b[l], reads=[mk], writes=["wor_sb"], semkey="wor_sb")
                items = []
                for _ in range(NTB):
                    for t in range(8):
                        items.append((wq_b[l, t], [mk]))
                    items.append((wv_b[l, :, 0:2048], [mk]))
                    items.append((wv_b[l, :, 2048:4096], [mk]))
                ws = WStream(c, "m_ws", wsb, 3, items)
                c.op("dve", lambda e: e.memset(xr_sb[:, :, :], 0.0), writes=[("xr", k) for k in range(4)])
                c.op("dve", lambda e: e.memset(hcar[:, :], 0.0), writes=["hcar"])
                c.op("dve", lambda e: e.memset(carry[:, :], 0.0), writes=["carry"])
                wbd_v = wbd_sb[:, :].rearrange("p (c g m) -> p c g m", c=4, g=2)
                wor_v = wor_sb[:, :].rearrange("p (k f) -> p k f", k=4)
                wf_v = wf_sb[:, :].rearrange("p (k h) -> p k h", k=8)
                rot = [0]

                def nextbank():
                    b = rot[0] % 4
                    rot[0] += 1
                    return b

                def col(idx):
                    return cst[:, idx:idx + 1]

                for tb in range(NTB):
                    rmsnorm_stats(tb, sq, rs_tmp, rstd, 6)
                    rmsnorm_apply(tb, cb + 8, rstd, xn)
                    gr_bank = {}
                    for t in range(8):
                        s, skey = ws.next()
                        wt = wsb[:, s, :].rearrange("p (k g m) -> p k g m", k=8, g=2)
                        for cc in range(2):
                            ch = 2 * t + cc
                            pb = nextbank()
                            for kc in range(8):
                                c.mm(PS[pb][:, :], wt[:, kc, cc, :], xn[:, kc, :], start=(kc == 0), stop=(kc == 7),
                                     reads=[skey, ("xn", kc)], writes=[psk(pb)], signal=(kc == 7))
                            if ch < 4:
                                c.act(stq[:, ch, :], PS[pb][:, :], AF.Copy, reads=[psk(pb)], writes=["stq"])
                            elif ch < 8:
                                c.act(stk[:, ch - 4, :], PS[pb][:, :], AF.Copy, reads=[psk(pb)], writes=["stk"])
                            elif ch < 12:
                                k = ch - 8
                                c.act(xr_sb[:, k, 3:TB + 3], PS[pb][:, :], AF.Copy, reads=[psk(pb)], writes=[("xr", k)])
                            else:
                                k = ch - 12
                                rec_chunk(l, k, pb, xr_sb, hcar, T, xc_bf, yrec, wbd_v, nextbank, col)
                        ws.release()
                    for fo in range(8):
                        pb = 4 + fo % 2
                        for kc in range(4):
                            c.mm(PS[pb][:, :], wor_v[:, kc, fo * 128:(fo + 1) * 128], yrec[:, kc, :],
                                 start=(kc == 0), stop=(kc == 3), reads=["wor_sb", ("yrec", kc)], writes=[psk(pb)],
                                 signal=(kc == 3))
                        c.op("dve", lambda e: e.tensor_tensor(out=X[:, fo, blk(tb)], in0=PS[pb][:, :],
                                                              in1=X[:, fo, blk(tb)], op=ALU.add),
                             reads=[psk(pb), xk(fo, tb)], writes=[xk(fo, tb)])
                    sA, kA = ws.next()
                    sB, kB = ws.next()
                    for tt in range(4):
                        pb = 4 + tt % 2
                        tok = slice(tt * 128, (tt + 1) * 128)
                        for kc in range(8):
                            sl, kk = (sA, kA) if kc < 4 else (sB, kB)
                            wv_t = wsb[:, sl, :].rearrange("p (k n) -> p k n", k=4)
                            c.mm(PS[pb][:, :], xn[:, kc, tok], wv_t[:, kc % 4, :], start=(kc == 0), stop=(kc == 7),
                                 reads=[kk, ("xn", kc)], writes=[psk(pb)], signal=(kc == 7))
                        c.act(stv[:, tt, :, :], PS[pb][:, :].rearrange("p (h d) -> p h d", h=8), AF.Copy,
                              reads=[psk(pb)], writes=["stv"])
                        for kc in range(8):
                            c.mm(PS[7][:, 0:8], xn[:, kc, tok], wf_v[:, kc, :], start=(kc == 0), stop=(kc == 7),
                                 reads=["wf_sb", ("xn", kc)], writes=[psk(7)], signal=(kc == 7))
                        bf_bc = cst[:, cb + 56:cb + 64]
                        c.op("dve", lambda e: e.tensor_tensor(out=fb[:, :], in0=PS[7][:, 0:8], in1=bf_bc, op=ALU.add),
                             reads=[psk(7), "cst"], writes=["fb"])
                        c.act(fb[:, :], fb[:, :], AF.Sigmoid, reads=["fb"], writes=["fb"])
                        c.act(fb[:, :], fb[:, :], AF.Ln, reads=["fb"], writes=["fb"])
                        c.mm(PS[7][:, 8:16], tri_f, fb[:, :], start=True, stop=True, reads=["cmat", "fb"],
                             writes=[psk(7)], signal=True)
                        c.mm(PS[7][:, 16:24], ones_f[:, :], fb[:, :], start=True, stop=True, reads=["ones_f", "fb"],
                             writes=[psk(7)], signal=True)
                        n = tb * 4 + tt
                        c.op("dve", lambda e: e.tensor_tensor(out=clog[:, n, :], in0=PS[7][:, 8:16], in1=carry[:, :],
                                                              op=ALU.add),
                             reads=[psk(7), "carry"], writes=["clog"])
                        c.op("dve", lambda e: e.tensor_tensor(out=carry[:, :], in0=PS[7][:, 16:24], in1=carry[:, :],
                                                              op=ALU.add),
                             reads=[psk(7), "carry"], writes=["carry"])
                        if tt == 1:
                            c.op("dve", lambda e: e.tensor_copy(out=cref[:, tb, :], in_=carry[:, :]),
                                 reads=["carry"], writes=["cref"])
                    ws.release(2)
                    qk_keys = [("qT_s", p) for p in range(4)]
                    c.dma("sp", qT_s.rearrange("(c p) t -> p c t", p=128)[:, :, blk(tb)], stq[:, :, :],
                          reads=["stq"], writes=qk_keys, semkey="stq")
                    c.dma("sp", kT_s.rearrange("(c p) t -> p c t", p=128)[:, :, blk(tb)], stk[:, :, :],
                          reads=["stk"], writes=[("kT_s", p) for p in range(4)], semkey="stk")
                    for p in range(4):
                        c.dma("sp", v_s[p, :, tb * 4:(tb + 1) * 4, :, :], stv[:, :, 2 * p:2 * p + 2, :],
                              reads=["stv"], writes=[("v_s", p)], semkey="stv")
                c.barrier()

        def rec_chunk(l, k, pb_gr, xr_sb, hcar, T, xc_bf, yrec, wbd_v, nextbank, col):
            cb = l * PL
            acc, tr, ti, ta, tth, tx = T
            xk_ = ("xr", k)
            c.op("dve", lambda e: e.tensor_scalar(out=acc[:, :], in0=xr_sb[:, k, 0:TB], scalar1=col(cb + 24 + 0 * 4 + k),
                                                  scalar2=col(cb + 40 + k), op0=ALU.mult, op1=ALU.add),
                 reads=[xk_, "cst"], writes=["t_acc"])
            for tap in range(1, 4):
                c.op("dve", lambda e: e.scalar_tensor_tensor(out=acc[:, :], in0=xr_sb[:, k, tap:tap + TB],
                                                             scalar=col(cb + 24 + tap * 4 + k), in1=acc[:, :],
                                                             op0=ALU.mult, op1=ALU.add),
                     reads=[xk_, "cst", "t_acc"], writes=["t_acc"])
            c.act(xr_sb[:, k, 0:3], xr_sb[:, k, TB:TB + 3], AF.Copy, reads=[xk_], writes=[xk_])
            c.act(xc_bf[:, :], acc[:, :], AF.Copy, reads=["t_acc"], writes=["xc_bf"])
            pa = nextbank()
            c.mm(PS[pa][:, :], wbd_v[:, k, 0, :], xc_bf[:, :], start=True, stop=True, reads=["wbd_sb", "xc_bf"],
                 writes=[psk(pa)], signal=True)
            px = nextbank()
            c.mm(PS[px][:, :], wbd_v[:, k, 1, :], xc_bf[:, :], start=True, stop=True, reads=["wbd_sb", "xc_bf"],
                 writes=[psk(px)], signal=True)
            c.act(tr[:, :], PS[pa][:, :], AF.Sigmoid, reads=[psk(pa), "cst"], writes=["t_r"], bias=col(cb + 44 + k))
            c.act(ti[:, :], PS[px][:, :], AF.Sigmoid, reads=[psk(px), "cst"], writes=["t_i"], bias=col(cb + 48 + k))
            sp1 = spc[:, l * 8 + k:l * 8 + k + 1]
            sp2 = spc[:, l * 8 + 4 + k:l * 8 + 4 + k + 1]
            c.act(ta[:, :], tr[:, :], AF.Exp, reads=["t_r", "spc"], writes=["t_a"], scale=sp1)
            c.act(tth[:, :], tr[:, :], AF.Tanh, reads=["t_r", "spc"], writes=["t_th"], scale=sp1)
            c.act(tr[:, :], tr[:, :], AF.Exp, reads=["t_r", "spc"], writes=["t_r"], scale=sp2)
            c.op("dve", lambda e: e.scalar_tensor_tensor(out=tth[:, :], in0=tr[:, :], scalar=1.0, in1=tth[:, :],
                                                         op0=ALU.add, op1=ALU.mult),
                 reads=["t_r", "t_th"], writes=["t_th"])
            c.act(tth[:, :], tth[:, :], AF.Sqrt, reads=["t_th"], writes=["t_th"], scale=-1.0)
            c.op("dve", lambda e: e.tensor_tensor(out=ti[:, :], in0=ti[:, :], in1=acc[:, :], op=ALU.mult),
                 reads=["t_i", "t_acc"], writes=["t_i"])
            c.op("dve", lambda e: e.tensor_tensor(out=ti[:, :], in0=ti[:, :], in1=tth[:, :], op=ALU.mult),
                 reads=["t_i", "t_th"], writes=["t_i"])
            c.op("dve", lambda e: e.tensor_tensor_scan(out=tr[:, :], data0=ta[:, :], data1=ti[:, :],
                                                       initial=hcar[:, k:k + 1], op0=ALU.mult, op1=ALU.add),
                 reads=["t_a", "t_i", "hcar", "t_r"], writes=["t_r"])
            c.act(hcar[:, k:k + 1], tr[:, TB - 1:TB], AF.Copy, reads=["t_r"], writes=["hcar"])
            c.act(tx[:, :], PS[pb_gr][:, :], AF.Square, reads=[psk(pb_gr)], writes=["t_x"], scale=math.sqrt(GELU_K))
            c.op("dve", lambda e: e.scalar_tensor_tensor(out=tx[:, :], in0=tx[:, :], scalar=1.0, in1=PS[pb_gr][:, :],
                                                         op0=ALU.add, op1=ALU.mult),
                 reads=["t_x", psk(pb_gr)], writes=["t_x"])
            c.act(tx[:, :], tx[:, :], AF.Sigmoid, reads=["t_x"], writes=["t_x"], scale=GELU_S)
            c.op("dve", lambda e: e.tensor_tensor(out=tx[:, :], in0=PS[pb_gr][:, :], in1=tx[:, :], op=ALU.mult),
                 reads=["t_x", psk(pb_gr)], writes=["t_x"])
            c.op("dve", lambda e: e.tensor_tensor(out=yrec[:, k, :], in0=tx[:, :], in1=tr[:, :], op=ALU.mult),
                 reads=["t_x", "t_r"], writes=[("yrec", k)])

        def m2_phase(l):
            c.barrier()
            if l + 1 < NL:
                cast_layer(l + 1)
            mk = ("mixw", l)
            with contextlib.ExitStack() as ph:
                kT_sb = sb(ph, "a_kT", [128, 2, S], BF16)
                v_sb = sb(ph, "a_v", [128, 2, 32, 2, 128], BF16)
                q_sb = sb(ph, "a_q", [128, 2, 2, TB], BF16)
                P_sb = sb(ph, "a_P", [128, 4, TB], BF16)
                bias_sb = sb(ph, "a_bias", [128, 2, 2, 32], F32)
                r_sb = sb(ph, "a_r", [128, TB], F32)
                rs_sb = sb(ph, "a_rs", [128, TB], F32)
                y_sb = sb(ph, "a_y", [128, 2, TB], BF16)
                woa_sb = sb(ph, "a_woa", [128, 4096], BF16)
                woa_v = woa_sb[:, :].rearrange("p (h f) -> p h f", h=4)
                c.dma("sp", woa_sb[:, :], woa_b[l], reads=[mk], writes=["woa_sb"], semkey="woa_sb")
                c.op("dve", lambda e: e.memset(v_sb[:, :, :, :, :], 1.0), writes=[("v_sb", 0), ("v_sb", 1)])
                c.op("dve", lambda e: e.memset(q_sb[:, :, :, :], 0.0), writes=[("q_sb", 0), ("q_sb", 1)])

                def load_pair(hp):
                    b = hp % 2
                    c.dma("sp", kT_sb[:, b, :], kT_s[hp * 128:(hp + 1) * 128, :], reads=[("kT_s", hp)],
                          writes=[("kT_sb", b)], semkey=("kT_sb", b))
                    c.dma("sp", v_sb[:, b, :, 0, 0:64], v_s[hp, :, :, 0, :], reads=[("v_s", hp)],
                          writes=[("v_sb", b)], semkey=("v_sb", b))
                    c.dma("sp", v_sb[:, b, :, 1, 64:128], v_s[hp, :, :, 1, :], reads=[("v_s", hp)],
                          writes=[("v_sb", b)], semkey=("v_sb", b))

                def load_q(qi):
                    hp, qb = seq[qi]
                    b = qi % 2
                    r0 = hp * 128
                    c.dma("sp", q_sb[0:64, b, 0, :], qT_s[r0:r0 + 64, blk(qb)], reads=[("qT_s", hp)],
                          writes=[("q_sb", b)], semkey=("q_sb", b))
                    c.dma("sp", q_sb[64:128, b, 1, :], qT_s[r0 + 64:r0 + 128, blk(qb)], reads=[("qT_s", hp)],
                          writes=[("q_sb", b)], semkey=("q_sb", b))

                seq = [(hp, qb) for hp in range(4) for qb in range(NTB)]
                blocks = []
                for qi, (hp, qb) in enumerate(seq):
                    nkt = 4 * qb + 4
                    for kt in range(nkt):
                        for hh in range(2):
                            blocks.append((qi, hp, qb, kt, hh, kt == nkt - 1 and hh == 1))
                SB = [0, 1, 6]
                LA = 2
                setup_done = [-1]

                def ensure_setup(qi):
                    while setup_done[0] < qi:
                        setup_done[0] += 1
                        q2 = setup_done[0]
                        hp, qb = seq[q2]
                        if q2 == 0:
                            load_pair(0)
                            load_q(0)
                        if qb == 1 and hp + 1 < 4:
                            load_pair(hp + 1)
                        if q2 + 1 < len(seq):
                            load_q(q2 + 1)
                        for hh in range(2):
                            hd = 2 * hp + hh
                            c.op("dve", lambda e: e.tensor_scalar(out=bias_sb[:, q2 % 2, hh, :], in0=clog[:, :, hd],
                                                                  scalar1=-1.0, scalar2=cref[:, qb, hd:hd + 1],
                                                                  op0=ALU.mult, op1=ALU.add),
                                 reads=["clog", "cref"], writes=[("bias", q2 % 2, hh)])

                def emit_qk(i):
                    qi, hp, qb, kt, hh, last = blocks[i]
                    kb, qbuf = hp % 2, qi % 2
                    n0 = max(0, kt * 128 - qb * TB)
                    N = TB - n0
                    diag = kt * 128 >= qb * TB
                    pS = SB[i % 3]
                    c.mm(PS[pS][:, 0:N], kT_sb[:, kb, kt * 128:(kt + 1) * 128], q_sb[:, qbuf, hh, n0:TB],
                         start=True, stop=(not diag), reads=[("kT_sb", kb), ("q_sb", qbuf)], writes=[psk(pS)],
                         signal=(not diag))
                    if diag:
                        c.mm(PS[pS][:, 0:128], ident_bf[:, :], mask_bf[:, :], start=False, stop=True,
                             reads=["ident_bf", "mask_bf"], writes=[psk(pS)], signal=True)

                def emit_exp_pv(i):
                    qi, hp, qb, kt, hh, last = blocks[i]
                    kb = hp % 2
                    n0 = max(0, kt * 128 - qb * TB)
                    N = TB - n0
                    pS = SB[i % 3]
                    pO = 2 + 2 * (qi % 2) + hh
                    pbuf = i % 4
                    c.act(P_sb[:, pbuf, 0:N], PS[pS][:, 0:N], AF.Exp, reads=[psk(pS), ("bias", qi % 2, hh)],
                          writes=[("P", pbuf)], scale=0.125, bias=bias_sb[:, qi % 2, hh, kt:kt + 1])
                    c.mm(PS[pO][:, n0:TB], v_sb[:, kb, kt, hh, :], P_sb[:, pbuf, 0:N], start=(kt == 0),
                         stop=(kt == 4 * qb + 3), reads=[("v_sb", kb), ("P", pbuf)], writes=[psk(pO)], signal=True)

                def emit_finalize(qi):
                    hp, qb = seq[qi]
                    yb = qi % 2
                    pA = 2 + 2 * (qi % 2)
                    pB = pA + 1
                    c.op("dve", lambda e: e.reciprocal(out=r_sb[64:128, :], in_=PS[pA][64:128, :]),
                         reads=[psk(pA)], writes=["r_hi"])
                    c.op("dve", lambda e: e.reciprocal(out=r_sb[0:64, :], in_=PS[pB][0:64, :]),
                         reads=[psk(pB)], writes=["r_lo"])
                    c.act(rs_sb[0:64, :], r_sb[64:128, :], AF.Copy, reads=["r_hi"], writes=["rs_lo"])
                    c.act(rs_sb[64:128, :], r_sb[0:64, :], AF.Copy, reads=["r_lo"], writes=["rs_hi"])
                    c.op("dve", lambda e: e.tensor_tensor(out=y_sb[0:64, yb, :], in0=PS[pA][0:64, :],
                                                          in1=rs_sb[0:64, :], op=ALU.mult),
                         reads=[psk(pA), "rs_lo"], writes=[("y", yb)])
                    c.op("dve", lambda e: e.tensor_tensor(out=y_sb[64:128, yb, :], in0=PS[pB][64:128, :],
                                                          in1=rs_sb[64:128, :], op=ALU.mult),
                         reads=[psk(pB), "rs_hi"], writes=[("y", yb)])

                def emit_outproj(qi, fo):
                    hp, qb = seq[qi]
                    yb = qi % 2
                    c.mm(PS[7][:, :], woa_v[:, hp, fo * 128:(fo + 1) * 128], y_sb[:, yb, :], start=True, stop=True,
                         reads=["woa_sb", ("y", yb)], writes=[psk(7)], signal=True)
                    c.op("dve", lambda e: e.tensor_tensor(out=X[:, fo, blk(qb)], in0=PS[7][:, :],
                                                          in1=X[:, fo, blk(qb)], op=ALU.add),
                         reads=[psk(7), xk(fo, qb)], writes=[xk(fo, qb)])

                pending = []
                nblk = len(blocks)
                for i in range(nblk + LA):
                    if i < nblk:
                        ensure_setup(blocks[i][0])
                        emit_qk(i)
                    j = i - LA
                    if j >= 0:
                        emit_exp_pv(j)
                        while pending and pending[0][0] <= j:
                            _, pq, pf = pending.pop(0)
                            emit_outproj(pq, pf)
                        if blocks[j][5]:
                            qi = blocks[j][0]
                            while pending:
                                _, pq, pf = pending.pop(0)
                                emit_outproj(pq, pf)
                            emit_finalize(qi)
                            for fo in range(8):
                                pending.append((j + 14 + 2 * fo, qi, fo))
                while pending:
                    _, pq, pf = pending.pop(0)
                    emit_outproj(pq, pf)
                c.barrier()

        for l in range(NL):
            ffn_phase(l, 0, l * PL + 0)
            m1_phase(l)
            m2_phase(l)
            ffn_phase(l, 1, l * PL + 16)

        c.barrier()
        if final:
            with contextlib.ExitStack() as ph:
                sq = sb(ph, "e_sq", [128, 2, TB], BF16)
                rs_tmp = sb(ph, "e_rs", [128, TB], F32)
                rstd = sb(ph, "e_rstd", [128, TB], F32)
                gb = NL * PL
                for tb in range(NTB):
                    rmsnorm_stats(tb, sq, rs_tmp, rstd, tb % 2)
                    for fc in range(8):
                        c.op("dve", lambda e: e.scalar_tensor_tensor(out=X[:, fc, blk(tb)], in0=X[:, fc, blk(tb)],
                                                                     scalar=cst[:, gb + fc:gb + fc + 1], in1=rstd[:, :],
                                                                     op0=ALU.mult, op1=ALU.mult),
                             reads=[xk(fc, tb), "rstd", "cst"], writes=[xk(fc, tb)])
                    c.dma("sp", oT.rearrange("(c p) t -> p c t", p=128)[:, :, blk(tb)], X[:, :, blk(tb)],
                          reads=[xk(fc, tb) for fc in range(8)], writes=["oT"], semkey=("Xout", tb))
                c.wait_all("sp", ["oT"])
        else:
            for fc in range(8):
                c.dma("sp", oT[fc * 128:(fc + 1) * 128, :], X[:, fc, :], reads=[xk(fc, tb) for tb in range(NTB)],
                      writes=["oT"], semkey=("Xout", fc))
            c.wait_all("sp", ["oT"])
        c.barrier()
        build_program.stats = (dict(c.n_ops), c.n_wait)
    return nc


def _prep_weights(inp, layers):
    NL = len(layers)
    f32 = np.float32
    L = list(layers)
    w_ffn_in = np.asarray(inp["w_ffn_in"], f32)[L]
    w_ffn_out = np.asarray(inp["w_ffn_out"], f32)[L]
    w_in = np.asarray(inp["w_in"], f32)[L]
    w_out = np.asarray(inp["w_out"], f32)[L]
    t = w_ffn_in.reshape(NL, 2, 8, 128, 2, NJ, 128)
    wi = np.ascontiguousarray(t.transpose(0, 1, 5, 3, 2, 4, 6)).reshape(NL, 2, NJ, 128, 2048)
    t = w_ffn_out.reshape(NL, 2, NJ, 128, 8, 128)
    wo = np.ascontiguousarray(t.transpose(0, 1, 4, 3, 2, 5)).reshape(NL, 2, 8, 128, 2816)
    colbase = [0, 128, 256, 384, 512, 640, 768, 896, 1544, 1672, 1800, 1928, 2056, 2184, 2312, 2440]
    chunks = np.stack([w_in[:, :, b:b + 128] for b in colbase], axis=1)
    t = chunks.reshape(NL, 8, 2, 8, 128, 128)
    wq = np.ascontiguousarray(t.transpose(0, 1, 4, 3, 2, 5)).reshape(NL, 8, 128, 2048)
    t = w_in[:, :, 1024:1536].reshape(NL, 8, 128, 512)
    wv = np.ascontiguousarray(t.transpose(0, 2, 1, 3)).reshape(NL, 128, 4096)
    t = w_in[:, :, 1536:1544].reshape(NL, 8, 128, 8)
    wf = np.ascontiguousarray(t.transpose(0, 2, 1, 3)).reshape(NL, 128, 64)
    t = w_out[:, 512:1024, :].reshape(NL, 4, 128, 1024)
    wor = np.ascontiguousarray(t.transpose(0, 2, 1, 3)).reshape(NL, 128, 4096)
    t = w_out[:, 0:512, :].reshape(NL, 4, 128, 1024)
    woa = np.ascontiguousarray(t.transpose(0, 2, 1, 3)).reshape(NL, 128, 4096)
    wa = np.asarray(inp["w_rg_a"], f32)[L]
    wx = np.asarray(inp["w_rg_x"], f32)[L]
    wbd = np.zeros((NL, 128, 4, 2, 128), f32)
    for cch in range(4):
        for half in range(2):
            rs = slice(half * 64, half * 64 + 64)
            wbd[:, rs, cch, 0, rs] = wa[:, 2 * cch + half]
            wbd[:, rs, cch, 1, rs] = wx[:, 2 * cch + half]
    wbd = wbd.reshape(NL, 128, 1024)
    cst = np.zeros((128, NL * PL + 8), f32)
    ng = np.asarray(inp["norm_g"], f32)[L]
    cw = np.asarray(inp["conv_w"], f32)[L]
    for li in range(NL):
        b = li * PL
        cst[:, b:b + 24] = ng[li].reshape(3, 8, 128).transpose(2, 0, 1).reshape(128, 24)
        cst[:, b + 24:b + 40] = cw[li].reshape(4, 4, 128).transpose(2, 0, 1).reshape(128, 16)
        cst[:, b + 40:b + 44] = np.asarray(inp["conv_b"], f32)[L[li]].reshape(4, 128).T
        cst[:, b + 44:b + 48] = np.asarray(inp["b_rg_a"], f32)[L[li]].reshape(4, 128).T
        cst[:, b + 48:b + 52] = np.asarray(inp["b_rg_x"], f32)[L[li]].reshape(4, 128).T
        cst[:, b + 52:b + 56] = np.asarray(inp["rg_lambda"], f32)[L[li]].reshape(4, 128).T
        cst[:, b + 56:b + 64] = np.asarray(inp["b_f"], f32)[L[li]][None, :]
    cst[:, NL * PL:NL * PL + 8] = np.asarray(inp["final_g"], f32).reshape(8, 128).T
    cmat = np.zeros((128, 384), f32)
    cmat[:, 0:128] = np.eye(128, dtype=f32)
    kk = np.arange(128)[:, None]
    qq = np.arange(128)[None, :]
    cmat[:, 128:256] = np.where(kk > qq, MASKVAL, 0.0)
    cmat[:, 256:384] = (kk <= qq).astype(f32)
    return dict(cst=cst, cmat=cmat, wi=wi, wo=wo, wq=wq, wv=wv, wf=wf, wor=wor, woa=woa, wbd=wbd)


FUSED = True
_PROGS = {}


def _prog(NL, final):
    key = (NL, final)
    if key not in _PROGS:
        _PROGS[key] = build_program(NL, final)
    return _PROGS[key]


def kernel(**inputs):
    x = np.asarray(inputs["x"], np.float32)
    xT = [np.ascontiguousarray(x[b].T) for b in range(NB)]
    if FUSED:
        groups = [list(range(DEPTH))]
    else:
        groups = [[l] for l in range(DEPTH)]
    for gi, layers in enumerate(groups):
        final = gi == len(groups) - 1
        w = _prep_weights(inputs, layers)
        nc = _prog(len(layers), final)
        in_maps = [dict(w, xT=xT[b]) for b in range(NB)]
        res = run_bass_kernel_spmd(nc, in_maps, core_ids=list(range(NB)))
        xT = [np.asarray(res.results[b]["oT"], np.float32) for b in range(NB)]
    out = np.stack([xT[b].T for b in range(NB)], axis=0)
    return np.ascontiguousarray(out.astype(np.float32))
```

```python
import contextlib
import math
import numpy as np
import concourse.bass as bass
import concourse.mybir as mybir
from concourse.bass_utils import run_bass_kernel_spmd

F32 = mybir.dt.float32
BF16 = mybir.dt.bfloat16
AF = mybir.ActivationFunctionType
ALU = mybir.AluOpType

D = 1024
S = 4096
NB = 8
DEPTH = 4
DFF = 2816
NJ = DFF // 128
TB = 512
NTB = S // TB
EPS = 1e-6
PL = 64
GELU_K = 0.044715
GELU_S = 2.0 * math.sqrt(2.0 / math.pi)
MASKVAL = -30000.0


class Ctx:
    COMPUTE = ("pe", "act", "dve", "pool")

    def __init__(self, nc, stack):
        self.nc = nc
        self.stack = stack
        self.eng = {"pe": nc.tensor, "act": nc.scalar, "dve": nc.vector, "pool": nc.gpsimd, "sp": nc.sync}
        self.sems = {}
        self.cnt = {}
        for e in self.COMPUTE:
            self.sems[e] = stack.enter_context(nc.semaphore("s_" + e))
            self.cnt[e] = 0
        self.waited = {e: {} for e in self.eng}
        self.tw = {}
        self.tr = {}
        self.n_wait = 0
        self.n_ops = {e: 0 for e in self.eng}

    def _deps(self, reads, writes):
        deps = {}
        for k in reads:
            for s, v in self.tw.get(k, {}).items():
                if deps.get(s, 0) < v:
                    deps[s] = v
        for k in writes:
            for s, v in self.tw.get(k, {}).items():
                if deps.get(s, 0) < v:
                    deps[s] = v
            for s, v in self.tr.get(k, {}).items():
                if deps.get(s, 0) < v:
                    deps[s] = v
        return deps

    def _emit_waits(self, ename, deps):
        E = self.eng[ename]
        wd = self.waited[ename]
        for s, v in deps.items():
            if s == "pe" and ename == "pe":
                continue
            if wd.get(s, 0) >= v:
                continue
            if s == "pe" and v > self.cnt["pe"]:
                raise RuntimeError("wait on an un-signalled PE op")
            E.wait_ge(self.sems[s], v)
            wd[s] = v
            self.n_wait += 1

    def _record(self, semkey, val, reads, writes):
        for k in reads:
            d = self.tr.setdefault(k, {})
            if d.get(semkey, 0) < val:
                d[semkey] = val
        for k in writes:
            d = self.tw.setdefault(k, {})
            if d.get(semkey, 0) < val:
                d[semkey] = val

    def op(self, ename, fn, reads=(), writes=(), signal=True):
        reads = list(reads) + ["*"]
        self._emit_waits(ename, self._deps(reads, writes))
        inst = fn(self.eng[ename])
        self.n_ops[ename] += 1
        if signal:
            self.cnt[ename] += 1
            inst.then_inc(self.sems[ename], 1)
            val = self.cnt[ename]
        else:
            assert ename == "pe"
            val = self.cnt[ename] + 1
        self._record(ename, val, reads, writes)
        return inst

    def dma(self, qname, out, in_, reads, writes, semkey, **kw):
        sk = ("dma", semkey)
        if sk not in self.sems:
            self.sems[sk] = self.stack.enter_context(self.nc.semaphore("d%d" % len(self.sems)))
            self.cnt[sk] = 0
        reads = list(reads) + ["*"]
        self._emit_waits(qname, self._deps(reads, writes))
        inst = self.eng[qname].dma_start(out=out, in_=in_, **kw)
        self.n_ops[qname] += 1
        self.cnt[sk] += 16
        inst.then_inc(self.sems[sk], 16)
        self._record(sk, self.cnt[sk], reads, writes)
        return inst

    def wait_all(self, ename, keys):
        deps = {}
        for k in keys:
            for d in (self.tw.get(k, {}), self.tr.get(k, {})):
                for s, v in d.items():
                    if deps.get(s, 0) < v:
                        deps[s] = v
        if ename == "pe":
            deps.pop("pe", None)
        self._emit_waits(ename, deps)

    def barrier(self):
        for e in ("pe", "act", "dve", "pool", "sp"):
            self.wait_all(e, ["*"])

    def mm(self, out, lhsT, rhs, start, stop, reads, writes, signal=False):
        return self.op("pe", lambda e: e.matmul(out, lhsT=lhsT, rhs=rhs, start=start, stop=stop),
                       reads=reads, writes=writes, signal=signal)

    def act(self, out, in_, func, reads, writes, **kw):
        return self.op("act", lambda e: e.activation(out=out, in_=in_, func=func, **kw), reads=reads, writes=writes)


class WStream:
    def __init__(self, c, name, buf, nslots, items):
        self.c, self.name, self.buf, self.nslots, self.items = c, name, buf, nslots, items
        self.issued = 0
        self.consumed = 0
        for _ in range(nslots):
            self._issue()

    def _issue(self):
        if self.issued < len(self.items):
            src, keys = self.items[self.issued]
            s = self.issued % self.nslots
            self.c.dma("sp", self.buf[:, s, :], src, reads=keys, writes=[(self.name, s)], semkey=(self.name, s))
            self.issued += 1

    def next(self):
        s = self.consumed % self.nslots
        self.consumed += 1
        return s, (self.name, s)

    def release(self, n=1):
        for _ in range(n):
            self._issue()


def build_program(NL, final):
    nc = bass.Bass("TRN2", target_bir_lowering=False)
    NC = NL * PL + 8

    def din(name, shape, dt=F32):
        return nc.dram_tensor(name, shape, dt, kind="ExternalInput").ap()

    def dint(name, shape, dt=BF16):
        return nc.dram_tensor(name, shape, dt, kind="Internal").ap()

    xT = din("xT", [D, S])
    cst_d = din("cst", [128, NC])
    cmat_d = din("cmat", [128, 384])
    wi_d = din("wi", [NL, 2, NJ, 128, 2048])
    wo_d = din("wo", [NL, 2, 8, 128, 2816])
    wq_d = din("wq", [NL, 8, 128, 2048])
    wv_d = din("wv", [NL, 128, 4096])
    wf_d = din("wf", [NL, 128, 64])
    wor_d = din("wor", [NL, 128, 4096])
    woa_d = din("woa", [NL, 128, 4096])
    wbd_d = din("wbd", [NL, 128, 1024])
    oT = nc.dram_tensor("oT", [D, S], F32, kind="ExternalOutput").ap()

    wi_b = dint("wi_b", [NL, 2, NJ, 128, 2048])
    wo_b = dint("wo_b", [NL, 2, 8, 128, 2816])
    wq_b = dint("wq_b", [NL, 8, 128, 2048])
    wv_b = dint("wv_b", [NL, 128, 4096])
    wf_b = dint("wf_b", [NL, 128, 64])
    wor_b = dint("wor_b", [NL, 128, 4096])
    woa_b = dint("woa_b", [NL, 128, 4096])
    wbd_b = dint("wbd_b", [NL, 128, 1024])
    qT_s = dint("qT_s", [512, S])
    kT_s = dint("kT_s", [512, S])
    v_s = dint("v_s", [4, 128, 32, 2, 64])

    with contextlib.ExitStack() as st:
        c = Ctx(nc, st)

        uid = [0]

        def sb(stack, name, shape, dt):
            uid[0] += 1
            return stack.enter_context(nc.sbuf_tensor("%s_%d" % (name, uid[0]), shape, dt))

        X = sb(st, "X", [128, 8, S], F32)
        cst = sb(st, "cst_sb", [128, NC], F32)
        tri_sb = sb(st, "tri_sb", [128, 128], F32)
        ident_bf = sb(st, "ident_bf", [128, 128], BF16)
        mask_bf = sb(st, "mask_bf", [128, 128], BF16)
        ones_bf = sb(st, "ones_bf", [128, 128], BF16)
        ones_f = sb(st, "ones_f", [128, 128], F32)
        spc = sb(st, "spc", [128, NL * 8], F32)
        clog = sb(st, "clog", [128, 32, 8], F32)
        cref = sb(st, "cref", [128, NTB, 8], F32)
        PS = [st.enter_context(nc.psum_tensor("ps%d" % b, [128, 512], F32)) for b in range(8)]

        def psk(b):
            return ("ps", b)

        def xk(fc, tb):
            return ("X", fc, tb)

        def blk(tb):
            return slice(tb * TB, (tb + 1) * TB)

        def cast(dst, src, key, tag, maxrows=1024):
            rows = dst.shape[0]
            r0 = 0
            i = 0
            while r0 < rows:
                r1 = min(rows, r0 + maxrows)
                c.dma("pool", dst[r0:r1, :], src[r0:r1, :], reads=[], writes=[key], semkey=("cast", tag, i))
                r0 = r1
                i += 1

        def cast_ffn(l, i):
            for half in range(2):
                js = slice(half * 11, half * 11 + 11)
                cast(wi_b[l, i, js].rearrange("j p e -> (j p) e"), wi_d[l, i, js].rearrange("j p e -> (j p) e"),
                     ("wi_b", l, i, half), ("wi", i, half), maxrows=704)
            for half in range(2):
                fs = slice(half * 4, half * 4 + 4)
                cast(wo_b[l, i, fs].rearrange("f p (a e) -> (f p a) e", a=2),
                     wo_d[l, i, fs].rearrange("f p (a e) -> (f p a) e", a=2),
                     ("wo_b", l, i, half), ("wo", i, half), maxrows=1024)

        def cast_mix(l):
            k = ("mixw", l)
            cast(wq_b[l].rearrange("t p e -> (t p) e"), wq_d[l].rearrange("t p e -> (t p) e"), k, ("wq",), maxrows=512)
            cast(wv_b[l].rearrange("p (a e) -> (p a) e", a=2), wv_d[l].rearrange("p (a e) -> (p a) e", a=2), k, ("wv",))
            cast(wf_b[l], wf_d[l], k, ("wf",))
            cast(wbd_b[l], wbd_d[l], k, ("wbd",))
            cast(wor_b[l].rearrange("p (a e) -> (p a) e", a=2), wor_d[l].rearrange("p (a e) -> (p a) e", a=2), k, ("wor",))
            cast(woa_b[l].rearrange("p (a e) -> (p a) e", a=2), woa_d[l].rearrange("p (a e) -> (p a) e", a=2), k, ("woa",))

        def cast_layer(l):
            cast_ffn(l, 0)
            cast_mix(l)
            cast_ffn(l, 1)

        cast_layer(0)
        c.dma("sp", cst[:, :], cst_d[:, :], reads=[], writes=["cst"], semkey="cst")
        c.dma("sp", tri_sb[:, :], cmat_d[:, 256:384], reads=[], writes=["cmat"], semkey="cmat")
        for fc in range(8):
            c.dma("sp", X[:, fc, :], xT[fc * 128:(fc + 1) * 128, :], reads=[],
                  writes=[xk(fc, tb) for tb in range(NTB)], semkey=("Xld", fc))
        c.op("dve", lambda e: e.memset(ones_bf[:, :], 1.0), writes=["ones_bf"])
        c.op("dve", lambda e: e.memset(ones_f[:, :], 1.0), writes=["ones_f"])
        with contextlib.ExitStack() as ph0:
            cm_tmp = sb(ph0, "cm_tmp", [128, 256], F32)
            c.dma("sp", cm_tmp[:, :], cmat_d[:, 0:256], reads=[], writes=["cm_tmp"], semkey="cm_tmp")
            c.act(ident_bf[:, :], cm_tmp[:, 0:128], AF.Copy, reads=["cm_tmp"], writes=["ident_bf"])
            c.act(mask_bf[:, :], cm_tmp[:, 128:256], AF.Copy, reads=["cm_tmp"], writes=["mask_bf"])
        tri_f = tri_sb[:, :]
        for l in range(NL):
            lam = cst[:, l * PL + 52:l * PL + 56]
            c.act(spc[:, l * 8:l * 8 + 4], lam, AF.Sigmoid, reads=["cst"], writes=["spc"])
            c.act(spc[:, l * 8:l * 8 + 4], spc[:, l * 8:l * 8 + 4], AF.Ln, reads=["spc"], writes=["spc"])
            c.op("dve", lambda e: e.tensor_scalar(out=spc[:, l * 8 + 4:l * 8 + 8], in0=spc[:, l * 8:l * 8 + 4],
                                                  scalar1=16.0, scalar2=None, op0=ALU.mult),
                 reads=["spc"], writes=["spc"])
            c.op("dve", lambda e: e.tensor_scalar(out=spc[:, l * 8:l * 8 + 4], in0=spc[:, l * 8:l * 8 + 4],
                                                  scalar1=8.0, scalar2=None, op0=ALU.mult),
                 reads=["spc"], writes=["spc"])

        def rmsnorm_stats(tb, sq, rs_tmp, rstd, ps_stat):
            for fc in range(8):
                b = fc % 2
                c.act(sq[:, b, :], X[:, fc, blk(tb)], AF.Square, reads=[xk(fc, tb)], writes=[("sq", b)])
                c.mm(PS[ps_stat][:, :], ones_bf[:, :], sq[:, b, :], start=(fc == 0), stop=(fc == 7),
                     reads=[("sq", b), "ones_bf"], writes=[psk(ps_stat)], signal=True)
            c.act(rs_tmp[:, :], PS[ps_stat][:, :], AF.Sqrt, reads=[psk(ps_stat)], writes=["rs_tmp"],
                  scale=1.0 / D, bias=EPS)
            c.op("dve", lambda e: e.reciprocal(out=rstd[:, :], in_=rs_tmp[:, :]), reads=["rs_tmp"], writes=["rstd"])

        def rmsnorm_apply(tb, gbase, rstd, xn):
            for fc in range(8):
                c.op("dve", lambda e: e.scalar_tensor_tensor(out=xn[:, fc, :], in0=X[:, fc, blk(tb)],
                                                             scalar=cst[:, gbase + fc:gbase + fc + 1], in1=rstd[:, :],
                                                             op0=ALU.mult, op1=ALU.mult),
                     reads=[xk(fc, tb), "rstd", "cst"], writes=[("xn", fc)])

        def ffn_phase(l, i, gbase):
            if not (l == 0 and i == 0):
                c.barrier()
            with contextlib.ExitStack() as ph:
                xn = sb(ph, "f_xn", [128, 8, TB], BF16)
                sq = sb(ph, "f_sq", [128, 2, TB], BF16)
                rs_tmp = sb(ph, "f_rs", [128, TB], F32)
                rstd = sb(ph, "f_rstd", [128, TB], F32)
                h = sb(ph, "f_h", [128, NJ, TB], BF16)
                sg = sb(ph, "f_sg", [128, 2, TB], F32)
                wib = sb(ph, "f_wi", [128, 4, 2048], BF16)
                wob = sb(ph, "f_wo", [128, 3, 2816], BF16)
                wi_items = [(wi_b[l, i, j], [("wi_b", l, i, j // 11)]) for _ in range(NTB) for j in range(NJ)]
                wo_items = [(wo_b[l, i, fo], [("wo_b", l, i, fo // 4)]) for _ in range(NTB) for fo in range(8)]
                wis = WStream(c, "wi_s", wib, 4, wi_items)
                wos = WStream(c, "wo_s", wob, 3, wo_items)
                for tb in range(NTB):
                    rmsnorm_stats(tb, sq, rs_tmp, rstd, 6)
                    rmsnorm_apply(tb, gbase, rstd, xn)
                    for j in range(NJ):
                        s, skey = wis.next()
                        wt = wib[:, s, :].rearrange("p (k g m) -> p k g m", k=8, g=2)
                        pg, pu = j % 2, 2 + j % 2
                        for gu, pb in ((0, pg), (1, pu)):
                            for kc in range(8):
                                c.mm(PS[pb][:, :], wt[:, kc, gu, :], xn[:, kc, :], start=(kc == 0), stop=(kc == 7),
                                     reads=[skey, ("xn", kc)], writes=[psk(pb)], signal=(kc == 7))
                        wis.release()
                        c.act(sg[:, j % 2, :], PS[pg][:, :], AF.Silu, reads=[psk(pg)], writes=[("sg", j % 2)])
                        c.op("dve", lambda e: e.tensor_tensor(out=h[:, j, :], in0=PS[pu][:, :], in1=sg[:, j % 2, :],
                                                              op=ALU.mult),
                             reads=[psk(pu), ("sg", j % 2)], writes=[("h", j)])
                    for fo in range(8):
                        s, skey = wos.next()
                        wt = wob[:, s, :].rearrange("p (j m) -> p j m", j=NJ)
                        pb = 4 + fo % 2
                        for j in range(NJ):
                            c.mm(PS[pb][:, :], wt[:, j, :], h[:, j, :], start=(j == 0), stop=(j == NJ - 1),
                                 reads=[skey, ("h", j)], writes=[psk(pb)], signal=(j == NJ - 1))
                        wos.release()
                        c.op("dve", lambda e: e.scalar_tensor_tensor(out=X[:, fo, blk(tb)], in0=PS[pb][:, :], scalar=0.5,
                                                                     in1=X[:, fo, blk(tb)], op0=ALU.mult, op1=ALU.add),
                             reads=[psk(pb), xk(fo, tb)], writes=[xk(fo, tb)])
                c.barrier()

        def m1_phase(l):
            c.barrier()
            cb = l * PL
            with contextlib.ExitStack() as ph:
                xn = sb(ph, "m_xn", [128, 8, TB], BF16)
                sq = sb(ph, "m_sq", [128, 1, TB], BF16)
                rstd = sb(ph, "m_rstd", [128, TB], F32)
                wsb = sb(ph, "m_ws", [128, 3, 2048], BF16)
                wf_sb = sb(ph, "m_wf", [128, 64], BF16)
                wbd_sb = sb(ph, "m_wbd", [128, 1024], BF16)
                stq = sb(ph, "m_stq", [128, 4, TB], BF16)
                stk = sb(ph, "m_stk", [128, 4, TB], BF16)
                stv = sb(ph, "m_stv", [128, 4, 8, 64], BF16)
                xr_sb = sb(ph, "m_xr", [128, 4, TB + 3], F32)
                hcar = sb(ph, "m_hcar", [128, 4], F32)
                carry = sb(ph, "m_carry", [128, 8], F32)
                fb = sb(ph, "m_fb", [128, 32], F32)
                bfb = sb(ph, "m_bfb", [128, 32], F32)
                spx = sb(ph, "m_spx", [128, 16], F32)
                acc = sb(ph, "m_acc", [128, 2, TB], F32)
                xcb = sb(ph, "m_xcb", [128, 2, TB], BF16)
                tr = sb(ph, "m_tr", [128, 2, TB], F32)
                ti = sb(ph, "m_ti", [128, 2, TB], F32)
                ta = sb(ph, "m_ta", [128, 2, TB], F32)
                tth = sb(ph, "m_tth", [128, 2, TB], F32)
                tx = sb(ph, "m_tx", [128, 2, TB], F32)
                yrec = sb(ph, "m_yrec", [128, 4, TB], BF16)
                mk = ("mixw", l)
                c.dma("sp", wf_sb[:, :], wf_b[l], reads=[mk], writes=["wf_sb"], semkey="wf_sb")
                c.dma("sp", wbd_sb[:, :], wbd_b[l], reads=[mk], writes=["wbd_sb"], semkey="wbd_sb")
                items = []
                for _ in range(NTB):
                    for t in (4, 5, 0, 1, 2, 3, 6, 7):
                        items.append((wq_b[l, t], [mk]))
                    items.append((wv_b[l, :, 0:2048], [mk]))
                    items.append((wv_b[l, :, 2048:4096], [mk]))
                    items.append((wor_b[l, :, 0:2048], [mk]))
                    items.append((wor_b[l, :, 2048:4096], [mk]))
                ws = WStream(c, "m_ws", wsb, 3, items)
                c.op("dve", lambda e: e.memset(xr_sb[:, :, :], 0.0), writes=[("xr", k) for k in range(4)])
                c.op("dve", lambda e: e.memset(hcar[:, :], 0.0), writes=["hcar"])
                c.op("dve", lambda e: e.memset(carry[:, :], 0.0), writes=["carry"])
                for tt in range(4):
                    c.op("dve", lambda e: e.tensor_copy(out=bfb[:, tt * 8:(tt + 1) * 8], in_=cst[:, cb + 56:cb + 64]),
                         reads=["cst"], writes=["bfb"])
                c.op("dve", lambda e: e.tensor_scalar(out=spx[:, 0:4], in0=spc[:, l * 8:l * 8 + 4], scalar1=0.5,
                                                      scalar2=None, op0=ALU.mult), reads=["spc"], writes=["spx"])
                c.op("dve", lambda e: e.tensor_copy(out=spx[:, 4:8], in_=spc[:, l * 8:l * 8 + 4]),
                     reads=["spc"], writes=["spx"])
                c.op("dve", lambda e: e.tensor_scalar(out=spx[:, 8:12], in0=cst[:, cb + 44:cb + 48], scalar1=0.5,
                                                      scalar2=None, op0=ALU.mult), reads=["cst"], writes=["spx"])
                c.op("dve", lambda e: e.tensor_scalar(out=spx[:, 12:16], in0=cst[:, cb + 48:cb + 52], scalar1=0.5,
                                                      scalar2=None, op0=ALU.mult), reads=["cst"], writes=["spx"])
                wbd_v = wbd_sb[:, :].rearrange("p (c g m) -> p c g m", c=4, g=2)
                wf_v = wf_sb[:, :].rearrange("p (k h) -> p k h", k=8)
                rot = [0]

                def nextbank():
                    b = rot[0] % 4
                    rot[0] += 1
                    return b

                def col(idx):
                    return cst[:, idx:idx + 1]

                def sx(j):
                    return spx[:, j:j + 1]

                def stats(tb):
                    for fc in range(8):
                        b = 0
                        c.act(sq[:, b, :], X[:, fc, blk(tb)], AF.Square, reads=[xk(fc, tb)], writes=[("sq", b)])
                        c.mm(PS[6][:, :], ones_bf[:, :], sq[:, b, :], start=(fc == 0), stop=(fc == 7),
                             reads=[("sq", b), "ones_bf"], writes=[psk(6)], signal=True)

                def stats_fin(tb):
                    c.act(rstd[:, :], PS[6][:, :], AF.Sqrt, reads=[psk(6)], writes=["rstd"], scale=1.0 / D, bias=EPS)
                    c.op("dve", lambda e: e.reciprocal(out=rstd[:, :], in_=rstd[:, :]), reads=["rstd"], writes=["rstd"])

                def inproj_chunk(wt, skey, cc, evac):
                    pb = nextbank()
                    for kc in range(8):
                        c.mm(PS[pb][:, :], wt[:, kc, cc, :], xn[:, kc, :], start=(kc == 0), stop=(kc == 7),
                             reads=[skey, ("xn", kc)], writes=[psk(pb)], signal=(kc == 7))
                    evac(pb)

                def inproj_tile(evacs):
                    s, skey = ws.next()
                    wt = wsb[:, s, :].rearrange("p (k g m) -> p k g m", k=8, g=2)
                    for cc in range(2):
                        inproj_chunk(wt, skey, cc, evacs[cc])
                    ws.release()

                def conv(k):
                    kk = k % 2
                    xk_ = ("xr", k)
                    c.op("dve", lambda e: e.tensor_scalar(out=acc[:, kk, :], in0=xr_sb[:, k, 0:TB],
                                                          scalar1=col(cb + 24 + k), scalar2=col(cb + 40 + k),
                                                          op0=ALU.mult, op1=ALU.add),
                         reads=[xk_, "cst"], writes=[("acc", kk)])
                    for tap in range(1, 4):
                        c.op("dve", lambda e: e.scalar_tensor_tensor(out=acc[:, kk, :], in0=xr_sb[:, k, tap:tap + TB],
                                                                     scalar=col(cb + 24 + tap * 4 + k), in1=acc[:, kk, :],
                                                                     op0=ALU.mult, op1=ALU.add),
                             reads=[xk_, "cst", ("acc", kk)], writes=[("acc", kk)])
                    c.act(xr_sb[:, k, 0:3], xr_sb[:, k, TB:TB + 3], AF.Copy, reads=[xk_], writes=[xk_])
                    c.act(xcb[:, kk, :], acc[:, kk, :], AF.Copy, reads=[("acc", kk)], writes=[("xcb", kk)])

                def gates(k):
                    kk = k % 2
                    pa = nextbank()
                    c.mm(PS[pa][:, :], wbd_v[:, k, 0, :], xcb[:, kk, :], start=True, stop=True,
                         reads=["wbd_sb", ("xcb", kk)], writes=[psk(pa)], signal=True)
                    px = nextbank()
                    c.mm(PS[px][:, :], wbd_v[:, k, 1, :], xcb[:, kk, :], start=True, stop=True,
                         reads=["wbd_sb", ("xcb", kk)], writes=[psk(px)], signal=True)
                    c.act(tr[:, kk, :], PS[pa][:, :], AF.Tanh, reads=[psk(pa), "spx"], writes=[("tr", kk)],
                          scale=0.5, bias=sx(8 + k))
                    c.act(ti[:, kk, :], PS[px][:, :], AF.Tanh, reads=[psk(px), "spx"], writes=[("ti", kk)],
                          scale=0.5, bias=sx(12 + k))

                def gate_funcs(k):
                    kk = k % 2
                    c.act(ta[:, kk, :], tr[:, kk, :], AF.Exp, reads=[("tr", kk), "spx"], writes=[("ta", kk)],
                          scale=sx(k), bias=sx(k))
                    c.act(tth[:, kk, :], tr[:, kk, :], AF.Tanh, reads=[("tr", kk), "spx"], writes=[("tth", kk)],
                          scale=sx(k), bias=sx(k))
                    c.act(tr[:, kk, :], tr[:, kk, :], AF.Exp, reads=[("tr", kk), "spx"], writes=[("tr", kk)],
                          scale=sx(4 + k), bias=sx(4 + k))
                    c.op("dve", lambda e: e.scalar_tensor_tensor(out=tth[:, kk, :], in0=tr[:, kk, :], scalar=1.0,
                                                                 in1=tth[:, kk, :], op0=ALU.add, op1=ALU.mult),
                         reads=[("tr", kk), ("tth", kk)], writes=[("tth", kk)])
                    c.op("dve", lambda e: e.scalar_tensor_tensor(out=ti[:, kk, :], in0=ti[:, kk, :], scalar=1.0,
                                                                 in1=acc[:, kk, :], op0=ALU.add, op1=ALU.mult),
                         reads=[("ti", kk), ("acc", kk)], writes=[("ti", kk)])

                def gelu_pre(k, pb):
                    kk = k % 2
                    c.act(tx[:, kk, :], PS[pb][:, :], AF.Square, reads=[psk(pb)], writes=[("tx", kk)],
                          scale=math.sqrt(GELU_K))
                    c.op("dve", lambda e: e.scalar_tensor_tensor(out=tx[:, kk, :], in0=tx[:, kk, :], scalar=1.0,
                                                                 in1=PS[pb][:, :], op0=ALU.add, op1=ALU.mult),
                         reads=[("tx", kk), psk(pb)], writes=[("tx", kk)])
                    c.act(tx[:, kk, :], tx[:, kk, :], AF.Tanh, reads=[("tx", kk)], writes=[("tx", kk)],
                          scale=0.5 * GELU_S)
                    c.op("dve", lambda e: e.scalar_tensor_tensor(out=tx[:, kk, :], in0=tx[:, kk, :], scalar=1.0,
                                                                 in1=PS[pb][:, :], op0=ALU.add, op1=ALU.mult),
                         reads=[("tx", kk), psk(pb)], writes=[("tx", kk)])

                def rec_tail_sqrt(k):
                    kk = k % 2
                    c.act(tth[:, kk, :], tth[:, kk, :], AF.Sqrt, reads=[("tth", kk)], writes=[("tth", kk)],
                          scale=-1.0 / 16.0)

                def rec_tail(k):
                    kk = k % 2
                    c.op("dve", lambda e: e.tensor_tensor(out=ti[:, kk, :], in0=ti[:, kk, :], in1=tth[:, kk, :],
                                                          op=ALU.mult),
                         reads=[("ti", kk), ("tth", kk)], writes=[("ti", kk)])
                    c.op("dve", lambda e: e.tensor_tensor_scan(out=tr[:, kk, :], data0=ta[:, kk, :], data1=ti[:, kk, :],
                                                               initial=hcar[:, k:k + 1], op0=ALU.mult, op1=ALU.add),
                         reads=[("ta", kk), ("ti", kk), "hcar", ("tr", kk)], writes=[("tr", kk)])
                    c.act(hcar[:, k:k + 1], tr[:, kk, TB - 1:TB], AF.Copy, reads=[("tr", kk)], writes=["hcar"])
                    c.op("dve", lambda e: e.tensor_tensor(out=yrec[:, k, :], in0=tx[:, kk, :], in1=tr[:, kk, :],
                                                          op=ALU.mult),
                         reads=[("tx", kk), ("tr", kk)], writes=[("yrec", k)])

                def ev_q(ch):
                    return lambda pb: c.act(stq[:, ch, :], PS[pb][:, :], AF.Copy, reads=[psk(pb)], writes=["stq"])

                def ev_k(ch):
                    return lambda pb: c.act(stk[:, ch, :], PS[pb][:, :], AF.Copy, reads=[psk(pb)], writes=["stk"])

                def ev_xr(k):
                    return lambda pb: c.act(xr_sb[:, k, 3:TB + 3], PS[pb][:, :], AF.Copy, reads=[psk(pb)],
                                            writes=[("xr", k)])

                def ev_gr(k):
                    return lambda pb: gelu_pre(k, pb)

                stats(0)
                stats_fin(0)
                rmsnorm_apply(0, cb + 8, rstd, xn)
                for tb in range(NTB):
                    inproj_tile([ev_xr(0), ev_xr(1)])
                    conv(0)
                    conv(1)
                    inproj_tile([ev_xr(2), ev_xr(3)])
                    inproj_tile([ev_q(0), ev_q(1)])
                    inproj_tile([ev_q(2), ev_q(3)])
                    if tb + 1 < NTB:
                        stats(tb + 1)
                    gates(0)
                    gates(1)
                    gate_funcs(0)
                    gate_funcs(1)
                    conv(2)
                    conv(3)
                    inproj_tile([ev_k(0), ev_k(1)])
                    inproj_tile([ev_k(2), ev_k(3)])
                    inproj_tile([ev_gr(0), ev_gr(1)])
                    rec_tail_sqrt(0)
                    rec_tail_sqrt(1)
                    rec_tail(0)
                    rec_tail(1)
                    gates(2)
                    gates(3)
                    gate_funcs(2)
                    gate_funcs(3)
                    inproj_tile([ev_gr(2), ev_gr(3)])
                    sA, kA = ws.next()
                    sB, kB = ws.next()
                    for tt in range(4):
                        pb = 4 + tt % 2
                        tok = slice(tt * 128, (tt + 1) * 128)
                        for kc in range(8):
                            sl, kk_ = (sA, kA) if kc < 4 else (sB, kB)
                            wv_t = wsb[:, sl, :].rearrange("p (k n) -> p k n", k=4)
                            c.mm(PS[pb][:, :], xn[:, kc, tok], wv_t[:, kc % 4, :], start=(kc == 0), stop=(kc == 7),
                                 reads=[kk_, ("xn", kc)], writes=[psk(pb)], signal=(kc == 7))
                        c.act(stv[:, tt, :, :], PS[pb][:, :].rearrange("p (h d) -> p h d", h=8), AF.Copy,
                              reads=[psk(pb)], writes=["stv"])
                        for kc in range(8):
                            c.mm(PS[7][:, tt * 8:(tt + 1) * 8], xn[:, kc, tok], wf_v[:, kc, :], start=(kc == 0),
                                 stop=(kc == 7), reads=["wf_sb", ("xn", kc)], writes=[psk(7)], signal=(kc == 7))
                    ws.release(2)
                    rec_tail_sqrt(2)
                    rec_tail_sqrt(3)
                    if tb + 1 < NTB:
                        stats_fin(tb + 1)
                        rmsnorm_apply(tb + 1, cb + 8, rstd, xn)
                    rec_tail(2)
                    rec_tail(3)
                    sA, kA = ws.next()
                    sB, kB = ws.next()
                    for fo in range(8):
                        pb = 4 + fo % 2
                        for kc in range(4):
                            sl, kk_ = (sA, kA) if kc < 2 else (sB, kB)
                            wor_t = wsb[:, sl, :].rearrange("p (k f) -> p k f", k=2)
                            c.mm(PS[pb][:, :], wor_t[:, kc % 2, fo * 128:(fo + 1) * 128], yrec[:, kc, :],
                                 start=(kc == 0), stop=(kc == 3), reads=[kk_, ("yrec", kc)], writes=[psk(pb)],
                                 signal=(kc == 3))
                        c.op("dve", lambda e: e.tensor_tensor(out=X[:, fo, blk(tb)], in0=PS[pb][:, :],
                                                              in1=X[:, fo, blk(tb)], op=ALU.add),
                             reads=[psk(pb), xk(fo, tb)], writes=[xk(fo, tb)])
                    ws.release(2)
                    c.op("dve", lambda e: e.tensor_tensor(out=fb[:, :], in0=PS[7][:, 0:32], in1=bfb[:, :], op=ALU.add),
                         reads=[psk(7), "bfb"], writes=["fb"])
                    c.act(fb[:, :], fb[:, :], AF.Tanh, reads=["fb"], writes=["fb"], scale=0.5)
                    c.act(fb[:, :], fb[:, :], AF.Ln, reads=["fb"], writes=["fb"], scale=0.5, bias=0.5)
                    c.mm(PS[7][:, 32:64], tri_f, fb[:, :], start=True, stop=True, reads=["cmat", "fb"],
                         writes=[psk(7)], signal=True)
                    c.mm(PS[7][:, 64:96], ones_f[:, :], fb[:, :], start=True, stop=True, reads=["ones_f", "fb"],
                         writes=[psk(7)], signal=True)
                    for tt in range(4):
                        n = tb * 4 + tt
                        c.op("dve", lambda e: e.tensor_tensor(out=clog[:, n, :], in0=PS[7][:, 32 + tt * 8:40 + tt * 8],
                                                              in1=carry[:, :], op=ALU.add),
                             reads=[psk(7), "carry"], writes=["clog"])
                        c.op("dve", lambda e: e.tensor_tensor(out=carry[:, :], in0=PS[7][:, 64 + tt * 8:72 + tt * 8],
                                                              in1=carry[:, :], op=ALU.add),
                             reads=[psk(7), "carry"], writes=["carry"])
                        if tt == 1:
                            c.op("dve", lambda e: e.tensor_copy(out=cref[:, tb, :], in_=carry[:, :]),
                                 reads=["carry"], writes=["cref"])
                    c.dma("sp", qT_s.rearrange("(c p) t -> p c t", p=128)[:, :, blk(tb)], stq[:, :, :],
                          reads=["stq"], writes=[("qT_s", p) for p in range(4)], semkey="stq")
                    c.dma("sp", kT_s.rearrange("(c p) t -> p c t", p=128)[:, :, blk(tb)], stk[:, :, :],
                          reads=["stk"], writes=[("kT_s", p) for p in range(4)], semkey="stk")
                    for p in range(4):
                        c.dma("sp", v_s[p, :, tb * 4:(tb + 1) * 4, :, :], stv[:, :, 2 * p:2 * p + 2, :],
                              reads=["stv"], writes=[("v_s", p)], semkey="stv")
                c.barrier()

        def m2_phase(l):
            c.barrier()
            if l + 1 < NL:
                cast_layer(l + 1)
            mk = ("mixw", l)
            with contextlib.ExitStack() as ph:
                kT_sb = sb(ph, "a_kT", [128, 2, S], BF16)
                v_sb = sb(ph, "a_v", [128, 2, 32, 2, 128], BF16)
                q_sb = sb(ph, "a_q", [128, 2, 2, TB], BF16)
                P_sb = sb(ph, "a_P", [128, 4, TB], BF16)
                bias_sb = sb(ph, "a_bias", [128, 2, 2, 32], F32)
                r_sb = sb(ph, "a_r", [128, TB], F32)
                rs_sb = sb(ph, "a_rs", [128, TB], F32)
                y_sb = sb(ph, "a_y", [128, 2, TB], BF16)
                woa_sb = sb(ph, "a_woa", [128, 4096], BF16)
                woa_v = woa_sb[:, :].rearrange("p (h f) -> p h f", h=4)
                c.dma("sp", woa_sb[:, :], woa_b[l], reads=[mk], writes=["woa_sb"], semkey="woa_sb")
                c.op("dve", lambda e: e.memset(v_sb[:, :, :, :, :], 1.0), writes=[("v_sb", 0), ("v_sb", 1)])
                c.op("dve", lambda e: e.memset(q_sb[:, :, :, :], 0.0), writes=[("q_sb", 0), ("q_sb", 1)])

                def load_pair(hp):
                    b = hp % 2
                    c.dma("sp", kT_sb[:, b, :], kT_s[hp * 128:(hp + 1) * 128, :], reads=[("kT_s", hp)],
                          writes=[("kT_sb", b)], semkey=("kT_sb", b))
                    c.dma("sp", v_sb[:, b, :, 0, 0:64], v_s[hp, :, :, 0, :], reads=[("v_s", hp)],
                          writes=[("v_sb", b)], semkey=("v_sb", b))
                    c.dma("sp", v_sb[:, b, :, 1, 64:128], v_s[hp, :, :, 1, :], reads=[("v_s", hp)],
                          writes=[("v_sb", b)], semkey=("v_sb", b))

                def load_q(qi):
                    hp, qb = seq[qi]
                    b = qi % 2
                    r0 = hp * 128
                    c.dma("sp", q_sb[0:64, b, 0, :], qT_s[r0:r0 + 64, blk(qb)], reads=[("qT_s", hp)],
                          writes=[("q_sb", b)], semkey=("q_sb", b))
                    c.dma("sp", q_sb[64:128, b, 1, :], qT_s[r0 + 64:r0 + 128, blk(qb)], reads=[("qT_s", hp)],
                          writes=[("q_sb", b)], semkey=("q_sb", b))

                seq = [(hp, qb) for hp in range(4) for qb in range(NTB)]
                blocks = []
                for qi, (hp, qb) in enumerate(seq):
                    nkt = 4 * qb + 4
                    for kt in range(nkt):
                        for hh in range(2):
                            blocks.append((qi, hp, qb, kt, hh, kt == nkt - 1 and hh == 1))
                SB = [0, 1, 6]
                LA = 2
                setup_done = [-1]

                def ensure_setup(qi):
                    while setup_done[0] < qi:
                        setup_done[0] += 1
                        q2 = setup_done[0]
                        hp, qb = seq[q2]
                        if q2 == 0:
                            load_pair(0)
                            load_q(0)
                        if qb == 1 and hp + 1 < 4:
                            load_pair(hp + 1)
                        if q2 + 1 < len(seq):
                            load_q(q2 + 1)
                        for hh in range(2):
                            hd = 2 * hp + hh
                            c.op("dve", lambda e: e.tensor_scalar(out=bias_sb[:, q2 % 2, hh, :], in0=clog[:, :, hd],
                                                                  scalar1=-1.0, scalar2=cref[:, qb, hd:hd + 1],
                                                                  op0=ALU.mult, op1=ALU.add),
                                 reads=["clog", "cref"], writes=[("bias", q2 % 2, hh)])

                def emit_qk(i):
                    qi, hp, qb, kt, hh, last = blocks[i]
                    kb, qbuf = hp % 2, qi % 2
                    n0 = max(0, kt * 128 - qb * TB)
                    N = TB - n0
                    diag = kt * 128 >= qb * TB
                    pS = SB[i % 3]
                    c.mm(PS[pS][:, 0:N], kT_sb[:, kb, kt * 128:(kt + 1) * 128], q_sb[:, qbuf, hh, n0:TB],
                         start=True, stop=(not diag), reads=[("kT_sb", kb), ("q_sb", qbuf)], writes=[psk(pS)],
                         signal=(not diag))
                    if diag:
                        c.mm(PS[pS][:, 0:128], ident_bf[:, :], mask_bf[:, :], start=False, stop=True,
                             reads=["ident_bf", "mask_bf"], writes=[psk(pS)], signal=True)

                def emit_exp_pv(i):
                    qi, hp, qb, kt, hh, last = blocks[i]
                    kb = hp % 2
                    n0 = max(0, kt * 128 - qb * TB)
                    N = TB - n0
                    pS = SB[i % 3]
                    pO = 2 + 2 * (qi % 2) + hh
                    pbuf = i % 4
                    c.act(P_sb[:, pbuf, 0:N], PS[pS][:, 0:N], AF.Exp, reads=[psk(pS), ("bias", qi % 2, hh)],
                          writes=[("P", pbuf)], scale=0.125, bias=bias_sb[:, qi % 2, hh, kt:kt + 1])
                    c.mm(PS[pO][:, n0:TB], v_sb[:, kb, kt, hh, :], P_sb[:, pbuf, 0:N], start=(kt == 0),
                         stop=(kt == 4 * qb + 3), reads=[("v_sb", kb), ("P", pbuf)], writes=[psk(pO)], signal=True)

                def emit_finalize(qi):
                    hp, qb = seq[qi]
                    yb = qi % 2
                    pA = 2 + 2 * (qi % 2)
                    pB = pA + 1
                    c.op("dve", lambda e: e.reciprocal(out=r_sb[64:128, :], in_=PS[pA][64:128, :]),
                         reads=[psk(pA)], writes=["r_hi"])
                    c.op("dve", lambda e: e.reciprocal(out=r_sb[0:64, :], in_=PS[pB][0:64, :]),
                         reads=[psk(pB)], writes=["r_lo"])
                    c.act(rs_sb[0:64, :], r_sb[64:128, :], AF.Copy, reads=["r_hi"], writes=["rs_lo"])
                    c.act(rs_sb[64:128, :], r_sb[0:64, :], AF.Copy, reads=["r_lo"], writes=["rs_hi"])
                    c.op("dve", lambda e: e.tensor_tensor(out=y_sb[0:64, yb, :], in0=PS[pA][0:64, :],
                                                          in1=rs_sb[0:64, :], op=ALU.mult),
                         reads=[psk(pA), "rs_lo"], writes=[("y", yb)])
                    c.op("dve", lambda e: e.tensor_tensor(out=y_sb[64:128, yb, :], in0=PS[pB][64:128, :],
                                                          in1=rs_sb[64:128, :], op=ALU.mult),
                         reads=[psk(pB), "rs_hi"], writes=[("y", yb)])

                def emit_outproj(qi, fo):
                    hp, qb = seq[qi]
                    yb = qi % 2
                    c.mm(PS[7][:, :], woa_v[:, hp, fo * 128:(fo + 1) * 128], y_sb[:, yb, :], start=True, stop=True,
                         reads=["woa_sb", ("y", yb)], writes=[psk(7)], signal=True)
                    c.op("dve", lambda e: e.tensor_tensor(out=X[:, fo, blk(qb)], in0=PS[7][:, :],
                                                          in1=X[:, fo, blk(qb)], op=ALU.add),
                         reads=[psk(7), xk(fo, qb)], writes=[xk(fo, qb)])

                pending = []
                nblk = len(blocks)
                for i in range(nblk + LA):
                    if i < nblk:
                        ensure_setup(blocks[i][0])
                        emit_qk(i)
                    j = i - LA
                    if j >= 0:
                        emit_exp_pv(j)
                        while pending and pending[0][0] <= j:
                            _, pq, pf = pending.pop(0)
                            emit_outproj(pq, pf)
                        if blocks[j][5]:
                            qi = blocks[j][0]
                            while pending:
                                _, pq, pf = pending.pop(0)
                                emit_outproj(pq, pf)
                            emit_finalize(qi)
                            for fo in range(8):
                                pending.append((j + 14 + 2 * fo, qi, fo))
                while pending:
                    _, pq, pf = pending.pop(0)
                    emit_outproj(pq, pf)
                c.barrier()

        for l in range(NL):
            ffn_phase(l, 0, l * PL + 0)
            m1_phase(l)
            m2_phase(l)
            ffn_phase(l, 1, l * PL + 16)

        c.barrier()
        if final:
            with contextlib.ExitStack() as ph:
                sq = sb(ph, "e_sq", [128, 2, TB], BF16)
                rs_tmp = sb(ph, "e_rs", [128, TB], F32)
                rstd = sb(ph, "e_rstd", [128, TB], F32)
                gb = NL * PL
                for tb in range(NTB):
                    rmsnorm_stats(tb, sq, rs_tmp, rstd, tb % 2)
                    for fc in range(8):
                        c.op("dve", lambda e: e.scalar_tensor_tensor(out=X[:, fc, blk(tb)], in0=X[:, fc, blk(tb)],
                                                                     scalar=cst[:, gb + fc:gb + fc + 1], in1=rstd[:, :],
                                                                     op0=ALU.mult, op1=ALU.mult),
                             reads=[xk(fc, tb), "rstd", "cst"], writes=[xk(fc, tb)])
                    c.dma("sp", oT.rearrange("(c p) t -> p c t", p=128)[:, :, blk(tb)], X[:, :, blk(tb)],
                          reads=[xk(fc, tb) for fc in range(8)], writes=["oT"], semkey=("Xout", tb))
                c.wait_all("sp", ["oT"])
        else:
            for fc in range(8):
                c.dma("sp", oT[fc * 128:(fc + 1) * 128, :], X[:, fc, :], reads=[xk(fc, tb) for tb in range(NTB)],
                      writes=["oT"], semkey=("Xout", fc))
            c.wait_all("sp", ["oT"])
        c.barrier()
        build_program.stats = (dict(c.n_ops), c.n_wait)
    return nc


def _prep_weights(inp, layers):
    NL = len(layers)
    f32 = np.float32
    L = list(layers)
    w_ffn_in = np.asarray(inp["w_ffn_in"], f32)[L]
    w_ffn_out = np.asarray(inp["w_ffn_out"], f32)[L]
    w_in = np.asarray(inp["w_in"], f32)[L]
    w_out = np.asarray(inp["w_out"], f32)[L]
    t = w_ffn_in.reshape(NL, 2, 8, 128, 2, NJ, 128)
    wi = np.ascontiguousarray(t.transpose(0, 1, 5, 3, 2, 4, 6)).reshape(NL, 2, NJ, 128, 2048)
    t = w_ffn_out.reshape(NL, 2, NJ, 128, 8, 128)
    wo = np.ascontiguousarray(t.transpose(0, 1, 4, 3, 2, 5)).reshape(NL, 2, 8, 128, 2816)
    colbase = [0, 128, 256, 384, 512, 640, 768, 896, 1544, 1672, 1800, 1928, 2056, 2184, 2312, 2440]
    chunks = np.stack([w_in[:, :, b:b + 128] for b in colbase], axis=1)
    t = chunks.reshape(NL, 8, 2, 8, 128, 128)
    wq = np.ascontiguousarray(t.transpose(0, 1, 4, 3, 2, 5)).reshape(NL, 8, 128, 2048)
    t = w_in[:, :, 1024:1536].reshape(NL, 8, 128, 512)
    wv = np.ascontiguousarray(t.transpose(0, 2, 1, 3)).reshape(NL, 128, 4096)
    t = w_in[:, :, 1536:1544].reshape(NL, 8, 128, 8)
    wf = np.ascontiguousarray(t.transpose(0, 2, 1, 3)).reshape(NL, 128, 64)
    t = w_out[:, 512:1024, :].reshape(NL, 4, 128, 1024)
    wor = np.ascontiguousarray(t.transpose(0, 2, 1, 3)).reshape(NL, 128, 4096)
    t = w_out[:, 0:512, :].reshape(NL, 4, 128, 1024)
    woa = np.ascontiguousarray(t.transpose(0, 2, 1, 3)).reshape(NL, 128, 4096)
    wa = np.asarray(inp["w_rg_a"], f32)[L]
    wx = np.asarray(inp["w_rg_x"], f32)[L]
    wbd = np.zeros((NL, 128, 4, 2, 128), f32)
    for cch in range(4):
        for half in range(2):
            rs = slice(half * 64, half * 64 + 64)
            wbd[:, rs, cch, 0, rs] = wa[:, 2 * cch + half]
            wbd[:, rs, cch, 1, rs] = wx[:, 2 * cch + half]
    wbd = wbd.reshape(NL, 128, 1024)
    cst = np.zeros((128, NL * PL + 8), f32)
    ng = np.asarray(inp["norm_g"], f32)[L]
    cw = np.asarray(inp["conv_w"], f32)[L]
    for li in range(NL):
        b = li * PL
        cst[:, b:b + 24] = ng[li].reshape(3, 8, 128).transpose(2, 0, 1).reshape(128, 24)
        cst[:, b + 24:b + 40] = cw[li].reshape(4, 4, 128).transpose(2, 0, 1).reshape(128, 16)
        cst[:, b + 40:b + 44] = np.asarray(inp["conv_b"], f32)[L[li]].reshape(4, 128).T
        cst[:, b + 44:b + 48] = np.asarray(inp["b_rg_a"], f32)[L[li]].reshape(4, 128).T
        cst[:, b + 48:b + 52] = np.asarray(inp["b_rg_x"], f32)[L[li]].reshape(4, 128).T
        cst[:, b + 52:b + 56] = np.asarray(inp["rg_lambda"], f32)[L[li]].reshape(4, 128).T
        cst[:, b + 56:b + 64] = np.asarray(inp["b_f"], f32)[L[li]][None, :]
    cst[:, NL * PL:NL * PL + 8] = np.asarray(inp["final_g"], f32).reshape(8, 128).T
    cmat = np.zeros((128, 384), f32)
    cmat[:, 0:128] = np.eye(128, dtype=f32)
    kk = np.arange(128)[:, None]
    qq = np.arange(128)[None, :]
    cmat[:, 128:256] = np.where(kk > qq, MASKVAL, 0.0)
    cmat[:, 256:384] = (kk <= qq).astype(f32)
    return dict(cst=cst, cmat=cmat, wi=wi, wo=wo, wq=wq, wv=wv, wf=wf, wor=wor, woa=woa, wbd=wbd)


FUSED = True
_PROGS = {}


def _prog(NL, final):
    key = (NL, final)
    if key not in _PROGS:
        _PROGS[key] = build_program(NL, final)
    return _PROGS[key]


def kernel(**inputs):
    x = np.asarray(inputs["x"], np.float32)
    xT = [np.ascontiguousarray(x[b].T) for b in range(NB)]
    if FUSED:
        groups = [list(range(DEPTH))]
    else:
        groups = [[l] for l in range(DEPTH)]
    for gi, layers in enumerate(groups):
        final = gi == len(groups) - 1
        w = _prep_weights(inputs, layers)
        nc = _prog(len(layers), final)
        in_maps = [dict(w, xT=xT[b]) for b in range(NB)]
        res = run_bass_kernel_spmd(nc, in_maps, core_ids=list(range(NB)))
        xT = [np.asarray(res.results[b]["oT"], np.float32) for b in range(NB)]
    out = np.stack([xT[b].T for b in range(NB)], axis=0)
    return np.ascontiguousarray(out.astype(np.float32))
```

```python
import contextlib
import math
import numpy as np
import concourse.bass as bass
import concourse.mybir as mybir
from concourse.bass_utils import run_bass_kernel_spmd

F32 = mybir.dt.float32
BF16 = mybir.dt.bfloat16
AF = mybir.ActivationFunctionType
ALU = mybir.AluOpType

D = 1024
S = 4096
NB = 8
DEPTH = 4
DFF = 2816
NJ = DFF // 128
TB = 512
NTB = S // TB
EPS = 1e-6
PL = 64
GELU_K = 0.044715
GELU_S = 2.0 * math.sqrt(2.0 / math.pi)
MASKVAL = -30000.0


class Ctx:
    COMPUTE = ("pe", "act", "dve", "pool")

    def __init__(self, nc, stack):
        self.nc = nc
        self.stack = stack
        self.eng = {"pe": nc.tensor, "act": nc.scalar, "dve": nc.vector, "pool": nc.gpsimd, "sp": nc.sync}
        self.sems = {}
        self.cnt = {}
        for e in self.COMPUTE:
            self.sems[e] = stack.enter_context(nc.semaphore("s_" + e))
            self.cnt[e] = 0
        self.waited = {e: {} for e in self.eng}
        self.tw = {}
        self.tr = {}
        self.n_wait = 0
        self.n_ops = {e: 0 for e in self.eng}

    def _deps(self, reads, writes):
        deps = {}
        for k in reads:
            for s, v in self.tw.get(k, {}).items():
                if deps.get(s, 0) < v:
                    deps[s] = v
        for k in writes:
            for s, v in self.tw.get(k, {}).items():
                if deps.get(s, 0) < v:
                    deps[s] = v
            for s, v in self.tr.get(k, {}).items():
                if deps.get(s, 0) < v:
                    deps[s] = v
        return deps

    def _emit_waits(self, ename, deps):
        E = self.eng[ename]
        wd = self.waited[ename]
        for s, v in deps.items():
            if s == "pe" and ename == "pe":
                continue
            if wd.get(s, 0) >= v:
                continue
            if s == "pe" and v > self.cnt["pe"]:
                raise RuntimeError("wait on an un-signalled PE op")
            E.wait_ge(self.sems[s], v)
            wd[s] = v
            self.n_wait += 1

    def _record(self, semkey, val, reads, writes):
        for k in reads:
            d = self.tr.setdefault(k, {})
            if d.get(semkey, 0) < val:
                d[semkey] = val
        for k in writes:
            d = self.tw.setdefault(k, {})
            if d.get(semkey, 0) < val:
                d[semkey] = val

    def op(self, ename, fn, reads=(), writes=(), signal=True):
        reads = list(reads) + ["*"]
        self._emit_waits(ename, self._deps(reads, writes))
        inst = fn(self.eng[ename])
        self.n_ops[ename] += 1
        if signal:
            self.cnt[ename] += 1
            inst.then_inc(self.sems[ename], 1)
            val = self.cnt[ename]
        else:
            assert ename == "pe"
            val = self.cnt[ename] + 1
        self._record(ename, val, reads, writes)
        return inst

    def dma(self, qname, out, in_, reads, writes, semkey, **kw):
        sk = ("dma", semkey)
        if sk not in self.sems:
            self.sems[sk] = self.stack.enter_context(self.nc.semaphore("d%d" % len(self.sems)))
            self.cnt[sk] = 0
        reads = list(reads) + ["*"]
        self._emit_waits(qname, self._deps(reads, writes))
        inst = self.eng[qname].dma_start(out=out, in_=in_, **kw)
        self.n_ops[qname] += 1
        self.cnt[sk] += 16
        inst.then_inc(self.sems[sk], 16)
        self._record(sk, self.cnt[sk], reads, writes)
        return inst

    def wait_all(self, ename, keys):
        deps = {}
        for k in keys:
            for d in (self.tw.get(k, {}), self.tr.get(k, {})):
                for s, v in d.items():
                    if deps.get(s, 0) < v:
                        deps[s] = v
        if ename == "pe":
            deps.pop("pe", None)
        self._emit_waits(ename, deps)

    def barrier(self):
        for e in ("pe", "act", "dve", "pool", "sp"):
            self.wait_all(e, ["*"])

    def mm(self, out, lhsT, rhs, start, stop, reads, writes, signal=False):
        return self.op("pe", lambda e: e.matmul(out, lhsT=lhsT, rhs=rhs, start=start, stop=stop),
                       reads=reads, writes=writes, signal=signal)

    def act(self, out, in_, func, reads, writes, **kw):
        return self.op("act", lambda e: e.activation(out=out, in_=in_, func=func, **kw), reads=reads, writes=writes)


class WStream:
    def __init__(self, c, name, buf, nslots, items):
        self.c, self.name, self.buf, self.nslots, self.items = c, name, buf, nslots, items
        self.issued = 0
        self.consumed = 0
        for _ in range(nslots):
            self._issue()

    def _issue(self):
        if self.issued < len(self.items):
            src, keys = self.items[self.issued]
            s = self.issued % self.nslots
            self.c.dma("sp", self.buf[:, s, :], src, reads=keys, writes=[(self.name, s)], semkey=(self.name, s))
            self.issued += 1

    def next(self):
        s = self.consumed % self.nslots
        self.consumed += 1
        return s, (self.name, s)

    def release(self, n=1):
        for _ in range(n):
            self._issue()


def build_program(NL, final):
    nc = bass.Bass("TRN2", target_bir_lowering=False)
    NC = NL * PL + 8

    def din(name, shape, dt=F32):
        return nc.dram_tensor(name, shape, dt, kind="ExternalInput").ap()

    def dint(name, shape, dt=BF16):
        return nc.dram_tensor(name, shape, dt, kind="Internal").ap()

    xT = din("xT", [D, S])
    cst_d = din("cst", [128, NC])
    cmat_d = din("cmat", [128, 384])
    wi_d = din("wi", [NL, 2, NJ, 128, 2048])
    wo_d = din("wo", [NL, 2, 8, 128, 2816])
    wq_d = din("wq", [NL, 8, 128, 2048])
    wv_d = din("wv", [NL, 128, 4096])
    wf_d = din("wf", [NL, 128, 64])
    wor_d = din("wor", [NL, 128, 4096])
    woa_d = din("woa", [NL, 128, 4096])
    wbd_d = din("wbd", [NL, 128, 1024])
    oT = nc.dram_tensor("oT", [D, S], F32, kind="ExternalOutput").ap()

    wi_b = dint("wi_b", [NL, 2, NJ, 128, 2048])
    wo_b = dint("wo_b", [NL, 2, 8, 128, 2816])
    wq_b = dint("wq_b", [NL, 8, 128, 2048])
    wv_b = dint("wv_b", [NL, 128, 4096])
    wf_b = dint("wf_b", [NL, 128, 64])
    wor_b = dint("wor_b", [NL, 128, 4096])
    woa_b = dint("woa_b", [NL, 128, 4096])
    wbd_b = dint("wbd_b", [NL, 128, 1024])
    qT_s = dint("qT_s", [512, S])
    kT_s = dint("kT_s", [512, S])
    v_s = dint("v_s", [4, 128, 32, 2, 64])

    with contextlib.ExitStack() as st:
        c = Ctx(nc, st)

        uid = [0]

        def sb(stack, name, shape, dt):
            uid[0] += 1
            return stack.enter_context(nc.sbuf_tensor("%s_%d" % (name, uid[0]), shape, dt))

        X = sb(st, "X", [128, 8, S], F32)
        cst = sb(st, "cst_sb", [128, NC], F32)
        tri_sb = sb(st, "tri_sb", [128, 128], F32)
        ident_bf = sb(st, "ident_bf", [128, 128], BF16)
        mask_bf = sb(st, "mask_bf", [128, 128], BF16)
        ones_bf = sb(st, "ones_bf", [128, 128], BF16)
        ones_f = sb(st, "ones_f", [128, 128], F32)
        spc = sb(st, "spc", [128, NL * 8], F32)
        clog = sb(st, "clog", [128, 32, 8], F32)
        cref = sb(st, "cref", [128, NTB, 8], F32)
        PS = [st.enter_context(nc.psum_tensor("ps%d" % b, [128, 512], F32)) for b in range(8)]

        def psk(b):
            return ("ps", b)

        def xk(fc, tb):
            return ("X", fc, tb)

        def blk(tb):
            return slice(tb * TB, (tb + 1) * TB)

        def cast(dst, src, key, tag, maxrows=1024):
            rows = dst.shape[0]
            r0 = 0
            i = 0
            while r0 < rows:
                r1 = min(rows, r0 + maxrows)
                c.dma("pool", dst[r0:r1, :], src[r0:r1, :], reads=[], writes=[key], semkey=("cast", tag, i))
                r0 = r1
                i += 1

        def cast_ffn(l, i):
            for half in range(2):
                js = slice(half * 11, half * 11 + 11)
                cast(wi_b[l, i, js].rearrange("j p e -> (j p) e"), wi_d[l, i, js].rearrange("j p e -> (j p) e"),
                     ("wi_b", l, i, half), ("wi", i, half), maxrows=704)
            for half in range(2):
                fs = slice(half * 4, half * 4 + 4)
                cast(wo_b[l, i, fs].rearrange("f p (a e) -> (f p a) e", a=2),
                     wo_d[l, i, fs].rearrange("f p (a e) -> (f p a) e", a=2),
                     ("wo_b", l, i, half), ("wo", i, half), maxrows=1024)

        def cast_mix(l):
            k = ("mixw", l)
            cast(wq_b[l].rearrange("t p e -> (t p) e"), wq_d[l].rearrange("t p e -> (t p) e"), k, ("wq",), maxrows=512)
            cast(wv_b[l].rearrange("p (a e) -> (p a) e", a=2), wv_d[l].rearrange("p (a e) -> (p a) e", a=2), k, ("wv",))
            cast(wf_b[l], wf_d[l], k, ("wf",))
            cast(wbd_b[l], wbd_d[l], k, ("wbd",))
            cast(wor_b[l].rearrange("p (a e) -> (p a) e", a=2), wor_d[l].rearrange("p (a e) -> (p a) e", a=2), k, ("wor",))
            cast(woa_b[l].rearrange("p (a e) -> (p a) e", a=2), woa_d[l].rearrange("p (a e) -> (p a) e", a=2), k, ("woa",))

        def cast_layer(l):
            cast_ffn(l, 0)
            cast_mix(l)
            cast_ffn(l, 1)

        cast_layer(0)
        c.dma("sp", cst[:, :], cst_d[:, :], reads=[], writes=["cst"], semkey="cst")
        c.dma("sp", tri_sb[:, :], cmat_d[:, 256:384], reads=[], writes=["cmat"], semkey="cmat")
        for fc in range(8):
            c.dma("sp", X[:, fc, :], xT[fc * 128:(fc + 1) * 128, :], reads=[],
                  writes=[xk(fc, tb) for tb in range(NTB)], semkey=("Xld", fc))
        c.op("dve", lambda e: e.memset(ones_bf[:, :], 1.0), writes=["ones_bf"])
        c.op("dve", lambda e: e.memset(ones_f[:, :], 1.0), writes=["ones_f"])
        with contextlib.ExitStack() as ph0:
            cm_tmp = sb(ph0, "cm_tmp", [128, 256], F32)
            c.dma("sp", cm_tmp[:, :], cmat_d[:, 0:256], reads=[], writes=["cm_tmp"], semkey="cm_tmp")
            c.act(ident_bf[:, :], cm_tmp[:, 0:128], AF.Copy, reads=["cm_tmp"], writes=["ident_bf"])
            c.act(mask_bf[:, :], cm_tmp[:, 128:256], AF.Copy, reads=["cm_tmp"], writes=["mask_bf"])
        tri_f = tri_sb[:, :]
        for l in range(NL):
            lam = cst[:, l * PL + 52:l * PL + 56]
            c.act(spc[:, l * 8:l * 8 + 4], lam, AF.Sigmoid, reads=["cst"], writes=["spc"])
            c.act(spc[:, l * 8:l * 8 + 4], spc[:, l * 8:l * 8 + 4], AF.Ln, reads=["spc"], writes=["spc"])
            c.op("dve", lambda e: e.tensor_scalar(out=spc[:, l * 8 + 4:l * 8 + 8], in0=spc[:, l * 8:l * 8 + 4],
                                                  scalar1=16.0, scalar2=None, op0=ALU.mult),
                 reads=["spc"], writes=["spc"])
            c.op("dve", lambda e: e.tensor_scalar(out=spc[:, l * 8:l * 8 + 4], in0=spc[:, l * 8:l * 8 + 4],
                                                  scalar1=8.0, scalar2=None, op0=ALU.mult),
                 reads=["spc"], writes=["spc"])

        def rmsnorm_stats(tb, sq, rs_tmp, rstd, ps_stat):
            for fc in range(8):
                b = fc % 2
                c.act(sq[:, b, :], X[:, fc, blk(tb)], AF.Square, reads=[xk(fc, tb)], writes=[("sq", b)])
                c.mm(PS[ps_stat][:, :], ones_bf[:, :], sq[:, b, :], start=(fc == 0), stop=(fc == 7),
                     reads=[("sq", b), "ones_bf"], writes=[psk(ps_stat)], signal=True)
            c.act(rs_tmp[:, :], PS[ps_stat][:, :], AF.Sqrt, reads=[psk(ps_stat)], writes=["rs_tmp"],
                  scale=1.0 / D, bias=EPS)
            c.op("dve", lambda e: e.reciprocal(out=rstd[:, :], in_=rs_tmp[:, :]), reads=["rs_tmp"], writes=["rstd"])

        def rmsnorm_apply(tb, gbase, rstd, xn):
            for fc in range(8):
                c.op("dve", lambda e: e.scalar_tensor_tensor(out=xn[:, fc, :], in0=X[:, fc, blk(tb)],
                                                             scalar=cst[:, gbase + fc:gbase + fc + 1], in1=rstd[:, :],
                                                             op0=ALU.mult, op1=ALU.mult),
                     reads=[xk(fc, tb), "rstd", "cst"], writes=[("xn", fc)])

        def ffn_phase(l, i, gbase):
            if not (l == 0 and i == 0):
                c.barrier()
            with contextlib.ExitStack() as ph:
                xn = sb(ph, "f_xn", [128, 8, TB], BF16)
                sq = sb(ph, "f_sq", [128, 2, TB], BF16)
                rs_tmp = sb(ph, "f_rs", [128, TB], F32)
                rstd = sb(ph, "f_rstd", [128, TB], F32)
                h = sb(ph, "f_h", [128, NJ, TB], BF16)
                sg = sb(ph, "f_sg", [128, 2, TB], F32)
                wib = sb(ph, "f_wi", [128, 4, 2048], BF16)
                wob = sb(ph, "f_wo", [128, 3, 2816], BF16)
                wi_items = [(wi_b[l, i, j], [("wi_b", l, i, j // 11)]) for _ in range(NTB) for j in range(NJ)]
                wo_items = [(wo_b[l, i, fo], [("wo_b", l, i, fo // 4)]) for _ in range(NTB) for fo in range(8)]
                wis = WStream(c, "wi_s", wib, 4, wi_items)
                wos = WStream(c, "wo_s", wob, 3, wo_items)
                for tb in range(NTB):
                    rmsnorm_stats(tb, sq, rs_tmp, rstd, 6)
                    rmsnorm_apply(tb, gbase, rstd, xn)
                    for j in range(NJ):
                        s, skey = wis.next()
                        wt = wib[:, s, :].rearrange("p (k g m) -> p k g m", k=8, g=2)
                        pg, pu = j % 2, 2 + j % 2
                        for gu, pb in ((0, pg), (1, pu)):
                            for kc in range(8):
                                c.mm(PS[pb][:, :], wt[:, kc, gu, :], xn[:, kc, :], start=(kc == 0), stop=(kc == 7),
                                     reads=[skey, ("xn", kc)], writes=[psk(pb)], signal=(kc == 7))
                        wis.release()
                        c.act(sg[:, j % 2, :], PS[pg][:, :], AF.Silu, reads=[psk(pg)], writes=[("sg", j % 2)])
                        c.op("dve", lambda e: e.tensor_tensor(out=h[:, j, :], in0=PS[pu][:, :], in1=sg[:, j % 2, :],
                                                              op=ALU.mult),
                             reads=[psk(pu), ("sg", j % 2)], writes=[("h", j)])
                    for fo in range(8):
                        s, skey = wos.next()
                        wt = wob[:, s, :].rearrange("p (j m) -> p j m", j=NJ)
                        pb = 4 + fo % 2
                        for j in range(NJ):
                            c.mm(PS[pb][:, :], wt[:, j, :], h[:, j, :], start=(j == 0), stop=(j == NJ - 1),
                                 reads=[skey, ("h", j)], writes=[psk(pb)], signal=(j == NJ - 1))
                        wos.release()
                        c.op("dve", lambda e: e.scalar_tensor_tensor(out=X[:, fo, blk(tb)], in0=PS[pb][:, :], scalar=0.5,
                                                                     in1=X[:, fo, blk(tb)], op0=ALU.mult, op1=ALU.add),
                             reads=[psk(pb), xk(fo, tb)], writes=[xk(fo, tb)])
                c.barrier()

        def m1_phase(l):
            c.barrier()
            cb = l * PL
            with contextlib.ExitStack() as ph:
                xn = sb(ph, "m_xn", [128, 8, TB], BF16)
                sq = sb(ph, "m_sq", [128, 1, TB], BF16)
                rstd = sb(ph, "m_rstd", [128, TB], F32)
                wsb = sb(ph, "m_ws", [128, 3, 2048], BF16)
                wf_sb = sb(ph, "m_wf", [128, 64], BF16)
                wbd_sb = sb(ph, "m_wbd", [128, 1024], BF16)
                stq = sb(ph, "m_stq", [128, 4, TB], BF16)
                stk = sb(ph, "m_stk", [128, 4, TB], BF16)
                stv = sb(ph, "m_stv", [128, 4, 8, 64], BF16)
                xr_sb = sb(ph, "m_xr", [128, 4, TB + 3], F32)
                hcar = sb(ph, "m_hcar", [128, 4], F32)
                carry = sb(ph, "m_carry", [128, 8], F32)
                fb = sb(ph, "m_fb", [128, 32], F32)
                bfb = sb(ph, "m_bfb", [128, 32], F32)
                spx = sb(ph, "m_spx", [128, 16], F32)
                acc = sb(ph, "m_acc", [128, 2, TB], F32)
                xcb = sb(ph, "m_xcb", [128, 2, TB], BF16)
                tr = sb(ph, "m_tr", [128, 2, TB], F32)
                ti = sb(ph, "m_ti", [128, 2, TB], F32)
                ta = sb(ph, "m_ta", [128, 2, TB], F32)
                tth = sb(ph, "m_tth", [128, 2, TB], F32)
                tx = sb(ph, "m_tx", [128, 2, TB], F32)
                yrec = sb(ph, "m_yrec", [128, 4, TB], BF16)
                mk = ("mixw", l)
                c.dma("sp", wf_sb[:, :], wf_b[l], reads=[mk], writes=["wf_sb"], semkey="wf_sb")
                c.dma("sp", wbd_sb[:, :], wbd_b[l], reads=[mk], writes=["wbd_sb"], semkey="wbd_sb")
                items = []
                for _ in range(NTB):
                    for t in (4, 5, 0, 1, 2, 3, 6, 7):
                        items.append((wq_b[l, t], [mk]))
                    items.append((wv_b[l, :, 0:2048], [mk]))
                    items.append((wv_b[l, :, 2048:4096], [mk]))
                    items.append((wor_b[l, :, 0:2048], [mk]))
                    items.append((wor_b[l, :, 2048:4096], [mk]))
                ws = WStream(c, "m_ws", wsb, 3, items)
                c.op("dve", lambda e: e.memset(xr_sb[:, :, :], 0.0), writes=[("xr", k) for k in range(4)])
                c.op("dve", lambda e: e.memset(hcar[:, :], 0.0), writes=["hcar"])
                c.op("dve", lambda e: e.memset(carry[:, :], 0.0), writes=["carry"])
                for tt in range(4):
                    c.op("dve", lambda e: e.tensor_copy(out=bfb[:, tt * 8:(tt + 1) * 8], in_=cst[:, cb + 56:cb + 64]),
                         reads=["cst"], writes=["bfb"])
                c.op("dve", lambda e: e.tensor_scalar(out=spx[:, 0:4], in0=spc[:, l * 8:l * 8 + 4], scalar1=0.5,
                                                      scalar2=None, op0=ALU.mult), reads=["spc"], writes=["spx"])
                c.op("dve", lambda e: e.tensor_copy(out=spx[:, 4:8], in_=spc[:, l * 8:l * 8 + 4]),
                     reads=["spc"], writes=["spx"])
                c.op("dve", lambda e: e.tensor_scalar(out=spx[:, 8:12], in0=cst[:, cb + 44:cb + 48], scalar1=0.5,
                                                      scalar2=None, op0=ALU.mult), reads=["cst"], writes=["spx"])
                c.op("dve", lambda e: e.tensor_scalar(out=spx[:, 12:16], in0=cst[:, cb + 48:cb + 52], scalar1=0.5,
                                                      scalar2=None, op0=ALU.mult), reads=["cst"], writes=["spx"])
                wbd_v = wbd_sb[:, :].rearrange("p (c g m) -> p c g m", c=4, g=2)
                wf_v = wf_sb[:, :].rearrange("p (k h) -> p k h", k=8)
                rot = [0]

                def nextbank():
                    b = rot[0] % 4
                    rot[0] += 1
                    return b

                def col(idx):
                    return cst[:, idx:idx + 1]

                def sx(j):
                    return spx[:, j:j + 1]

                def sqbuf(fc):
                    b = fc % 5
                    if b < 4:
                        return yrec[:, b, :], ("yrec", b)
                    return sq[:, 0, :], ("sq", 0)

                def stats_sq(tb, fc):
                    buf, key = sqbuf(fc)
                    c.act(buf, X[:, fc, blk(tb)], AF.Square, reads=[xk(fc, tb)], writes=[key])

                def stats_mm(tb, fc):
                    buf, key = sqbuf(fc)
                    c.mm(PS[6][:, :], ones_bf[:, :], buf, start=(fc == 0), stop=(fc == 7),
                         reads=[key, "ones_bf"], writes=[psk(6)], signal=True)

                def stats(tb):
                    for fc in range(8):
                        stats_sq(tb, fc)
                        stats_mm(tb, fc)

                def stats_fin(tb):
                    c.act(rstd[:, :], PS[6][:, :], AF.Sqrt, reads=[psk(6)], writes=["rstd"], scale=1.0 / D, bias=EPS)
                    c.op("dve", lambda e: e.reciprocal(out=rstd[:, :], in_=rstd[:, :]), reads=["rstd"], writes=["rstd"])

                def inproj_chunk(wt, skey, cc, evac):
                    pb = nextbank()
                    for kc in range(8):
                        c.mm(PS[pb][:, :], wt[:, kc, cc, :], xn[:, kc, :], start=(kc == 0), stop=(kc == 7),
                             reads=[skey, ("xn", kc)], writes=[psk(pb)], signal=(kc == 7))
                    evac(pb)

                def inproj_tile(evacs):
                    s, skey = ws.next()
                    wt = wsb[:, s, :].rearrange("p (k g m) -> p k g m", k=8, g=2)
                    for cc in range(2):
                        inproj_chunk(wt, skey, cc, evacs[cc])
                    ws.release()

                def conv(k):
                    kk = k % 2
                    xk_ = ("xr", k)
                    c.op("dve", lambda e: e.tensor_scalar(out=acc[:, kk, :], in0=xr_sb[:, k, 0:TB],
                                                          scalar1=col(cb + 24 + k), scalar2=col(cb + 40 + k),
                                                          op0=ALU.mult, op1=ALU.add),
                         reads=[xk_, "cst"], writes=[("acc", kk)])
                    for tap in range(1, 4):
                        c.op("dve", lambda e: e.scalar_tensor_tensor(out=acc[:, kk, :], in0=xr_sb[:, k, tap:tap + TB],
                                                                     scalar=col(cb + 24 + tap * 4 + k), in1=acc[:, kk, :],
                                                                     op0=ALU.mult, op1=ALU.add),
                             reads=[xk_, "cst", ("acc", kk)], writes=[("acc", kk)])
                    c.act(xr_sb[:, k, 0:3], xr_sb[:, k, TB:TB + 3], AF.Copy, reads=[xk_], writes=[xk_])
                    c.act(xcb[:, kk, :], acc[:, kk, :], AF.Copy, reads=[("acc", kk)], writes=[("xcb", kk)])

                def gates(k):
                    kk = k % 2
                    pa = nextbank()
                    c.mm(PS[pa][:, :], wbd_v[:, k, 0, :], xcb[:, kk, :], start=True, stop=True,
                         reads=["wbd_sb", ("xcb", kk)], writes=[psk(pa)], signal=True)
                    px = nextbank()
                    c.mm(PS[px][:, :], wbd_v[:, k, 1, :], xcb[:, kk, :], start=True, stop=True,
                         reads=["wbd_sb", ("xcb", kk)], writes=[psk(px)], signal=True)
                    c.act(tr[:, kk, :], PS[pa][:, :], AF.Tanh, reads=[psk(pa), "spx"], writes=[("tr", kk)],
                          scale=0.5, bias=sx(8 + k))
                    c.act(ti[:, kk, :], PS[px][:, :], AF.Tanh, reads=[psk(px), "spx"], writes=[("ti", kk)],
                          scale=0.5, bias=sx(12 + k))

                def gate_funcs(k):
                    kk = k % 2
                    c.act(ta[:, kk, :], tr[:, kk, :], AF.Exp, reads=[("tr", kk), "spx"], writes=[("ta", kk)],
                          scale=sx(k), bias=sx(k))
                    c.act(tth[:, kk, :], tr[:, kk, :], AF.Tanh, reads=[("tr", kk), "spx"], writes=[("tth", kk)],
                          scale=sx(k), bias=sx(k))
                    c.act(tr[:, kk, :], tr[:, kk, :], AF.Exp, reads=[("tr", kk), "spx"], writes=[("tr", kk)],
                          scale=sx(4 + k), bias=sx(4 + k))
                    c.op("dve", lambda e: e.scalar_tensor_tensor(out=tth[:, kk, :], in0=tr[:, kk, :], scalar=1.0,
                                                                 in1=tth[:, kk, :], op0=ALU.add, op1=ALU.mult),
                         reads=[("tr", kk), ("tth", kk)], writes=[("tth", kk)])
                    c.op("dve", lambda e: e.scalar_tensor_tensor(out=ti[:, kk, :], in0=ti[:, kk, :], scalar=1.0,
                                                                 in1=acc[:, kk, :], op0=ALU.add, op1=ALU.mult),
                         reads=[("ti", kk), ("acc", kk)], writes=[("ti", kk)])

                def gelu_pre(k, pb):
                    kk = k % 2
                    c.act(tx[:, kk, :], PS[pb][:, :], AF.Square, reads=[psk(pb)], writes=[("tx", kk)],
                          scale=math.sqrt(GELU_K))
                    c.op("dve", lambda e: e.scalar_tensor_tensor(out=tx[:, kk, :], in0=tx[:, kk, :], scalar=1.0,
                                                                 in1=PS[pb][:, :], op0=ALU.add, op1=ALU.mult),
                         reads=[("tx", kk), psk(pb)], writes=[("tx", kk)])
                    c.act(tx[:, kk, :], tx[:, kk, :], AF.Tanh, reads=[("tx", kk)], writes=[("tx", kk)],
                          scale=0.5 * GELU_S)
                    c.op("dve", lambda e: e.scalar_tensor_tensor(out=tx[:, kk, :], in0=tx[:, kk, :], scalar=1.0,
                                                                 in1=PS[pb][:, :], op0=ALU.add, op1=ALU.mult),
                         reads=[("tx", kk), psk(pb)], writes=[("tx", kk)])

                def rec_tail_sqrt(k):
                    kk = k % 2
                    c.act(tth[:, kk, :], tth[:, kk, :], AF.Sqrt, reads=[("tth", kk)], writes=[("tth", kk)],
                          scale=-1.0 / 16.0)

                def rec_tail(k):
                    kk = k % 2
                    c.op("dve", lambda e: e.tensor_tensor(out=ti[:, kk, :], in0=ti[:, kk, :], in1=tth[:, kk, :],
                                                          op=ALU.mult),
                         reads=[("ti", kk), ("tth", kk)], writes=[("ti", kk)])
                    c.op("dve", lambda e: e.tensor_tensor_scan(out=tr[:, kk, :], data0=ta[:, kk, :], data1=ti[:, kk, :],
                                                               initial=hcar[:, k:k + 1], op0=ALU.mult, op1=ALU.add),
                         reads=[("ta", kk), ("ti", kk), "hcar", ("tr", kk)], writes=[("tr", kk)])
                    c.act(hcar[:, k:k + 1], tr[:, kk, TB - 1:TB], AF.Copy, reads=[("tr", kk)], writes=["hcar"])
                    c.op("dve", lambda e: e.tensor_tensor(out=yrec[:, k, :], in0=tx[:, kk, :], in1=tr[:, kk, :],
                                                          op=ALU.mult),
                         reads=[("tx", kk), ("tr", kk)], writes=[("yrec", k)])

                def ev_q(ch):
                    return lambda pb: c.act(stq[:, ch, :], PS[pb][:, :], AF.Copy, reads=[psk(pb)], writes=["stq"])

                def ev_k(ch):
                    return lambda pb: c.act(stk[:, ch, :], PS[pb][:, :], AF.Copy, reads=[psk(pb)], writes=["stk"])

                def ev_xr(k):
                    return lambda pb: c.act(xr_sb[:, k, 3:TB + 3], PS[pb][:, :], AF.Copy, reads=[psk(pb)],
                                            writes=[("xr", k)])

                def ev_gr(k):
                    return lambda pb: gelu_pre(k, pb)

                stats(0)
                stats_fin(0)
                rmsnorm_apply(0, cb + 8, rstd, xn)
                for tb in range(NTB):
                    inproj_tile([ev_xr(0), ev_xr(1)])
                    conv(0)
                    conv(1)
                    inproj_tile([ev_xr(2), ev_xr(3)])
                    if tb + 1 < NTB:
                        for fc in range(5):
                            stats_sq(tb + 1, fc)
                    inproj_tile([ev_q(0), ev_q(1)])
                    inproj_tile([ev_q(2), ev_q(3)])
                    if tb + 1 < NTB:
                        for fc in range(8):
                            if fc >= 5:
                                stats_sq(tb + 1, fc)
                            stats_mm(tb + 1, fc)
                    gates(0)
                    gates(1)
                    gate_funcs(0)
                    gate_funcs(1)
                    conv(2)
                    conv(3)
                    inproj_tile([ev_k(0), ev_k(1)])
                    inproj_tile([ev_k(2), ev_k(3)])
                    inproj_tile([ev_gr(0), ev_gr(1)])
                    rec_tail_sqrt(0)
                    rec_tail_sqrt(1)
                    rec_tail(0)
                    rec_tail(1)
                    gates(2)
                    gates(3)
                    gate_funcs(2)
                    gate_funcs(3)
                    inproj_tile([ev_gr(2), ev_gr(3)])
                    sA, kA = ws.next()
                    sB, kB = ws.next()
                    for tt in range(4):
                        pb = 4 + tt % 2
                        tok = slice(tt * 128, (tt + 1) * 128)
                        for kc in range(8):
                            sl, kk_ = (sA, kA) if kc < 4 else (sB, kB)
                            wv_t = wsb[:, sl, :].rearrange("p (k n) -> p k n", k=4)
                            c.mm(PS[pb][:, :], xn[:, kc, tok], wv_t[:, kc % 4, :], start=(kc == 0), stop=(kc == 7),
                                 reads=[kk_, ("xn", kc)], writes=[psk(pb)], signal=(kc == 7))
                        c.act(stv[:, tt, :, :], PS[pb][:, :].rearrange("p (h d) -> p h d", h=8), AF.Copy,
                              reads=[psk(pb)], writes=["stv"])
                        for kc in range(8):
                            c.mm(PS[7][:, tt * 8:(tt + 1) * 8], xn[:, kc, tok], wf_v[:, kc, :], start=(kc == 0),
                                 stop=(kc == 7), reads=["wf_sb", ("xn", kc)], writes=[psk(7)], signal=(kc == 7))
                    ws.release(2)
                    rec_tail_sqrt(2)
                    rec_tail_sqrt(3)
                    rec_tail(2)
                    rec_tail(3)
                    if tb + 1 < NTB:
                        stats_fin(tb + 1)
                        rmsnorm_apply(tb + 1, cb + 8, rstd, xn)
                    sA, kA = ws.next()
                    sB, kB = ws.next()
                    for fo in range(8):
                        pb = 4 + fo % 2
                        for kc in range(4):
                            sl, kk_ = (sA, kA) if kc < 2 else (sB, kB)
                            wor_t = wsb[:, sl, :].rearrange("p (k f) -> p k f", k=2)
                            c.mm(PS[pb][:, :], wor_t[:, kc % 2, fo * 128:(fo + 1) * 128], yrec[:, kc, :],
                                 start=(kc == 0), stop=(kc == 3), reads=[kk_, ("yrec", kc)], writes=[psk(pb)],
                                 signal=(kc == 3))
                        c.op("dve", lambda e: e.tensor_tensor(out=X[:, fo, blk(tb)], in0=PS[pb][:, :],
                                                              in1=X[:, fo, blk(tb)], op=ALU.add),
                             reads=[psk(pb), xk(fo, tb)], writes=[xk(fo, tb)])
                    ws.release(2)
                    c.op("dve", lambda e: e.tensor_tensor(out=fb[:, :], in0=PS[7][:, 0:32], in1=bfb[:, :], op=ALU.add),
                         reads=[psk(7), "bfb"], writes=["fb"])
                    c.act(fb[:, :], fb[:, :], AF.Tanh, reads=["fb"], writes=["fb"], scale=0.5)
                    c.act(fb[:, :], fb[:, :], AF.Ln, reads=["fb"], writes=["fb"], scale=0.5, bias=0.5)
                    c.mm(PS[7][:, 32:64], tri_f, fb[:, :], start=True, stop=True, reads=["cmat", "fb"],
                         writes=[psk(7)], signal=True)
                    c.mm(PS[7][:, 64:96], ones_f[:, :], fb[:, :], start=True, stop=True, reads=["ones_f", "fb"],
                         writes=[psk(7)], signal=True)
                    for tt in range(4):
                        n = tb * 4 + tt
                        c.op("dve", lambda e: e.tensor_tensor(out=clog[:, n, :], in0=PS[7][:, 32 + tt * 8:40 + tt * 8],
                                                              in1=carry[:, :], op=ALU.add),
                             reads=[psk(7), "carry"], writes=["clog"])
                        c.op("dve", lambda e: e.tensor_tensor(out=carry[:, :], in0=PS[7][:, 64 + tt * 8:72 + tt * 8],
                                                              in1=carry[:, :], op=ALU.add),
                             reads=[psk(7), "carry"], writes=["carry"])
                        if tt == 1:
                            c.op("dve", lambda e: e.tensor_copy(out=cref[:, tb, :], in_=carry[:, :]),
                                 reads=["carry"], writes=["cref"])
                    c.dma("sp", qT_s.rearrange("(c p) t -> p c t", p=128)[:, :, blk(tb)], stq[:, :, :],
                          reads=["stq"], writes=[("qT_s", p) for p in range(4)], semkey="stq")
                    c.dma("sp", kT_s.rearrange("(c p) t -> p c t", p=128)[:, :, blk(tb)], stk[:, :, :],
                          reads=["stk"], writes=[("kT_s", p) for p in range(4)], semkey="stk")
                    for p in range(4):
                        c.dma("sp", v_s[p, :, tb * 4:(tb + 1) * 4, :, :], stv[:, :, 2 * p:2 * p + 2, :],
                              reads=["stv"], writes=[("v_s", p)], semkey="stv")
                c.barrier()

        def m2_phase(l):
            c.barrier()
            if l + 1 < NL:
                cast_layer(l + 1)
            mk = ("mixw", l)
            with contextlib.ExitStack() as ph:
                kT_sb = sb(ph, "a_kT", [128, 2, S], BF16)
                v_sb = sb(ph, "a_v", [128, 2, 32, 2, 128], BF16)
                q_sb = sb(ph, "a_q", [128, 2, 2, TB], BF16)
                P_sb = sb(ph, "a_P", [128, 4, TB], BF16)
                bias_sb = sb(ph, "a_bias", [128, 2, 2, 32], F32)
                r_sb = sb(ph, "a_r", [128, TB], F32)
                rs_sb = sb(ph, "a_rs", [128, TB], F32)
                y_sb = sb(ph, "a_y", [128, 2, TB], BF16)
                woa_sb = sb(ph, "a_woa", [128, 4096], BF16)
                woa_v = woa_sb[:, :].rearrange("p (h f) -> p h f", h=4)
                c.dma("sp", woa_sb[:, :], woa_b[l], reads=[mk], writes=["woa_sb"], semkey="woa_sb")
                c.op("dve", lambda e: e.memset(v_sb[:, :, :, :, :], 1.0), writes=[("v_sb", 0), ("v_sb", 1)])
                c.op("dve", lambda e: e.memset(q_sb[:, :, :, :], 0.0), writes=[("q_sb", 0), ("q_sb", 1)])

                def load_pair(hp):
                    b = hp % 2
                    c.dma("sp", kT_sb[:, b, :], kT_s[hp * 128:(hp + 1) * 128, :], reads=[("kT_s", hp)],
                          writes=[("kT_sb", b)], semkey=("kT_sb", b))
                    c.dma("sp", v_sb[:, b, :, 0, 0:64], v_s[hp, :, :, 0, :], reads=[("v_s", hp)],
                          writes=[("v_sb", b)], semkey=("v_sb", b))
                    c.dma("sp", v_sb[:, b, :, 1, 64:128], v_s[hp, :, :, 1, :], reads=[("v_s", hp)],
                          writes=[("v_sb", b)], semkey=("v_sb", b))

                def load_q(qi):
                    hp, qb = seq[qi]
                    b = qi % 2
                    r0 = hp * 128
                    c.dma("sp", q_sb[0:64, b, 0, :], qT_s[r0:r0 + 64, blk(qb)], reads=[("qT_s", hp)],
                          writes=[("q_sb", b)], semkey=("q_sb", b))
                    c.dma("sp", q_sb[64:128, b, 1, :], qT_s[r0 + 64:r0 + 128, blk(qb)], reads=[("qT_s", hp)],
                          writes=[("q_sb", b)], semkey=("q_sb", b))

                seq = [(hp, qb) for hp in range(4) for qb in range(NTB)]
                blocks = []
                for qi, (hp, qb) in enumerate(seq):
                    nkt = 4 * qb + 4
                    for kt in range(nkt):
                        for hh in range(2):
                            blocks.append((qi, hp, qb, kt, hh, kt == nkt - 1 and hh == 1))
                SB = [0, 1, 6]
                LA = 2
                setup_done = [-1]

                def ensure_setup(qi):
                    while setup_done[0] < qi:
                        setup_done[0] += 1
                        q2 = setup_done[0]
                        hp, qb = seq[q2]
                        if q2 == 0:
                            load_pair(0)
                            load_q(0)
                        if qb == 1 and hp + 1 < 4:
                            load_pair(hp + 1)
                        if q2 + 1 < len(seq):
                            load_q(q2 + 1)
                        for hh in range(2):
                            hd = 2 * hp + hh
                            c.op("dve", lambda e: e.tensor_scalar(out=bias_sb[:, q2 % 2, hh, :], in0=clog[:, :, hd],
                                                                  scalar1=-1.0, scalar2=cref[:, qb, hd:hd + 1],
                                                                  op0=ALU.mult, op1=ALU.add),
                                 reads=["clog", "cref"], writes=[("bias", q2 % 2, hh)])

                def emit_qk(i):
                    qi, hp, qb, kt, hh, last = blocks[i]
                    kb, qbuf = hp % 2, qi % 2
                    n0 = max(0, kt * 128 - qb * TB)
                    N = TB - n0
                    diag = kt * 128 >= qb * TB
                    pS = SB[i % 3]
                    c.mm(PS[pS][:, 0:N], kT_sb[:, kb, kt * 128:(kt + 1) * 128], q_sb[:, qbuf, hh, n0:TB],
                         start=True, stop=(not diag), reads=[("kT_sb", kb), ("q_sb", qbuf)], writes=[psk(pS)],
                         signal=(not diag))
                    if diag:
                        c.mm(PS[pS][:, 0:128], ident_bf[:, :], mask_bf[:, :], start=False, stop=True,
                             reads=["ident_bf", "mask_bf"], writes=[psk(pS)], signal=True)

                def emit_exp_pv(i):
                    qi, hp, qb, kt, hh, last = blocks[i]
                    kb = hp % 2
                    n0 = max(0, kt * 128 - qb * TB)
                    N = TB - n0
                    pS = SB[i % 3]
                    pO = 2 + 2 * (qi % 2) + hh
                    pbuf = i % 4
                    c.act(P_sb[:, pbuf, 0:N], PS[pS][:, 0:N], AF.Exp, reads=[psk(pS), ("bias", qi % 2, hh)],
                          writes=[("P", pbuf)], scale=0.125, bias=bias_sb[:, qi % 2, hh, kt:kt + 1])
                    c.mm(PS[pO][:, n0:TB], v_sb[:, kb, kt, hh, :], P_sb[:, pbuf, 0:N], start=(kt == 0),
                         stop=(kt == 4 * qb + 3), reads=[("v_sb", kb), ("P", pbuf)], writes=[psk(pO)], signal=True)

                def emit_fin_a(qi):
                    pA = 2 + 2 * (qi % 2)
                    pB = pA + 1
                    c.op("dve", lambda e: e.reciprocal(out=r_sb[64:128, :], in_=PS[pA][64:128, :]),
                         reads=[psk(pA)], writes=["r_hi"])
                    c.op("dve", lambda e: e.reciprocal(out=r_sb[0:64, :], in_=PS[pB][0:64, :]),
                         reads=[psk(pB)], writes=["r_lo"])

                def emit_fin_b(qi):
                    yb = qi % 2
                    pA = 2 + 2 * (qi % 2)
                    pB = pA + 1
                    c.act(rs_sb[0:64, :], r_sb[64:128, :], AF.Copy, reads=["r_hi"], writes=["rs_lo"])
                    c.act(rs_sb[64:128, :], r_sb[0:64, :], AF.Copy, reads=["r_lo"], writes=["rs_hi"])
                    c.op("dve", lambda e: e.tensor_tensor(out=y_sb[0:64, yb, :], in0=PS[pA][0:64, :],
                                                          in1=rs_sb[0:64, :], op=ALU.mult),
                         reads=[psk(pA), "rs_lo"], writes=[("y", yb)])
                    c.op("dve", lambda e: e.tensor_tensor(out=y_sb[64:128, yb, :], in0=PS[pB][64:128, :],
                                                          in1=rs_sb[64:128, :], op=ALU.mult),
                         reads=[psk(pB), "rs_hi"], writes=[("y", yb)])

                def emit_outproj(qi, fo):
                    hp, qb = seq[qi]
                    yb = qi % 2
                    c.mm(PS[7][:, :], woa_v[:, hp, fo * 128:(fo + 1) * 128], y_sb[:, yb, :], start=True, stop=True,
                         reads=["woa_sb", ("y", yb)], writes=[psk(7)], signal=True)
                    c.op("dve", lambda e: e.tensor_tensor(out=X[:, fo, blk(qb)], in0=PS[7][:, :],
                                                          in1=X[:, fo, blk(qb)], op=ALU.add),
                         reads=[psk(7), xk(fo, qb)], writes=[xk(fo, qb)])

                pending = []
                pfin = []
                nblk = len(blocks)
                FDEL = 14

                def do_fin_b(jnow):
                    _, fq = pfin.pop(0)
                    while pending and pending[0][1] <= fq - 2:
                        _, pq, pf = pending.pop(0)
                        emit_outproj(pq, pf)
                    emit_fin_b(fq)
                    for fo in range(8):
                        pending.append((jnow + 3 + 2 * fo, fq, fo))

                for i in range(nblk + LA):
                    if i < nblk:
                        ensure_setup(blocks[i][0])
                        emit_qk(i)
                    j = i - LA
                    if j >= 0:
                        emit_exp_pv(j)
                        if pfin and pfin[0][0] <= j:
                            do_fin_b(j)
                        while pending and pending[0][0] <= j:
                            _, pq, pf = pending.pop(0)
                            emit_outproj(pq, pf)
                        if blocks[j][5]:
                            qi = blocks[j][0]
                            while pfin:
                                do_fin_b(j)
                            emit_fin_a(qi)
                            pfin.append((j + FDEL, qi))
                while pfin:
                    do_fin_b(nblk)
                while pending:
                    _, pq, pf = pending.pop(0)
                    emit_outproj(pq, pf)
                c.barrier()

        for l in range(NL):
            ffn_phase(l, 0, l * PL + 0)
            m1_phase(l)
            m2_phase(l)
            ffn_phase(l, 1, l * PL + 16)

        c.barrier()
        if final:
            with contextlib.ExitStack() as ph:
                sq = sb(ph, "e_sq", [128, 2, TB], BF16)
                rs_tmp = sb(ph, "e_rs", [128, TB], F32)
                rstd = sb(ph, "e_rstd", [128, TB], F32)
                gb = NL * PL
                for tb in range(NTB):
                    rmsnorm_stats(tb, sq, rs_tmp, rstd, tb % 2)
                    for fc in range(8):
                        c.op("dve", lambda e: e.scalar_tensor_tensor(out=X[:, fc, blk(tb)], in0=X[:, fc, blk(tb)],
                                                                     scalar=cst[:, gb + fc:gb + fc + 1], in1=rstd[:, :],
                                                                     op0=ALU.mult, op1=ALU.mult),
                             reads=[xk(fc, tb), "rstd", "cst"], writes=[xk(fc, tb)])
                    c.dma("sp", oT.rearrange("(c p) t -> p c t", p=128)[:, :, blk(tb)], X[:, :, blk(tb)],
                          reads=[xk(fc, tb) for fc in range(8)], writes=["oT"], semkey=("Xout", tb))
                c.wait_all("sp", ["oT"])
        else:
            for fc in range(8):
                c.dma("sp", oT[fc * 128:(fc + 1) * 128, :], X[:, fc, :], reads=[xk(fc, tb) for tb in range(NTB)],
                      writes=["oT"], semkey=("Xout", fc))
            c.wait_all("sp", ["oT"])
        c.barrier()
        build_program.stats = (dict(c.n_ops), c.n_wait)
    return nc


def _prep_weights(inp, layers):
    NL = len(layers)
    f32 = np.float32
    L = list(layers)
    w_ffn_in = np.asarray(inp["w_ffn_in"], f32)[L]
    w_ffn_out = np.asarray(inp["w_ffn_out"], f32)[L]
    w_in = np.asarray(inp["w_in"], f32)[L]
    w_out = np.asarray(inp["w_out"], f32)[L]
    t = w_ffn_in.reshape(NL, 2, 8, 128, 2, NJ, 128)
    wi = np.ascontiguousarray(t.transpose(0, 1, 5, 3, 2, 4, 6)).reshape(NL, 2, NJ, 128, 2048)
    t = w_ffn_out.reshape(NL, 2, NJ, 128, 8, 128)
    wo = np.ascontiguousarray(t.transpose(0, 1, 4, 3, 2, 5)).reshape(NL, 2, 8, 128, 2816)
    colbase = [0, 128, 256, 384, 512, 640, 768, 896, 1544, 1672, 1800, 1928, 2056, 2184, 2312, 2440]
    chunks = np.stack([w_in[:, :, b:b + 128] for b in colbase], axis=1)
    t = chunks.reshape(NL, 8, 2, 8, 128, 128)
    wq = np.ascontiguousarray(t.transpose(0, 1, 4, 3, 2, 5)).reshape(NL, 8, 128, 2048)
    t = w_in[:, :, 1024:1536].reshape(NL, 8, 128, 512)
    wv = np.ascontiguousarray(t.transpose(0, 2, 1, 3)).reshape(NL, 128, 4096)
    t = w_in[:, :, 1536:1544].reshape(NL, 8, 128, 8)
    wf = np.ascontiguousarray(t.transpose(0, 2, 1, 3)).reshape(NL, 128, 64)
    t = w_out[:, 512:1024, :].reshape(NL, 4, 128, 1024)
    wor = np.ascontiguousarray(t.transpose(0, 2, 1, 3)).reshape(NL, 128, 4096)
    t = w_out[:, 0:512, :].reshape(NL, 4, 128, 1024)
    woa = np.ascontiguousarray(t.transpose(0, 2, 1, 3)).reshape(NL, 128, 4096)
    wa = np.asarray(inp["w_rg_a"], f32)[L]
    wx = np.asarray(inp["w_rg_x"], f32)[L]
    wbd = np.zeros((NL, 128, 4, 2, 128), f32)
    for cch in range(4):
        for half in range(2):
            rs = slice(half * 64, half * 64 + 64)
            wbd[:, rs, cch, 0, rs] = wa[:, 2 * cch + half]
            wbd[:, rs, cch, 1, rs] = wx[:, 2 * cch + half]
    wbd = wbd.reshape(NL, 128, 1024)
    cst = np.zeros((128, NL * PL + 8), f32)
    ng = np.asarray(inp["norm_g"], f32)[L]
    cw = np.asarray(inp["conv_w"], f32)[L]
    for li in range(NL):
        b = li * PL
        cst[:, b:b + 24] = ng[li].reshape(3, 8, 128).transpose(2, 0, 1).reshape(128, 24)
        cst[:, b + 24:b + 40] = cw[li].reshape(4, 4, 128).transpose(2, 0, 1).reshape(128, 16)
        cst[:, b + 40:b + 44] = np.asarray(inp["conv_b"], f32)[L[li]].reshape(4, 128).T
        cst[:, b + 44:b + 48] = np.asarray(inp["b_rg_a"], f32)[L[li]].reshape(4, 128).T
        cst[:, b + 48:b + 52] = np.asarray(inp["b_rg_x"], f32)[L[li]].reshape(4, 128).T
        cst[:, b + 52:b + 56] = np.asarray(inp["rg_lambda"], f32)[L[li]].reshape(4, 128).T
        cst[:, b + 56:b + 64] = np.asarray(inp["b_f"], f32)[L[li]][None, :]
    cst[:, NL * PL:NL * PL + 8] = np.asarray(inp["final_g"], f32).reshape(8, 128).T
    cmat = np.zeros((128, 384), f32)
    cmat[:, 0:128] = np.eye(128, dtype=f32)
    kk = np.arange(128)[:, None]
    qq = np.arange(128)[None, :]
    cmat[:, 128:256] = np.where(kk > qq, MASKVAL, 0.0)
    cmat[:, 256:384] = (kk <= qq).astype(f32)
    return dict(cst=cst, cmat=cmat, wi=wi, wo=wo, wq=wq, wv=wv, wf=wf, wor=wor, woa=woa, wbd=wbd)


FUSED = True
_PROGS = {}


def _prog(NL, final):
    key = (NL, final)
    if key not in _PROGS:
        _PROGS[key] = build_program(NL, final)
    return _PROGS[key]


def kernel(**inputs):
    x = np.asarray(inputs["x"], np.float32)
    xT = [np.ascontiguousarray(x[b].T) for b in range(NB)]
    if FUSED:
        groups = [list(range(DEPTH))]
    else:
        groups = [[l] for l in range(DEPTH)]
    for gi, layers in enumerate(groups):
        final = gi == len(groups) - 1
        w = _prep_weights(inputs, layers)
        nc = _prog(len(layers), final)
        in_maps = [dict(w, xT=xT[b]) for b in range(NB)]
        res = run_bass_kernel_spmd(nc, in_maps, core_ids=list(range(NB)))
        xT = [np.asarray(res.results[b]["oT"], np.float32) for b in range(NB)]
    out = np.stack([xT[b].T for b in range(NB)], axis=0)
    return np.ascontiguousarray(out.astype(np.float32))
```

```python
import contextlib
import math
import numpy as np
import concourse.bass as bass
import concourse.mybir as mybir
from concourse.bass_utils import run_bass_kernel_spmd

F32 = mybir.dt.float32
BF16 = mybir.dt.bfloat16
AF = mybir.ActivationFunctionType
ALU = mybir.AluOpType

D = 1024
S = 4096
NB = 8
DEPTH = 4
DFF = 2816
NJ = DFF // 128
TB = 512
NTB = S // TB
EPS = 1e-6
PL = 64
GELU_K = 0.044715
GELU_S = 2.0 * math.sqrt(2.0 / math.pi)
MASKVAL = -30000.0


class Ctx:
    COMPUTE = ("pe", "act", "dve", "pool")

    def __init__(self, nc, stack):
        self.nc = nc
        self.stack = stack
        self.eng = {"pe": nc.tensor, "act": nc.scalar, "dve": nc.vector, "pool": nc.gpsimd, "sp": nc.sync}
        self.sems = {}
        self.cnt = {}
        for e in self.COMPUTE:
            self.sems[e] = stack.enter_context(nc.semaphore("s_" + e))
            self.cnt[e] = 0
        self.waited = {e: {} for e in self.eng}
        self.tw = {}
        self.tr = {}
        self.n_wait = 0
        self.n_ops = {e: 0 for e in self.eng}

    def _deps(self, reads, writes):
        deps = {}
        for k in reads:
            for s, v in self.tw.get(k, {}).items():
                if deps.get(s, 0) < v:
                    deps[s] = v
        for k in writes:
            for s, v in self.tw.get(k, {}).items():
                if deps.get(s, 0) < v:
                    deps[s] = v
            for s, v in self.tr.get(k, {}).items():
                if deps.get(s, 0) < v:
                    deps[s] = v
        return deps

    def _emit_waits(self, ename, deps):
        E = self.eng[ename]
        wd = self.waited[ename]
        for s, v in deps.items():
            if s == "pe" and ename == "pe":
                continue
            if wd.get(s, 0) >= v:
                continue
            if s == "pe" and v > self.cnt["pe"]:
                raise RuntimeError("wait on an un-signalled PE op")
            E.wait_ge(self.sems[s], v)
            wd[s] = v
            self.n_wait += 1

    def _record(self, semkey, val, reads, writes):
        for k in reads:
            d = self.tr.setdefault(k, {})
            if d.get(semkey, 0) < val:
                d[semkey] = val
        for k in writes:
            d = self.tw.setdefault(k, {})
            if d.get(semkey, 0) < val:
                d[semkey] = val

    def op(self, ename, fn, reads=(), writes=(), signal=True):
        reads = list(reads) + ["*"]
        self._emit_waits(ename, self._deps(reads, writes))
        inst = fn(self.eng[ename])
        self.n_ops[ename] += 1
        if signal:
            self.cnt[ename] += 1
            inst.then_inc(self.sems[ename], 1)
            val = self.cnt[ename]
        else:
            assert ename == "pe"
            val = self.cnt[ename] + 1
        self._record(ename, val, reads, writes)
        return inst

    def dma(self, qname, out, in_, reads, writes, semkey, **kw):
        sk = ("dma", semkey)
        if sk not in self.sems:
            self.sems[sk] = self.stack.enter_context(self.nc.semaphore("d%d" % len(self.sems)))
            self.cnt[sk] = 0
        reads = list(reads) + ["*"]
        self._emit_waits(qname, self._deps(reads, writes))
        inst = self.eng[qname].dma_start(out=out, in_=in_, **kw)
        self.n_ops[qname] += 1
        self.cnt[sk] += 16
        inst.then_inc(self.sems[sk], 16)
        self._record(sk, self.cnt[sk], reads, writes)
        return inst

    def wait_all(self, ename, keys):
        deps = {}
        for k in keys:
            for d in (self.tw.get(k, {}), self.tr.get(k, {})):
                for s, v in d.items():
                    if deps.get(s, 0) < v:
                        deps[s] = v
        if ename == "pe":
            deps.pop("pe", None)
        self._emit_waits(ename, deps)

    def barrier(self):
        for e in ("pe", "act", "dve", "pool", "sp"):
            self.wait_all(e, ["*"])

    def mm(self, out, lhsT, rhs, start, stop, reads, writes, signal=False):
        return self.op("pe", lambda e: e.matmul(out, lhsT=lhsT, rhs=rhs, start=start, stop=stop),
                       reads=reads, writes=writes, signal=signal)

    def act(self, out, in_, func, reads, writes, **kw):
        return self.op("act", lambda e: e.activation(out=out, in_=in_, func=func, **kw), reads=reads, writes=writes)


class WStream:
    def __init__(self, c, name, buf, nslots, items):
        self.c, self.name, self.buf, self.nslots, self.items = c, name, buf, nslots, items
        self.issued = 0
        self.consumed = 0
        for _ in range(nslots):
            self._issue()

    def _issue(self):
        if self.issued < len(self.items):
            src, keys = self.items[self.issued]
            s = self.issued % self.nslots
            self.c.dma("sp", self.buf[:, s, :], src, reads=keys, writes=[(self.name, s)], semkey=(self.name, s))
            self.issued += 1

    def next(self):
        s = self.consumed % self.nslots
        self.consumed += 1
        return s, (self.name, s)

    def release(self, n=1):
        for _ in range(n):
            self._issue()


def build_program(NL, final):
    nc = bass.Bass("TRN2", target_bir_lowering=False)
    NC = NL * PL + 8

    def din(name, shape, dt=F32):
        return nc.dram_tensor(name, shape, dt, kind="ExternalInput").ap()

    def dint(name, shape, dt=BF16):
        return nc.dram_tensor(name, shape, dt, kind="Internal").ap()

    xT = din("xT", [D, S])
    cst_d = din("cst", [128, NC])
    cmat_d = din("cmat", [128, 384])
    wi_d = din("wi", [NL, 2, NJ, 128, 2048])
    wo_d = din("wo", [NL, 2, 8, 128, 2816])
    wq_d = din("wq", [NL, 8, 128, 2048])
    wv_d = din("wv", [NL, 128, 4096])
    wf_d = din("wf", [NL, 128, 64])
    wor_d = din("wor", [NL, 128, 4096])
    woa_d = din("woa", [NL, 128, 4096])
    wbd_d = din("wbd", [NL, 128, 1024])
    oT = nc.dram_tensor("oT", [D, S], F32, kind="ExternalOutput").ap()

    wi_b = dint("wi_b", [NL, 2, NJ, 128, 2048])
    wo_b = dint("wo_b", [NL, 2, 8, 128, 2816])
    wq_b = dint("wq_b", [NL, 8, 128, 2048])
    wv_b = dint("wv_b", [NL, 128, 4096])
    wf_b = dint("wf_b", [NL, 128, 64])
    wor_b = dint("wor_b", [NL, 128, 4096])
    woa_b = dint("woa_b", [NL, 128, 4096])
    wbd_b = dint("wbd_b", [NL, 128, 1024])
    qT_s = dint("qT_s", [512, S])
    kT_s = dint("kT_s", [512, S])
    v_s = dint("v_s", [4, 128, 32, 2, 64])

    with contextlib.ExitStack() as st:
        c = Ctx(nc, st)

        uid = [0]

        def sb(stack, name, shape, dt):
            uid[0] += 1
            return stack.enter_context(nc.sbuf_tensor("%s_%d" % (name, uid[0]), shape, dt))

        X = sb(st, "X", [128, 8, S], F32)
        cst = sb(st, "cst_sb", [128, NC], F32)
        tri_sb = sb(st, "tri_sb", [128, 128], F32)
        ident_bf = sb(st, "ident_bf", [128, 128], BF16)
        mask_bf = sb(st, "mask_bf", [128, 128], BF16)
        ones_bf = sb(st, "ones_bf", [128, 128], BF16)
        ones_f = sb(st, "ones_f", [128, 128], F32)
        spc = sb(st, "spc", [128, NL * 8], F32)
        clog = sb(st, "clog", [128, 32, 8], F32)
        cref = sb(st, "cref", [128, NTB, 8], F32)
        PS = [st.enter_context(nc.psum_tensor("ps%d" % b, [128, 512], F32)) for b in range(8)]

        def psk(b):
            return ("ps", b)

        def xk(fc, tb):
            return ("X", fc, tb)

        def blk(tb):
            return slice(tb * TB, (tb + 1) * TB)

        def cast(dst, src, key, tag, maxrows=1024):
            rows = dst.shape[0]
            r0 = 0
            i = 0
            while r0 < rows:
                r1 = min(rows, r0 + maxrows)
                c.dma("pool", dst[r0:r1, :], src[r0:r1, :], reads=[], writes=[key], semkey=("cast", tag, i))
                r0 = r1
                i += 1

        def cast_ffn(l, i):
            for half in range(2):
                js = slice(half * 11, half * 11 + 11)
                cast(wi_b[l, i, js].rearrange("j p e -> (j p) e"), wi_d[l, i, js].rearrange("j p e -> (j p) e"),
                     ("wi_b", l, i, half), ("wi", i, half), maxrows=704)
            for half in range(2):
                fs = slice(half * 4, half * 4 + 4)
                cast(wo_b[l, i, fs].rearrange("f p (a e) -> (f p a) e", a=2),
                     wo_d[l, i, fs].rearrange("f p (a e) -> (f p a) e", a=2),
                     ("wo_b", l, i, half), ("wo", i, half), maxrows=1024)

        def cast_mix(l):
            k = ("mixw", l)
            cast(wq_b[l].rearrange("t p e -> (t p) e"), wq_d[l].rearrange("t p e -> (t p) e"), k, ("wq",), maxrows=512)
            cast(wv_b[l].rearrange("p (a e) -> (p a) e", a=2), wv_d[l].rearrange("p (a e) -> (p a) e", a=2), k, ("wv",))
            cast(wf_b[l], wf_d[l], k, ("wf",))
            cast(wbd_b[l], wbd_d[l], k, ("wbd",))
            cast(wor_b[l].rearrange("p (a e) -> (p a) e", a=2), wor_d[l].rearrange("p (a e) -> (p a) e", a=2), k, ("wor",))
            cast(woa_b[l].rearrange("p (a e) -> (p a) e", a=2), woa_d[l].rearrange("p (a e) -> (p a) e", a=2), k, ("woa",))

        def cast_layer(l):
            cast_ffn(l, 0)
            cast_mix(l)
            cast_ffn(l, 1)

        cast_layer(0)
        c.dma("sp", cst[:, :], cst_d[:, :], reads=[], writes=["cst"], semkey="cst")
        c.dma("sp", tri_sb[:, :], cmat_d[:, 256:384], reads=[], writes=["cmat"], semkey="cmat")
        for fc in range(8):
            c.dma("sp", X[:, fc, :], xT[fc * 128:(fc + 1) * 128, :], reads=[],
                  writes=[xk(fc, tb) for tb in range(NTB)], semkey=("Xld", fc))
        c.op("dve", lambda e: e.memset(ones_bf[:, :], 1.0), writes=["ones_bf"])
        c.op("dve", lambda e: e.memset(ones_f[:, :], 1.0), writes=["ones_f"])
        with contextlib.ExitStack() as ph0:
            cm_tmp = sb(ph0, "cm_tmp", [128, 256], F32)
            c.dma("sp", cm_tmp[:, :], cmat_d[:, 0:256], reads=[], writes=["cm_tmp"], semkey="cm_tmp")
            c.act(ident_bf[:, :], cm_tmp[:, 0:128], AF.Copy, reads=["cm_tmp"], writes=["ident_bf"])
            c.act(mask_bf[:, :], cm_tmp[:, 128:256], AF.Copy, reads=["cm_tmp"], writes=["mask_bf"])
        tri_f = tri_sb[:, :]
        for l in range(NL):
            lam = cst[:, l * PL + 52:l * PL + 56]
            c.act(spc[:, l * 8:l * 8 + 4], lam, AF.Sigmoid, reads=["cst"], writes=["spc"])
            c.act(spc[:, l * 8:l * 8 + 4], spc[:, l * 8:l * 8 + 4], AF.Ln, reads=["spc"], writes=["spc"])
            c.op("dve", lambda e: e.tensor_scalar(out=spc[:, l * 8 + 4:l * 8 + 8], in0=spc[:, l * 8:l * 8 + 4],
                                                  scalar1=16.0, scalar2=None, op0=ALU.mult),
                 reads=["spc"], writes=["spc"])
            c.op("dve", lambda e: e.tensor_scalar(out=spc[:, l * 8:l * 8 + 4], in0=spc[:, l * 8:l * 8 + 4],
                                                  scalar1=8.0, scalar2=None, op0=ALU.mult),
                 reads=["spc"], writes=["spc"])

        def rmsnorm_stats(tb, sq, rs_tmp, rstd, ps_stat):
            for fc in range(8):
                b = fc % 2
                c.act(sq[:, b, :], X[:, fc, blk(tb)], AF.Square, reads=[xk(fc, tb)], writes=[("sq", b)])
                c.mm(PS[ps_stat][:, :], ones_bf[:, :], sq[:, b, :], start=(fc == 0), stop=(fc == 7),
                     reads=[("sq", b), "ones_bf"], writes=[psk(ps_stat)], signal=True)
            c.act(rs_tmp[:, :], PS[ps_stat][:, :], AF.Sqrt, reads=[psk(ps_stat)], writes=["rs_tmp"],
                  scale=1.0 / D, bias=EPS)
            c.op("dve", lambda e: e.reciprocal(out=rstd[:, :], in_=rs_tmp[:, :]), reads=["rs_tmp"], writes=["rstd"])

        def rmsnorm_apply(tb, gbase, rstd, xn):
            for fc in range(8):
                c.op("dve", lambda e: e.scalar_tensor_tensor(out=xn[:, fc, :], in0=X[:, fc, blk(tb)],
                                                             scalar=cst[:, gbase + fc:gbase + fc + 1], in1=rstd[:, :],
                                                             op0=ALU.mult, op1=ALU.mult),
                     reads=[xk(fc, tb), "rstd", "cst"], writes=[("xn", fc)])

        def ffn_phase(l, i, gbase):
            if not (l == 0 and i == 0):
                c.barrier()
            with contextlib.ExitStack() as ph:
                xn = sb(ph, "f_xn", [128, 8, TB], BF16)
                sq = sb(ph, "f_sq", [128, 2, TB], BF16)
                rs_tmp = sb(ph, "f_rs", [128, TB], F32)
                rstd = sb(ph, "f_rstd", [128, TB], F32)
                h = sb(ph, "f_h", [128, NJ, TB], BF16)
                sg = sb(ph, "f_sg", [128, 2, TB], F32)
                wib = sb(ph, "f_wi", [128, 4, 2048], BF16)
                wob = sb(ph, "f_wo", [128, 3, 2816], BF16)
                wi_items = [(wi_b[l, i, j], [("wi_b", l, i, j // 11)]) for _ in range(NTB) for j in range(NJ)]
                wo_items = [(wo_b[l, i, fo], [("wo_b", l, i, fo // 4)]) for _ in range(NTB) for fo in range(8)]
                wis = WStream(c, "wi_s", wib, 4, wi_items)
                wos = WStream(c, "wo_s", wob, 3, wo_items)
                for tb in range(NTB):
                    rmsnorm_stats(tb, sq, rs_tmp, rstd, 6)
                    rmsnorm_apply(tb, gbase, rstd, xn)
                    for j in range(NJ):
                        s, skey = wis.next()
                        wt = wib[:, s, :].rearrange("p (k g m) -> p k g m", k=8, g=2)
                        pg, pu = j % 2, 2 + j % 2
                        for gu, pb in ((0, pg), (1, pu)):
                            for kc in range(8):
                                c.mm(PS[pb][:, :], wt[:, kc, gu, :], xn[:, kc, :], start=(kc == 0), stop=(kc == 7),
                                     reads=[skey, ("xn", kc)], writes=[psk(pb)], signal=(kc == 7))
                        wis.release()
                        c.act(sg[:, j % 2, :], PS[pg][:, :], AF.Silu, reads=[psk(pg)], writes=[("sg", j % 2)])
                        c.op("dve", lambda e: e.tensor_tensor(out=h[:, j, :], in0=PS[pu][:, :], in1=sg[:, j % 2, :],
                                                              op=ALU.mult),
                             reads=[psk(pu), ("sg", j % 2)], writes=[("h", j)])
                    for fo in range(8):
                        s, skey = wos.next()
                        wt = wob[:, s, :].rearrange("p (j m) -> p j m", j=NJ)
                        pb = 4 + fo % 2
                        for j in range(NJ):
                            c.mm(PS[pb][:, :], wt[:, j, :], h[:, j, :], start=(j == 0), stop=(j == NJ - 1),
                                 reads=[skey, ("h", j)], writes=[psk(pb)], signal=(j == NJ - 1))
                        wos.release()
                        c.op("dve", lambda e: e.scalar_tensor_tensor(out=X[:, fo, blk(tb)], in0=PS[pb][:, :], scalar=0.5,
                                                                     in1=X[:, fo, blk(tb)], op0=ALU.mult, op1=ALU.add),
                             reads=[psk(pb), xk(fo, tb)], writes=[xk(fo, tb)])
                c.barrier()

        def m1_phase(l):
            c.barrier()
            cb = l * PL
            with contextlib.ExitStack() as ph:
                xn = sb(ph, "m_xn", [128, 8, TB], BF16)
                sq = sb(ph, "m_sq", [128, 1, TB], BF16)
                rstd = sb(ph, "m_rstd", [128, TB], F32)
                wsb = sb(ph, "m_ws", [128, 3, 2048], BF16)
                wf_sb = sb(ph, "m_wf", [128, 64], BF16)
                wbd_sb = sb(ph, "m_wbd", [128, 1024], BF16)
                stq = sb(ph, "m_stq", [128, 4, TB], BF16)
                stk = sb(ph, "m_stk", [128, 4, TB], BF16)
                stv = sb(ph, "m_stv", [128, 4, 8, 64], BF16)
                xr_sb = sb(ph, "m_xr", [128, 4, TB + 3], F32)
                hcar = sb(ph, "m_hcar", [128, 4], F32)
                carry = sb(ph, "m_carry", [128, 8], F32)
                fb = sb(ph, "m_fb", [128, 32], F32)
                bfb = sb(ph, "m_bfb", [128, 32], F32)
                spx = sb(ph, "m_spx", [128, 16], F32)
                acc = sb(ph, "m_acc", [128, 2, TB], F32)
                xcb = sb(ph, "m_xcb", [128, 2, TB], BF16)
                tr = sb(ph, "m_tr", [128, 2, TB], F32)
                ti = sb(ph, "m_ti", [128, 2, TB], F32)
                ta = sb(ph, "m_ta", [128, 2, TB], F32)
                tth = sb(ph, "m_tth", [128, 2, TB], F32)
                tx = sb(ph, "m_tx", [128, 2, TB], F32)
                yrec = sb(ph, "m_yrec", [128, 4, TB], BF16)
                mk = ("mixw", l)
                c.dma("sp", wf_sb[:, :], wf_b[l], reads=[mk], writes=["wf_sb"], semkey="wf_sb")
                c.dma("sp", wbd_sb[:, :], wbd_b[l], reads=[mk], writes=["wbd_sb"], semkey="wbd_sb")
                items = []
                for _ in range(NTB):
                    for t in (4, 5, 0, 1, 2, 3, 6, 7):
                        items.append((wq_b[l, t], [mk]))
                    items.append((wv_b[l, :, 0:2048], [mk]))
                    items.append((wv_b[l, :, 2048:4096], [mk]))
                    items.append((wor_b[l, :, 0:2048], [mk]))
                    items.append((wor_b[l, :, 2048:4096], [mk]))
                ws = WStream(c, "m_ws", wsb, 3, items)
                c.op("dve", lambda e: e.memset(xr_sb[:, :, :], 0.0), writes=[("xr", k) for k in range(4)])
                c.op("dve", lambda e: e.memset(hcar[:, :], 0.0), writes=["hcar"])
                c.op("dve", lambda e: e.memset(carry[:, :], 0.0), writes=["carry"])
                for tt in range(4):
                    c.op("dve", lambda e: e.tensor_copy(out=bfb[:, tt * 8:(tt + 1) * 8], in_=cst[:, cb + 56:cb + 64]),
                         reads=["cst"], writes=["bfb"])
                c.op("dve", lambda e: e.tensor_scalar(out=spx[:, 0:4], in0=spc[:, l * 8:l * 8 + 4], scalar1=0.5,
                                                      scalar2=None, op0=ALU.mult), reads=["spc"], writes=["spx"])
                c.op("dve", lambda e: e.tensor_copy(out=spx[:, 4:8], in_=spc[:, l * 8:l * 8 + 4]),
                     reads=["spc"], writes=["spx"])
                c.op("dve", lambda e: e.tensor_scalar(out=spx[:, 8:12], in0=cst[:, cb + 44:cb + 48], scalar1=0.5,
                                                      scalar2=None, op0=ALU.mult), reads=["cst"], writes=["spx"])
                c.op("dve", lambda e: e.tensor_scalar(out=spx[:, 12:16], in0=cst[:, cb + 48:cb + 52], scalar1=0.5,
                                                      scalar2=None, op0=ALU.mult), reads=["cst"], writes=["spx"])
                wbd_v = wbd_sb[:, :].rearrange("p (c g m) -> p c g m", c=4, g=2)
                wf_v = wf_sb[:, :].rearrange("p (k h) -> p k h", k=8)
                rot = [0]

                def nextbank():
                    b = rot[0] % 4
                    rot[0] += 1
                    return b

                def col(idx):
                    return cst[:, idx:idx + 1]

                def sx(j):
                    return spx[:, j:j + 1]

                def sqbuf(fc):
                    b = fc % 5
                    if b < 4:
                        return yrec[:, b, :], ("yrec", b)
                    return sq[:, 0, :], ("sq", 0)

                def stats_sq(tb, fc):
                    buf, key = sqbuf(fc)
                    c.act(buf, X[:, fc, blk(tb)], AF.Square, reads=[xk(fc, tb)], writes=[key])

                def stats_mm(tb, fc):
                    buf, key = sqbuf(fc)
                    c.mm(PS[6][:, :], ones_bf[:, :], buf, start=(fc == 0), stop=(fc == 7),
                         reads=[key, "ones_bf"], writes=[psk(6)], signal=True)

                def stats(tb):
                    for fc in range(8):
                        stats_sq(tb, fc)
                        stats_mm(tb, fc)

                def stats_fin(tb):
                    c.act(rstd[:, :], PS[6][:, :], AF.Sqrt, reads=[psk(6)], writes=["rstd"], scale=1.0 / D, bias=EPS)
                    c.op("dve", lambda e: e.reciprocal(out=rstd[:, :], in_=rstd[:, :]), reads=["rstd"], writes=["rstd"])

                def inproj_chunk(wt, skey, cc, evac):
                    pb = nextbank()
                    for kc in range(8):
                        c.mm(PS[pb][:, :], wt[:, kc, cc, :], xn[:, kc, :], start=(kc == 0), stop=(kc == 7),
                             reads=[skey, ("xn", kc)], writes=[psk(pb)], signal=(kc == 7))
                    evac(pb)

                def inproj_tile(evacs):
                    s, skey = ws.next()
                    wt = wsb[:, s, :].rearrange("p (k g m) -> p k g m", k=8, g=2)
                    for cc in range(2):
                        inproj_chunk(wt, skey, cc, evacs[cc])
                    ws.release()

                def conv(k):
                    kk = k % 2
                    xk_ = ("xr", k)
                    c.op("dve", lambda e: e.tensor_scalar(out=acc[:, kk, :], in0=xr_sb[:, k, 0:TB],
                                                          scalar1=col(cb + 24 + k), scalar2=col(cb + 40 + k),
                                                          op0=ALU.mult, op1=ALU.add),
                         reads=[xk_, "cst"], writes=[("acc", kk)])
                    for tap in range(1, 4):
                        c.op("dve", lambda e: e.scalar_tensor_tensor(out=acc[:, kk, :], in0=xr_sb[:, k, tap:tap + TB],
                                                                     scalar=col(cb + 24 + tap * 4 + k), in1=acc[:, kk, :],
                                                                     op0=ALU.mult, op1=ALU.add),
                             reads=[xk_, "cst", ("acc", kk)], writes=[("acc", kk)])
                    c.act(xr_sb[:, k, 0:3], xr_sb[:, k, TB:TB + 3], AF.Copy, reads=[xk_], writes=[xk_])
                    c.act(xcb[:, kk, :], acc[:, kk, :], AF.Copy, reads=[("acc", kk)], writes=[("xcb", kk)])

                def gates(k):
                    kk = k % 2
                    pa = nextbank()
                    c.mm(PS[pa][:, :], wbd_v[:, k, 0, :], xcb[:, kk, :], start=True, stop=True,
                         reads=["wbd_sb", ("xcb", kk)], writes=[psk(pa)], signal=True)
                    px = nextbank()
                    c.mm(PS[px][:, :], wbd_v[:, k, 1, :], xcb[:, kk, :], start=True, stop=True,
                         reads=["wbd_sb", ("xcb", kk)], writes=[psk(px)], signal=True)
                    c.act(tr[:, kk, :], PS[pa][:, :], AF.Tanh, reads=[psk(pa), "spx"], writes=[("tr", kk)],
                          scale=0.5, bias=sx(8 + k))
                    c.act(ti[:, kk, :], PS[px][:, :], AF.Tanh, reads=[psk(px), "spx"], writes=[("ti", kk)],
                          scale=0.5, bias=sx(12 + k))

                def gate_funcs(k):
                    kk = k % 2
                    c.act(ta[:, kk, :], tr[:, kk, :], AF.Exp, reads=[("tr", kk), "spx"], writes=[("ta", kk)],
                          scale=sx(k), bias=sx(k))
                    c.act(tth[:, kk, :], tr[:, kk, :], AF.Tanh, reads=[("tr", kk), "spx"], writes=[("tth", kk)],
                          scale=sx(k), bias=sx(k))
                    c.act(tr[:, kk, :], tr[:, kk, :], AF.Exp, reads=[("tr", kk), "spx"], writes=[("tr", kk)],
                          scale=sx(4 + k), bias=sx(4 + k))
                    c.op("dve", lambda e: e.scalar_tensor_tensor(out=tth[:, kk, :], in0=tr[:, kk, :], scalar=1.0,
                                                                 in1=tth[:, kk, :], op0=ALU.add, op1=ALU.mult),
                         reads=[("tr", kk), ("tth", kk)], writes=[("tth", kk)])
                    c.op("dve", lambda e: e.scalar_tensor_tensor(out=ti[:, kk, :], in0=ti[:, kk, :], scalar=1.0,
                                                                 in1=acc[:, kk, :], op0=ALU.add, op1=ALU.mult),
                         reads=[("ti", kk), ("acc", kk)], writes=[("ti", kk)])

                def gelu_pre(k, pb):
                    kk = k % 2
                    c.act(tx[:, kk, :], PS[pb][:, :], AF.Square, reads=[psk(pb)], writes=[("tx", kk)],
                          scale=math.sqrt(GELU_K))
                    c.op("dve", lambda e: e.scalar_tensor_tensor(out=tx[:, kk, :], in0=tx[:, kk, :], scalar=1.0,
                                                                 in1=PS[pb][:, :], op0=ALU.add, op1=ALU.mult),
                         reads=[("tx", kk), psk(pb)], writes=[("tx", kk)])
                    c.act(tx[:, kk, :], tx[:, kk, :], AF.Tanh, reads=[("tx", kk)], writes=[("tx", kk)],
                          scale=0.5 * GELU_S)
                    c.op("dve", lambda e: e.scalar_tensor_tensor(out=tx[:, kk, :], in0=tx[:, kk, :], scalar=1.0,
                                                                 in1=PS[pb][:, :], op0=ALU.add, op1=ALU.mult),
                         reads=[("tx", kk), psk(pb)], writes=[("tx", kk)])

                def rec_tail_sqrt(k):
                    kk = k % 2
                    c.act(tth[:, kk, :], tth[:, kk, :], AF.Sqrt, reads=[("tth", kk)], writes=[("tth", kk)],
                          scale=-1.0 / 16.0)

                def rec_tail(k):
                    kk = k % 2
                    c.op("dve", lambda e: e.tensor_tensor(out=ti[:, kk, :], in0=ti[:, kk, :], in1=tth[:, kk, :],
                                                          op=ALU.mult),
                         reads=[("ti", kk), ("tth", kk)], writes=[("ti", kk)])
                    c.op("dve", lambda e: e.tensor_tensor_scan(out=tr[:, kk, :], data0=ta[:, kk, :], data1=ti[:, kk, :],
                                                               initial=hcar[:, k:k + 1], op0=ALU.mult, op1=ALU.add),
                         reads=[("ta", kk), ("ti", kk), "hcar", ("tr", kk)], writes=[("tr", kk)])
                    c.act(hcar[:, k:k + 1], tr[:, kk, TB - 1:TB], AF.Copy, reads=[("tr", kk)], writes=["hcar"])
                    c.op("dve", lambda e: e.tensor_tensor(out=yrec[:, k, :], in0=tx[:, kk, :], in1=tr[:, kk, :],
                                                          op=ALU.mult),
                         reads=[("tx", kk), ("tr", kk)], writes=[("yrec", k)])

                def ev_q(ch):
                    return lambda pb: c.act(stq[:, ch, :], PS[pb][:, :], AF.Copy, reads=[psk(pb)], writes=["stq"])

                def ev_k(ch):
                    return lambda pb: c.act(stk[:, ch, :], PS[pb][:, :], AF.Copy, reads=[psk(pb)], writes=["stk"])

                def ev_xr(k):
                    return lambda pb: c.act(xr_sb[:, k, 3:TB + 3], PS[pb][:, :], AF.Copy, reads=[psk(pb)],
                                            writes=[("xr", k)])

                def ev_gr(k):
                    return lambda pb: gelu_pre(k, pb)

                stats(0)
                stats_fin(0)
                rmsnorm_apply(0, cb + 8, rstd, xn)
                for tb in range(NTB):
                    inproj_tile([ev_xr(0), ev_xr(1)])
                    conv(0)
                    conv(1)
                    inproj_tile([ev_xr(2), ev_xr(3)])
                    if tb + 1 < NTB:
                        for fc in range(5):
                            stats_sq(tb + 1, fc)
                    inproj_tile([ev_q(0), ev_q(1)])
                    inproj_tile([ev_q(2), ev_q(3)])
                    if tb + 1 < NTB:
                        for fc in range(8):
                            if fc >= 5:
                                stats_sq(tb + 1, fc)
                            stats_mm(tb + 1, fc)
                    gates(0)
                    gates(1)
                    gate_funcs(0)
                    gate_funcs(1)
                    conv(2)
                    conv(3)
                    inproj_tile([ev_k(0), ev_k(1)])
                    inproj_tile([ev_k(2), ev_k(3)])
                    inproj_tile([ev_gr(0), ev_gr(1)])
                    rec_tail_sqrt(0)
                    rec_tail_sqrt(1)
                    rec_tail(0)
                    rec_tail(1)
                    gates(2)
                    gates(3)
                    gate_funcs(2)
                    gate_funcs(3)
                    inproj_tile([ev_gr(2), ev_gr(3)])
                    sA, kA = ws.next()
                    sB, kB = ws.next()
                    for tt in range(4):
                        pb = 4 + tt % 2
                        tok = slice(tt * 128, (tt + 1) * 128)
                        for kc in range(8):
                            sl, kk_ = (sA, kA) if kc < 4 else (sB, kB)
                            wv_t = wsb[:, sl, :].rearrange("p (k n) -> p k n", k=4)
                            c.mm(PS[pb][:, :], xn[:, kc, tok], wv_t[:, kc % 4, :], start=(kc == 0), stop=(kc == 7),
                                 reads=[kk_, ("xn", kc)], writes=[psk(pb)], signal=(kc == 7))
                        c.act(stv[:, tt, :, :], PS[pb][:, :].rearrange("p (h d) -> p h d", h=8), AF.Copy,
                              reads=[psk(pb)], writes=["stv"])
                        for kc in range(8):
                            c.mm(PS[7][:, tt * 8:(tt + 1) * 8], xn[:, kc, tok], wf_v[:, kc, :], start=(kc == 0),
                                 stop=(kc == 7), reads=["wf_sb", ("xn", kc)], writes=[psk(7)], signal=(kc == 7))
                    ws.release(2)
                    rec_tail_sqrt(2)
                    rec_tail_sqrt(3)
                    rec_tail(2)
                    rec_tail(3)
                    if tb + 1 < NTB:
                        stats_fin(tb + 1)
                        rmsnorm_apply(tb + 1, cb + 8, rstd, xn)
                    sA, kA = ws.next()
                    sB, kB = ws.next()
                    for fo in range(8):
                        pb = 4 + fo % 2
                        for kc in range(4):
                            sl, kk_ = (sA, kA) if kc < 2 else (sB, kB)
                            wor_t = wsb[:, sl, :].rearrange("p (k f) -> p k f", k=2)
                            c.mm(PS[pb][:, :], wor_t[:, kc % 2, fo * 128:(fo + 1) * 128], yrec[:, kc, :],
                                 start=(kc == 0), stop=(kc == 3), reads=[kk_, ("yrec", kc)], writes=[psk(pb)],
                                 signal=(kc == 3))
                        c.op("dve", lambda e: e.tensor_tensor(out=X[:, fo, blk(tb)], in0=PS[pb][:, :],
                                                              in1=X[:, fo, blk(tb)], op=ALU.add),
                             reads=[psk(pb), xk(fo, tb)], writes=[xk(fo, tb)])
                    ws.release(2)
                    c.op("dve", lambda e: e.tensor_tensor(out=fb[:, :], in0=PS[7][:, 0:32], in1=bfb[:, :], op=ALU.add),
                         reads=[psk(7), "bfb"], writes=["fb"])
                    c.act(fb[:, :], fb[:, :], AF.Tanh, reads=["fb"], writes=["fb"], scale=0.5)
                    c.act(fb[:, :], fb[:, :], AF.Ln, reads=["fb"], writes=["fb"], scale=0.5, bias=0.5)
                    c.mm(PS[7][:, 32:64], tri_f, fb[:, :], start=True, stop=True, reads=["cmat", "fb"],
                         writes=[psk(7)], signal=True)
                    c.mm(PS[7][:, 64:96], ones_f[:, :], fb[:, :], start=True, stop=True, reads=["ones_f", "fb"],
                         writes=[psk(7)], signal=True)
                    for tt in range(4):
                        n = tb * 4 + tt
                        c.op("dve", lambda e: e.tensor_tensor(out=clog[:, n, :], in0=PS[7][:, 32 + tt * 8:40 + tt * 8],
                                                              in1=carry[:, :], op=ALU.add),
                             reads=[psk(7), "carry"], writes=["clog"])
                        c.op("dve", lambda e: e.tensor_tensor(out=carry[:, :], in0=PS[7][:, 64 + tt * 8:72 + tt * 8],
                                                              in1=carry[:, :], op=ALU.add),
                             reads=[psk(7), "carry"], writes=["carry"])
                        if tt == 1:
                            c.op("dve", lambda e: e.tensor_copy(out=cref[:, tb, :], in_=carry[:, :]),
                                 reads=["carry"], writes=["cref"])
                    c.dma("sp", qT_s.rearrange("(c p) t -> p c t", p=128)[:, :, blk(tb)], stq[:, :, :],
                          reads=["stq"], writes=[("qT_s", p) for p in range(4)], semkey="stq")
                    c.dma("sp", kT_s.rearrange("(c p) t -> p c t", p=128)[:, :, blk(tb)], stk[:, :, :],
                          reads=["stk"], writes=[("kT_s", p) for p in range(4)], semkey="stk")
                    for p in range(4):
                        c.dma("sp", v_s[p, :, tb * 4:(tb + 1) * 4, :, :], stv[:, :, 2 * p:2 * p + 2, :],
                              reads=["stv"], writes=[("v_s", p)], semkey=("stv", p))
                c.barrier()

        def m2_phase(l):
            c.barrier()
            if l + 1 < NL:
                cast_layer(l + 1)
            mk = ("mixw", l)
            with contextlib.ExitStack() as ph:
                kT_sb = sb(ph, "a_kT", [128, 2, S], BF16)
                v_sb = sb(ph, "a_v", [128, 2, 32, 2, 128], BF16)
                q_sb = sb(ph, "a_q", [128, 2, 2, TB], BF16)
                P_sb = sb(ph, "a_P", [128, 4, TB], BF16)
                bias_sb = sb(ph, "a_bias", [128, 2, 2, 32], F32)
                r_sb = sb(ph, "a_r", [128, TB], F32)
                rs_sb = sb(ph, "a_rs", [128, TB], F32)
                y_sb = sb(ph, "a_y", [128, 2, TB], BF16)
                woa_sb = sb(ph, "a_woa", [128, 4096], BF16)
                woa_v = woa_sb[:, :].rearrange("p (h f) -> p h f", h=4)
                c.dma("sp", woa_sb[:, :], woa_b[l], reads=[mk], writes=["woa_sb"], semkey="woa_sb")
                c.op("dve", lambda e: e.memset(v_sb[:, :, :, :, :], 1.0), writes=[("v_sb", 0), ("v_sb", 1)])
                c.op("dve", lambda e: e.memset(q_sb[:, :, :, :], 0.0), writes=[("q_sb", 0), ("q_sb", 1)])

                def load_pair(hp):
                    b = hp % 2
                    c.dma("sp", kT_sb[:, b, :], kT_s[hp * 128:(hp + 1) * 128, :], reads=[("kT_s", hp)],
                          writes=[("kT_sb", b)], semkey=("kT_sb", b))
                    c.dma("sp", v_sb[:, b, :, 0, 0:64], v_s[hp, :, :, 0, :], reads=[("v_s", hp)],
                          writes=[("v_sb", b)], semkey=("v_sb", b))
                    c.dma("sp", v_sb[:, b, :, 1, 64:128], v_s[hp, :, :, 1, :], reads=[("v_s", hp)],
                          writes=[("v_sb", b)], semkey=("v_sb", b))

                def load_q(qi):
                    hp, qb = seq[qi]
                    b = qi % 2
                    r0 = hp * 128
                    c.dma("sp", q_sb[0:64, b, 0, :], qT_s[r0:r0 + 64, blk(qb)], reads=[("qT_s", hp)],
                          writes=[("q_sb", b)], semkey=("q_sb", b))
                    c.dma("sp", q_sb[64:128, b, 1, :], qT_s[r0 + 64:r0 + 128, blk(qb)], reads=[("qT_s", hp)],
                          writes=[("q_sb", b)], semkey=("q_sb", b))

                seq = [(hp, qb) for hp in range(4) for qb in range(NTB)]
                blocks = []
                for qi, (hp, qb) in enumerate(seq):
                    nkt = 4 * qb + 4
                    for kt in range(nkt):
                        for hh in range(2):
                            blocks.append((qi, hp, qb, kt, hh, kt == nkt - 1 and hh == 1))
                SB = [0, 1, 6]
                LA = 2
                setup_done = [-1]

                def ensure_setup(qi):
                    while setup_done[0] < qi:
                        setup_done[0] += 1
                        q2 = setup_done[0]
                        hp, qb = seq[q2]
                        if q2 == 0:
                            load_pair(0)
                            load_q(0)
                        if qb == 1 and hp + 1 < 4:
                            load_pair(hp + 1)
                        if q2 + 1 < len(seq):
                            load_q(q2 + 1)
                        for hh in range(2):
                            hd = 2 * hp + hh
                            c.op("dve", lambda e: e.tensor_scalar(out=bias_sb[:, q2 % 2, hh, :], in0=clog[:, :, hd],
                                                                  scalar1=-1.0, scalar2=cref[:, qb, hd:hd + 1],
                                                                  op0=ALU.mult, op1=ALU.add),
                                 reads=["clog", "cref"], writes=[("bias", q2 % 2, hh)])

                def emit_qk(i):
                    qi, hp, qb, kt, hh, last = blocks[i]
                    kb, qbuf = hp % 2, qi % 2
                    n0 = max(0, kt * 128 - qb * TB)
                    N = TB - n0
                    diag = kt * 128 >= qb * TB
                    pS = SB[i % 3]
                    c.mm(PS[pS][:, 0:N], kT_sb[:, kb, kt * 128:(kt + 1) * 128], q_sb[:, qbuf, hh, n0:TB],
                         start=True, stop=(not diag), reads=[("kT_sb", kb), ("q_sb", qbuf)], writes=[psk(pS)],
                         signal=(not diag))
                    if diag:
                        c.mm(PS[pS][:, 0:128], ident_bf[:, :], mask_bf[:, :], start=False, stop=True,
                             reads=["ident_bf", "mask_bf"], writes=[psk(pS)], signal=True)

                def emit_exp_pv(i):
                    qi, hp, qb, kt, hh, last = blocks[i]
                    kb = hp % 2
                    n0 = max(0, kt * 128 - qb * TB)
                    N = TB - n0
                    pS = SB[i % 3]
                    pO = 2 + 2 * (qi % 2) + hh
                    pbuf = i % 4
                    c.act(P_sb[:, pbuf, 0:N], PS[pS][:, 0:N], AF.Exp, reads=[psk(pS), ("bias", qi % 2, hh)],
                          writes=[("P", pbuf)], scale=0.125, bias=bias_sb[:, qi % 2, hh, kt:kt + 1])
                    c.mm(PS[pO][:, n0:TB], v_sb[:, kb, kt, hh, :], P_sb[:, pbuf, 0:N], start=(kt == 0),
                         stop=(kt == 4 * qb + 3), reads=[("v_sb", kb), ("P", pbuf)], writes=[psk(pO)], signal=True)

                def emit_fin_a(qi):
                    pA = 2 + 2 * (qi % 2)
                    pB = pA + 1
                    c.op("dve", lambda e: e.reciprocal(out=r_sb[64:128, :], in_=PS[pA][64:128, :]),
                         reads=[psk(pA)], writes=["r_hi"])
                    c.op("dve", lambda e: e.reciprocal(out=r_sb[0:64, :], in_=PS[pB][0:64, :]),
                         reads=[psk(pB)], writes=["r_lo"])

                def emit_fin_b(qi):
                    yb = qi % 2
                    pA = 2 + 2 * (qi % 2)
                    pB = pA + 1
                    c.act(rs_sb[0:64, :], r_sb[64:128, :], AF.Copy, reads=["r_hi"], writes=["rs_lo"])
                    c.act(rs_sb[64:128, :], r_sb[0:64, :], AF.Copy, reads=["r_lo"], writes=["rs_hi"])
                    c.op("dve", lambda e: e.tensor_tensor(out=y_sb[0:64, yb, :], in0=PS[pA][0:64, :],
                                                          in1=rs_sb[0:64, :], op=ALU.mult),
                         reads=[psk(pA), "rs_lo"], writes=[("y", yb)])
                    c.op("dve", lambda e: e.tensor_tensor(out=y_sb[64:128, yb, :], in0=PS[pB][64:128, :],
                                                          in1=rs_sb[64:128, :], op=ALU.mult),
                         reads=[psk(pB), "rs_hi"], writes=[("y", yb)])

                def emit_outproj(qi, fo):
                    hp, qb = seq[qi]
                    yb = qi % 2
                    c.mm(PS[7][:, :], woa_v[:, hp, fo * 128:(fo + 1) * 128], y_sb[:, yb, :], start=True, stop=True,
                         reads=["woa_sb", ("y", yb)], writes=[psk(7)], signal=True)
                    c.op("dve", lambda e: e.tensor_tensor(out=X[:, fo, blk(qb)], in0=PS[7][:, :],
                                                          in1=X[:, fo, blk(qb)], op=ALU.add),
                         reads=[psk(7), xk(fo, qb)], writes=[xk(fo, qb)])

                pending = []
                pfin = []
                nblk = len(blocks)
                FDEL = 14

                def do_fin_b(jnow):
                    _, fq = pfin.pop(0)
                    while pending and pending[0][1] <= fq - 2:
                        _, pq, pf = pending.pop(0)
                        emit_outproj(pq, pf)
                    emit_fin_b(fq)
                    for fo in range(8):
                        pending.append((jnow + 3 + 2 * fo, fq, fo))

                for i in range(nblk + LA):
                    if i < nblk:
                        ensure_setup(blocks[i][0])
                        emit_qk(i)
                    j = i - LA
                    if j >= 0:
                        emit_exp_pv(j)
                        if pfin and pfin[0][0] <= j:
                            do_fin_b(j)
                        while pending and pending[0][0] <= j:
                            _, pq, pf = pending.pop(0)
                            emit_outproj(pq, pf)
                        if blocks[j][5]:
                            qi = blocks[j][0]
                            while pfin:
                                do_fin_b(j)
                            emit_fin_a(qi)
                            pfin.append((j + FDEL, qi))
                while pfin:
                    do_fin_b(nblk)
                while pending:
                    _, pq, pf = pending.pop(0)
                    emit_outproj(pq, pf)
                c.barrier()

        for l in range(NL):
            ffn_phase(l, 0, l * PL + 0)
            m1_phase(l)
            m2_phase(l)
            ffn_phase(l, 1, l * PL + 16)

        c.barrier()
        if final:
            with contextlib.ExitStack() as ph:
                sq = sb(ph, "e_sq", [128, 2, TB], BF16)
                rs_tmp = sb(ph, "e_rs", [128, TB], F32)
                rstd = sb(ph, "e_rstd", [128, TB], F32)
                gb = NL * PL
                for tb in range(NTB):
                    rmsnorm_stats(tb, sq, rs_tmp, rstd, tb % 2)
                    for fc in range(8):
                        c.op("dve", lambda e: e.scalar_tensor_tensor(out=X[:, fc, blk(tb)], in0=X[:, fc, blk(tb)],
                                                                     scalar=cst[:, gb + fc:gb + fc + 1], in1=rstd[:, :],
                                                                     op0=ALU.mult, op1=ALU.mult),
                             reads=[xk(fc, tb), "rstd", "cst"], writes=[xk(fc, tb)])
                    c.dma("sp", oT.rearrange("(c p) t -> p c t", p=128)[:, :, blk(tb)], X[:, :, blk(tb)],
                          reads=[xk(fc, tb) for fc in range(8)], writes=["oT"], semkey=("Xout", tb))
                c.wait_all("sp", ["oT"])
        else:
            for fc in range(8):
                c.dma("sp", oT[fc * 128:(fc + 1) * 128, :], X[:, fc, :], reads=[xk(fc, tb) for tb in range(NTB)],
                      writes=["oT"], semkey=("Xout", fc))
            c.wait_all("sp", ["oT"])
        c.barrier()
        build_program.stats = (dict(c.n_ops), c.n_wait)
    return nc


def _prep_weights(inp, layers):
    NL = len(layers)
    f32 = np.float32
    L = list(layers)
    w_ffn_in = np.asarray(inp["w_ffn_in"], f32)[L]
    w_ffn_out = np.asarray(inp["w_ffn_out"], f32)[L]
    w_in = np.asarray(inp["w_in"], f32)[L]
    w_out = np.asarray(inp["w_out"], f32)[L]
    t = w_ffn_in.reshape(NL, 2, 8, 128, 2, NJ, 128)
    wi = np.ascontiguousarray(t.transpose(0, 1, 5, 3, 2, 4, 6)).reshape(NL, 2, NJ, 128, 2048)
    t = w_ffn_out.reshape(NL, 2, NJ, 128, 8, 128)
    wo = np.ascontiguousarray(t.transpose(0, 1, 4, 3, 2, 5)).reshape(NL, 2, 8, 128, 2816)
    colbase = [0, 128, 256, 384, 512, 640, 768, 896, 1544, 1672, 1800, 1928, 2056, 2184, 2312, 2440]
    chunks = np.stack([w_in[:, :, b:b + 128] for b in colbase], axis=1)
    t = chunks.reshape(NL, 8, 2, 8, 128, 128)
    wq = np.ascontiguousarray(t.transpose(0, 1, 4, 3, 2, 5)).reshape(NL, 8, 128, 2048)
    t = w_in[:, :, 1024:1536].reshape(NL, 8, 128, 512)
    wv = np.ascontiguousarray(t.transpose(0, 2, 1, 3)).reshape(NL, 128, 4096)
    t = w_in[:, :, 1536:1544].reshape(NL, 8, 128, 8)
    wf = np.ascontiguousarray(t.transpose(0, 2, 1, 3)).reshape(NL, 128, 64)
    t = w_out[:, 512:1024, :].reshape(NL, 4, 128, 1024)
    wor = np.ascontiguousarray(t.transpose(0, 2, 1, 3)).reshape(NL, 128, 4096)
    t = w_out[:, 0:512, :].reshape(NL, 4, 128, 1024)
    woa = np.ascontiguousarray(t.transpose(0, 2, 1, 3)).reshape(NL, 128, 4096)
    wa = np.asarray(inp["w_rg_a"], f32)[L]
    wx = np.asarray(inp["w_rg_x"], f32)[L]
    wbd = np.zeros((NL, 128, 4, 2, 128), f32)
    for cch in range(4):
        for half in range(2):
            rs = slice(half * 64, half * 64 + 64)
            wbd[:, rs, cch, 0, rs] = wa[:, 2 * cch + half]
            wbd[:, rs, cch, 1, rs] = wx[:, 2 * cch + half]
    wbd = wbd.reshape(NL, 128, 1024)
    cst = np.zeros((128, NL * PL + 8), f32)
    ng = np.asarray(inp["norm_g"], f32)[L]
    cw = np.asarray(inp["conv_w"], f32)[L]
    for li in range(NL):
        b = li * PL
        cst[:, b:b + 24] = ng[li].reshape(3, 8, 128).transpose(2, 0, 1).reshape(128, 24)
        cst[:, b + 24:b + 40] = cw[li].reshape(4, 4, 128).transpose(2, 0, 1).reshape(128, 16)
        cst[:, b + 40:b + 44] = np.asarray(inp["conv_b"], f32)[L[li]].reshape(4, 128).T
        cst[:, b + 44:b + 48] = np.asarray(inp["b_rg_a"], f32)[L[li]].reshape(4, 128).T
        cst[:, b + 48:b + 52] = np.asarray(inp["b_rg_x"], f32)[L[li]].reshape(4, 128).T
        cst[:, b + 52:b + 56] = np.asarray(inp["rg_lambda"], f32)[L[li]].reshape(4, 128).T
        cst[:, b + 56:b + 64] = np.asarray(inp["b_f"], f32)[L[li]][None, :]
    cst[:, NL * PL:NL * PL + 8] = np.asarray(inp["final_g"], f32).reshape(8, 128).T
    cmat = np.zeros((128, 384), f32)
    cmat[:, 0:128] = np.eye(128, dtype=f32)
    kk = np.arange(128)[:, None]
    qq = np.arange(128)[None, :]
    cmat[:, 128:256] = np.where(kk > qq, MASKVAL, 0.0)
    cmat[:, 256:384] = (kk <= qq).astype(f32)
    return dict(cst=cst, cmat=cmat, wi=wi, wo=wo, wq=wq, wv=wv, wf=wf, wor=wor, woa=woa, wbd=wbd)


FUSED = True
_PROGS = {}


def _prog(NL, final):
    key = (NL, final)
    if key not in _PROGS:
        _PROGS[key] = build_program(NL, final)
    return _PROGS[key]


def kernel(**inputs):
    x = np.asarray(inputs["x"], np.float32)
    xT = [np.ascontiguousarray(x[b].T) for b in range(NB)]
    if FUSED:
        groups = [list(range(DEPTH))]
    else:
        groups = [[l] for l in range(DEPTH)]
    for gi, layers in enumerate(groups):
        final = gi == len(groups) - 1
        w = _prep_weights(inputs, layers)
        nc = _prog(len(layers), final)
        in_maps = [dict(w, xT=xT[b]) for b in range(NB)]
        res = run_bass_kernel_spmd(nc, in_maps, core_ids=list(range(NB)))
        xT = [np.asarray(res.results[b]["oT"], np.float32) for b in range(NB)]
    out = np.stack([xT[b].T for b in range(NB)], axis=0)
    return np.ascontiguousarray(out.astype(np.float32))
```
